# Optimizing a Trainium2 kernel written in Bass

```python
import jax, jax.numpy as jnp
from jax import lax
import numpy as np

D_MODEL = 2048
BATCH = 2
SEQ = 4096
DEPTH = 1
DEC_BATCH = 32
DEC_SEQ = 4
PAST_LEN = 16384
PAGE_SIZE = 128

GLA_WIDTH = D_MODEL // 2
N_GLA_HEADS = 4
GLA_DV = GLA_WIDTH // N_GLA_HEADS
GLA_DK = GLA_DV // 2
GLA_KEY_WIDTH = N_GLA_HEADS * GLA_DK
GLA_GATE_RANK = 16
GLA_GATE_NORM = 16.0
GLA_CHUNK = 16
ATT_WIDTH = D_MODEL - GLA_WIDTH
N_ATT_HEADS = 8
ATT_HEAD_DIM = ATT_WIDTH // N_ATT_HEADS
DILATED_CONFIGS = ((128, 1), (512, 4), (2048, 16))
MAX_WINDOW = 2048
SUB_WINDOW = 128
ATT_BLOCK = 128
D_FF = 4 * D_MODEL
RMS_EPS = 1e-6
IN_COLS = 2 * GLA_KEY_WIDTH + 2 * GLA_WIDTH + GLA_GATE_RANK + 3 * ATT_WIDTH

kernel_name = 'hymba_gla_dilated_alibi_decode_step'


def rmsnorm(x, w):
    xf = x.astype(jnp.float32)
    r = lax.rsqrt(jnp.mean(xf * xf, axis=-1, keepdims=True) + RMS_EPS)
    return (xf * r).astype(x.dtype) * w


def alibi_slopes():
    h = jnp.arange(1, N_ATT_HEADS + 1, dtype=jnp.float32)
    return jnp.exp2(-8.0 * h / N_ATT_HEADS)


def mixer_inputs(x, attn_norm_w, w_in, w_gk_up, b_gk):
    B, T, _ = x.shape
    h = rmsnorm(x, attn_norm_w)
    proj = h @ w_in
    sizes = [GLA_KEY_WIDTH, GLA_KEY_WIDTH, GLA_WIDTH, GLA_WIDTH, GLA_GATE_RANK, ATT_WIDTH, ATT_WIDTH]
    cuts = [int(c) for c in np.cumsum(sizes)]
    gq, gk, gv, gg, glr, aq, ak, av = jnp.split(proj, cuts, axis=-1)
    log_a = jax.nn.log_sigmoid((glr @ w_gk_up + b_gk).astype(jnp.float32)) / GLA_GATE_NORM
    gla = (gq.reshape(B, T, N_GLA_HEADS, GLA_DK) * (GLA_DK ** -0.5),
           gk.reshape(B, T, N_GLA_HEADS, GLA_DK),
           gv.reshape(B, T, N_GLA_HEADS, GLA_DV),
           log_a.reshape(B, T, N_GLA_HEADS, GLA_DK))
    att = tuple(a.reshape(B, T, N_ATT_HEADS, ATT_HEAD_DIM) for a in (aq, ak, av))
    return gla, gg, att


def gla_chunked(q, k, v, log_a, s0):
    f32 = jnp.float32
    B, T = q.shape[0], q.shape[1]
    nc = -(-T // GLA_CHUNK)
    pad = nc * GLA_CHUNK - T
    def blocks(a):
        a = jnp.pad(a.astype(f32), ((0, 0), (0, pad), (0, 0), (0, 0)))
        return a.reshape(B, nc, GLA_CHUNK, a.shape[2], a.shape[3]).transpose(1, 0, 3, 2, 4)
    qc, kc, vc, gc = blocks(q), blocks(k), blocks(v), blocks(log_a)
    bc = jnp.cumsum(gc, axis=3)
    causal = jnp.tril(jnp.ones((GLA_CHUNK, GLA_CHUNK), dtype=bool))[:, :, None]

    def step(S, inp):
        qb, kb, vb, bb = inp
        b_last = bb[:, :, -1, :]
        o_inter = jnp.einsum('bhck,bhkv->bhcv', qb * jnp.exp(bb), S)
        diff = bb[:, :, :, None, :] - bb[:, :, None, :, :]
        decay = jnp.exp(jnp.where(causal, diff, -jnp.inf))
        A = jnp.einsum('bhtk,bhsk,bhtsk->bhts', qb, kb, decay)
        o_intra = jnp.einsum('bhts,bhsv->bhtv', A, vb)
        S_new = jnp.exp(b_last)[..., None] * S + jnp.einsum(
            'bhsk,bhsv->bhkv', kb * jnp.exp(b_last[:, :, None, :] - bb), vb)
        return S_new, o_inter + o_intra

    S_fin, o = lax.scan(step, s0.astype(f32), (qc, kc, vc, bc))
    o = o.transpose(1, 0, 3, 2, 4).reshape(B, nc * GLA_CHUNK, N_GLA_HEADS, GLA_DV)[:, :T]
    return o, S_fin


def strided_window_attention_prompt(q, k, v, dil, slopes):
    B, S, H, E = q.shape
    L = S // dil
    nb = -(-L // ATT_BLOCK)
    Lp = nb * ATT_BLOCK
    Bd = B * dil
    def sub(a):
        a = a.reshape(B, L, dil, H, E).transpose(0, 2, 1, 3, 4).reshape(Bd, L, H, E)
        return jnp.pad(a, ((0, 0), (0, Lp - L), (0, 0), (0, 0)))
    qs, ks, vs = sub(q), sub(k), sub(v)
    qb = qs.reshape(Bd, nb, ATT_BLOCK, H, E)
    def band(a):
        prev = jnp.pad(a, ((0, 0), (ATT_BLOCK, 0), (0, 0), (0, 0)))[:, :Lp]
        return jnp.concatenate([prev.reshape(Bd, nb, ATT_BLOCK, H, E),
                                a.reshape(Bd, nb, ATT_BLOCK, H, E)], axis=2)
    kb, vb = band(ks), band(vs)
    s = jnp.einsum('bnqhe,bnkhe->bnhqk', qb, kb,
                   preferred_element_type=jnp.float32) * (E ** -0.5)
    qi = jnp.arange(ATT_BLOCK)[:, None]
    ki = jnp.arange(2 * ATT_BLOCK)[None, :]
    dist = qi - ki + ATT_BLOCK
    key_idx = jnp.arange(nb)[:, None, None] * ATT_BLOCK - ATT_BLOCK + ki[None]
    valid = (dist >= 0)[None] & (dist <= SUB_WINDOW)[None] & (key_idx >= 0)
    bias = -slopes[:, None, None] * (dil * dist).astype(jnp.float32)[None]
    s = jnp.where(valid[None, :, None], s + bias[None, None], -jnp.inf)
    lse = jax.nn.logsumexp(s, axis=-1)
    p = jnp.exp(s - lse[..., None])
    o = jnp.einsum('bnhqk,bnkhe->bnqhe', p.astype(v.dtype), vb)
    o = o.reshape(Bd, Lp, H, E)[:, :L].reshape(B, dil, L, H, E).transpose(0, 2, 1, 3, 4).reshape(B, S, H, E)
    lse = lse.transpose(0, 1, 3, 2).reshape(Bd, Lp, H)[:, :L].reshape(B, dil, L, H).transpose(0, 2, 1, 3).reshape(B, S, H)
    return o, lse


def strided_window_attention_sample(q, k_all, v_all, w_cache, dil, slopes):
    B, T, H, E = q.shape
    J = SUB_WINDOW + 1
    j = jnp.arange(J)
    idx = w_cache + jnp.arange(T)[:, None] - dil * j[None, :]
    valid = idx >= 0
    idx_c = jnp.clip(idx, 0, None).reshape(-1)
    kg = jnp.take(k_all, idx_c, axis=1).reshape(B, T, J, H, E)
    vg = jnp.take(v_all, idx_c, axis=1).reshape(B, T, J, H, E)
    s = jnp.einsum('bthe,btjhe->bhtj', q, kg,
                   preferred_element_type=jnp.float32) * (E ** -0.5)
    bias = -slopes[:, None, None] * (dil * j).astype(jnp.float32)[None, None, :]
    s = jnp.where(valid[None, None], s + bias[None], -jnp.inf)
    lse = jax.nn.logsumexp(s, axis=-1)
    p = jnp.exp(s - lse[..., None])
    o = jnp.einsum('bhtj,btjhe->bthe', p.astype(v_all.dtype), vg)
    return o, lse.transpose(0, 2, 1)


def combine_dilations(outs, lses):
    w = jax.nn.softmax(jnp.stack(lses, 0), axis=0)
    return jnp.einsum('gbth,gbthe->bthe', w.astype(outs[0].dtype), jnp.stack(outs, 0))


def layer_output(x, gla_o, gg, att_o, gla_norm_w, att_out_norm_w, w_out,
                 ffn_norm_w, w_up, w_down):
    B, T, _ = x.shape
    gla_part = rmsnorm(gla_o, gla_norm_w) * jax.nn.silu(gg.astype(jnp.float32)).reshape(B, T, N_GLA_HEADS, GLA_DV)
    gla_part = gla_part.reshape(B, T, GLA_WIDTH).astype(x.dtype)
    att_part = rmsnorm(att_o.reshape(B, T, ATT_WIDTH), att_out_norm_w).astype(x.dtype)
    x = x + jnp.concatenate([gla_part, att_part], axis=-1) @ w_out
    h = rmsnorm(x, ffn_norm_w)
    return x + jnp.square(jax.nn.relu(h @ w_up)) @ w_down


def setup_inputs(seed: int = 0) -> dict:
    key = jax.random.key(seed)
    ks = jax.random.split(key, 16)
    f32 = jnp.float32
    w_s = min(MAX_WINDOW, PAST_LEN)
    def nrm(k, shape, scale):
        return scale * jax.random.normal(k, shape, f32)
    def gain(k, shape):
        return 1.0 + 0.01 * jax.random.normal(k, shape, f32)
    return {
        'x_prompt': nrm(ks[0], (BATCH, SEQ, D_MODEL), 1.0),
        'x_sample': nrm(ks[1], (DEC_BATCH, DEC_SEQ, D_MODEL), 1.0),
        'cache_k_win': nrm(ks[2], (DEPTH, DEC_BATCH, w_s, N_ATT_HEADS, ATT_HEAD_DIM), 1.0),
        'cache_v_win': nrm(ks[3], (DEPTH, DEC_BATCH, w_s, N_ATT_HEADS, ATT_HEAD_DIM), 1.0),
        'state_gla': nrm(ks[4], (DEPTH, DEC_BATCH, N_GLA_HEADS, GLA_DK, GLA_DV), 0.5),
        'attn_norm_w': gain(ks[5], (DEPTH, D_MODEL)),
        'w_in': nrm(ks[6], (DEPTH, D_MODEL, IN_COLS), D_MODEL ** -0.5),
        'w_gk_up': nrm(ks[7], (DEPTH, GLA_GATE_RANK, GLA_KEY_WIDTH), GLA_GATE_RANK ** -0.5),
        'b_gk': nrm(ks[8], (DEPTH, GLA_KEY_WIDTH), 0.1),
        'gla_norm_w': gain(ks[9], (DEPTH, GLA_DV)),
        'att_out_norm_w': gain(ks[10], (DEPTH, ATT_WIDTH)),
        'w_out': nrm(ks[11], (DEPTH, D_MODEL, D_MODEL), D_MODEL ** -0.5),
        'ffn_norm_w': gain(ks[12], (DEPTH, D_MODEL)),
        'w_up': nrm(ks[13], (DEPTH, D_MODEL, D_FF), D_MODEL ** -0.5),
        'w_down': nrm(ks[14], (DEPTH, D_FF, D_MODEL), D_FF ** -0.5),
        'final_norm_w': gain(ks[15], (D_MODEL,)),
    }


def reference(x_prompt, x_sample, cache_k_win, cache_v_win, state_gla,
              attn_norm_w, w_in, w_gk_up, b_gk, gla_norm_w, att_out_norm_w, w_out,
              ffn_norm_w, w_up, w_down, final_norm_w):
    slopes = alibi_slopes()
    xp, xs = x_prompt, x_sample
    S = xp.shape[1]
    w_p = min(MAX_WINDOW, S)
    w_s = cache_k_win.shape[2]
    kp_l, vp_l, sp_l, ks_l, vs_l, ss_l = [], [], [], [], [], []
    for l in range(DEPTH):
        (gq, gk, gv, ga), gg, (aq, ak, av) = mixer_inputs(xp, attn_norm_w[l], w_in[l], w_gk_up[l], b_gk[l])
        s0 = jnp.zeros((xp.shape[0], N_GLA_HEADS, GLA_DK, GLA_DV), jnp.float32)
        gla_o, s_fin_p = gla_chunked(gq, gk, gv, ga, s0)
        outs, lses = [], []
        for _, dil in DILATED_CONFIGS:
            o, lse = strided_window_attention_prompt(aq, ak, av, dil, slopes)
            outs.append(o)
            lses.append(lse)
        att_o = combine_dilations(outs, lses)
        xp = layer_output(xp, gla_o, gg, att_o, gla_norm_w[l], att_out_norm_w[l], w_out[l],
                          ffn_norm_w[l], w_up[l], w_down[l])
        kp_l.append(ak[:, S - w_p:])
        vp_l.append(av[:, S - w_p:])
        sp_l.append(s_fin_p)
        (gq, gk, gv, ga), gg, (aq, ak, av) = mixer_inputs(xs, attn_norm_w[l], w_in[l], w_gk_up[l], b_gk[l])
        gla_o, s_fin_s = gla_chunked(gq, gk, gv, ga, state_gla[l])
        k_all = jnp.concatenate([cache_k_win[l].astype(ak.dtype), ak], axis=1)
        v_all = jnp.concatenate([cache_v_win[l].astype(av.dtype), av], axis=1)
        outs, lses = [], []
        for _, dil in DILATED_CONFIGS:
            o, lse = strided_window_attention_sample(aq, k_all, v_all, w_s, dil, slopes)
            outs.append(o)
            lses.append(lse)
        att_o = combine_dilations(outs, lses)
        xs = layer_output(xs, gla_o, gg, att_o, gla_norm_w[l], att_out_norm_w[l], w_out[l],
                          ffn_norm_w[l], w_up[l], w_down[l])
        ks_l.append(ak)
        vs_l.append(av)
        ss_l.append(s_fin_s)
    y_prompt = rmsnorm(xp, final_norm_w)
    y_sample = rmsnorm(xs, final_norm_w)
    k_win_prompt = jnp.stack(kp_l, 0)
    v_win_prompt = jnp.stack(vp_l, 0)
    gla_prompt = jnp.stack(sp_l, 0)
    k_new_sample = jnp.stack(ks_l, 0)
    v_new_sample = jnp.stack(vs_l, 0)
    gla_sample = jnp.stack(ss_l, 0)
    return (y_prompt, y_sample, k_win_prompt, v_win_prompt, gla_prompt,
            k_new_sample, v_new_sample, gla_sample)
```

```python
import numpy as np
import concourse.bass as bass
import concourse.mybir as mybir
from concourse.bass_utils import run_bass_kernel_spmd

F32, BF16 = mybir.dt.float32, mybir.dt.bfloat16
AF = mybir.ActivationFunctionType
ALU = mybir.AluOpType
AX = mybir.AxisListType

NQ = 24


class Trk:
    __slots__ = ("w", "r")

    def __init__(self):
        self.w = None
        self.r = []


class Op:
    __slots__ = ("eng", "fn", "deps", "dma", "slot", "inc", "val")

    def __init__(self, eng, fn, deps, dma):
        self.eng, self.fn, self.deps, self.dma = eng, fn, deps, dma
        self.slot, self.inc, self.val = None, False, 0


class Prog:
    ENGS = ["pe", "act", "dve", "pool", "sp"]

    def __init__(self):
        self.ops = []
        self.slot_last = {"sp": [None] * NQ, "pool": [None] * NQ}
        self.dma_cnt = {"sp": 0, "pool": 0}
        self.last = {e: None for e in self.ENGS}

    def op(self, eng, fn, reads=(), writes=(), dma=False):
        deps = set()
        for t in reads:
            if t.w is not None:
                deps.add(t.w)
        for t in writes:
            if t.w is not None:
                deps.add(t.w)
            deps.update(t.r)
        if eng == "pe":
            deps = {d for d in deps if self.ops[d].eng != "pe"}
        i = len(self.ops)
        o = Op(eng, fn, deps, dma)
        if dma:
            k = self.dma_cnt[eng] % NQ
            self.dma_cnt[eng] += 1
            if self.slot_last[eng][k] is not None:
                deps.add(self.slot_last[eng][k])
            self.slot_last[eng][k] = i
            o.slot = k
        self.ops.append(o)
        for t in reads:
            t.r.append(i)
        for t in writes:
            t.w = i
            t.r = []
        self.last[eng] = i
        return i

    def barrier(self):
        allp = set(x for x in self.last.values() if x is not None)
        for q in self.slot_last.values():
            allp.update(x for x in q if x is not None)
        for e in self.ENGS:
            self.ops.append(Op(e, None, set(allp), False))

    def emit(self, nc):
        ops = self.ops
        for o in ops:
            for d in o.deps:
                ops[d].inc = True
        cnt = {e: 0 for e in self.ENGS}
        slotcnt = {"sp": [0] * NQ, "pool": [0] * NQ}
        for o in ops:
            if o.fn is None:
                continue
            if o.dma:
                slotcnt[o.eng][o.slot] += 16
                o.val = slotcnt[o.eng][o.slot]
            elif o.inc:
                cnt[o.eng] += 1
                o.val = cnt[o.eng]
        import contextlib

        with contextlib.ExitStack() as es:
            csem = {e: es.enter_context(nc.semaphore("c_" + e)) for e in self.ENGS}
            dsem = {q: [es.enter_context(nc.semaphore(f"d_{q}{k}")) for k in range(NQ)] for q in ("sp", "pool")}
            block = es.enter_context(nc.Block())

            def run(ename, eng):
                waited = {}
                for o in ops:
                    if o.eng != ename:
                        continue
                    for d in sorted(o.deps):
                        do = ops[d]
                        if do.dma:
                            sem, key = dsem[do.eng][do.slot], ("d", do.eng, do.slot)
                        else:
                            sem, key = csem[do.eng], ("c", do.eng)
                        if waited.get(key, 0) < do.val:
                            eng.wait_ge(sem, do.val)
                            waited[key] = do.val
                    if o.fn is None:
                        continue
                    ins = o.fn(eng)
                    if o.dma:
                        ins.then_inc(dsem[ename][o.slot], 16)
                    elif o.inc:
                        ins.then_inc(csem[ename], 1)

            @block.tensor
            def _(e):
                run("pe", e)

            @block.scalar
            def _(e):
                run("act", e)

            @block.vector
            def _(e):
                run("dve", e)

            @block.gpsimd
            def _(e):
                run("pool", e)

            @block.sync
            def _(e):
                run("sp", e)


class Arena:
    def __init__(self, nc, base=0):
        self.nc, self.off, self.n = nc, base, 0

    def alloc(self, shape, dt):
        per = int(np.prod(shape[1:])) * (4 if dt == F32 else 2)
        per = (per + 31) // 32 * 32
        h = self.nc.alloc_sbuf_tensor_at(f"sb{self.n}", list(shape), dt, offset=self.off)
        self.n += 1
        self.off += per
        assert self.off <= 229376, self.off
        return h


NTOK = 1040
PARTS_OWN = [(0, 512), (512, 512), (1024, 16)]
PARTS_PRE = [(0, 512), (512, 512)]
PARTS_MLP = [(0, 347), (347, 347), (694, 346)]
C_GQ, C_GK, C_GV, C_GG, C_GLR, C_AQ, C_AK, C_AV = 0, 512, 1024, 2048, 3072, 3088, 4112, 5136
SCALE = 128 ** -0.5
EPS = 1e-6


def attn_units():
    units = []
    for di, D in enumerate((1, 4, 16)):
        lq0, lq1 = 2048 // D, 3072 // D
        for r in range(D):
            j = 0
            while 128 * j < lq1:
                if 128 * (j + 1) > lq0 - 128:
                    nk = min(128, lq1 - 128 * j)
                    q0, q1 = max(128 * j, lq0), min(128 * j + 256, lq1)
                    if q1 > q0:
                        units.append(dict(di=di, D=D, r=r, j=j, nk=nk, kc0=128 * j * D + r, q0=q0, nq=q1 - q0,
                                          qc0=q0 * D + r - 2048, m0=q0 - 128 * j))
                j += 1
    return units


UNITS = attn_units()
NU = len(UNITS)


DEBUG = None


class StopBuild(Exception):
    pass


WLIST = None
WLOG = []
WOFF = {}
WTOT = [0]


def build():
    global WLIST
    WLIST = None
    del WLOG[:]
    WOFF.clear()
    WTOT[0] = 0
    try:
        _build(bass.Bass("TRN2", target_bir_lowering=False), Prog())
    except StopBuild:
        pass
    WLIST = list(WLOG)
    o = 0
    for key in WOFF:
        WOFF[key] = o
        o += 16 * sum(n for (_, n) in key)
    WTOT[0] = o
    nc = bass.Bass("TRN2", target_bir_lowering=False)
    P = Prog()
    try:
        _build(nc, P)
    except StopBuild:
        pass
    P.barrier()
    P.emit(nc)
    return nc


def _build(nc, P):

    def din(name, shape, dt=F32):
        return nc.dram_tensor(name, list(shape), dt, kind="ExternalInput").ap()

    def dout(name, shape):
        return nc.dram_tensor(name, list(shape), F32, kind="ExternalOutput").ap()

    xT = din("xT", [2048, 4096])
    xsT = din("xsT", [2048, 16])
    wflat = din("wflat", [128, max(WTOT[0], 16)])
    wo_flat = din("wo_flat", [128, 8 * 4096])
    wu_flat = din("wu_flat", [128, 16 * 8192])
    w_down = din("w_down", [8192, 2048])
    wgk_d = din("wgk", [17, 512])
    nw_d = din("nw", [128, 58])
    anwb_d = din("anwb", [16, 1024])
    ck_d = din("ck", [4, 2048, 1024])
    cv_d = din("cv", [4, 2048, 1024])
    st_d = din("st", [4, 4, 128, 256])
    identf_d = din("identf", [128, 128])
    identb_d = din("identb", [128, 128], BF16)
    onesb_d = din("onesb", [128, 128], BF16)
    U_d = din("U", [128, 128])
    Ub_d = din("Ub", [16, 16])
    amask_d = din("amask", [128, 3 * 8 * 256], BF16)
    kb_d = din("kb", [128, NU])
    selb_d = din("selb", [128, 256], BF16)
    selq_d = din("selq", [16, 16 * 128], BF16)
    sbias_d = din("sbias", [128, 24])
    sm_d = din("smask", [16, 4])

    y_o = dout("y_own", [1024, 2048])
    ys_o = dout("y_s", [16, 2048])
    k_o = dout("k_own", [1024, 1024])
    v_o = dout("v_own", [1024, 1024])
    gst_o = dout("gla_st", [4, 128, 256])
    ks_o = dout("k_s", [16, 1024])
    vs_o = dout("v_s", [16, 1024])
    gs_o = dout("gla_s", [4, 4, 128, 256])
    if DEBUG:
        dbgf = nc.dram_tensor("dbgf", [128, 8, NTOK], F32, kind="ExternalOutput").ap()
        dbgb = nc.dram_tensor("dbgb", [128, 8, NTOK], BF16, kind="ExternalOutput").ap()

    def dumpf(slot, ap, trks, p=128, n=NTOK):
        if DEBUG:
            P.op("sp", lambda e: e.dma_start(out=dbgf[0:p, slot, 0:n], in_=ap), reads=trks, dma=True)

    def dumpb(slot, ap, trks, p=128, n=NTOK):
        if DEBUG:
            P.op("sp", lambda e: e.dma_start(out=dbgb[0:p, slot, 0:n], in_=ap), reads=trks, dma=True)
    kvscr = nc.dram_tensor("kvscr", [2, 8, 128, 2048], BF16).ap()
    kvscr_t = [[Trk() for _ in range(8)] for _ in range(2)]
    ksvs_t = [Trk(), Trk()]

    A0 = Arena(nc, 16640)
    identf, identb, onesb, Uf = (A0.alloc([128, 128], F32), A0.alloc([128, 128], BF16),
                                 A0.alloc([128, 128], BF16), A0.alloc([128, 128], F32))
    nw = A0.alloc([128, 58], F32)
    Sf = A0.alloc([128, 4, 256], F32)
    Sb = A0.alloc([128, 4, 256], BF16)
    ct = Trk()
    Sf_t = [Trk() for _ in range(4)]
    Sb_t = [Trk() for _ in range(4)]

    def dma(q, out, in_, reads=(), writes=()):
        return P.op(q, lambda e: e.dma_start(out=out, in_=in_), reads=reads, writes=writes, dma=True)

    epsb = A0.alloc([128, 2], F32)
    xs_all = A0.alloc([128, 16, 16], F32)
    xs_t = Trk()
    for dst, src in ((identf, identf_d), (identb, identb_d), (onesb, onesb_d), (Uf, U_d), (nw, nw_d)):
        dma("sp", dst[:], src, writes=[ct])
    dma("sp", xs_all[:], xsT.rearrange("(kc p) n -> p kc n", p=128), writes=[xs_t])
    P.op("pool", lambda e: e.memset(Sf[:], 0.0), writes=Sf_t)
    P.op("pool", lambda e: e.memset(Sb[:], 0.0), writes=Sb_t)
    NW_ATTN, NW_FFN, NW_FIN, NW_GLA, NW_ATT = 0, 16, 32, 48, 50

    psf = [nc.alloc_psum_tensor(f"psf{i}", [128, 512], F32) for i in range(6)]
    psf_t = [Trk() for _ in range(6)]
    psb = [nc.alloc_psum_tensor(f"psb{i}", [128, 1024], BF16) for i in range(2)]
    psb_t = [Trk() for _ in range(2)]
    rr = {"f": 0, "b": 0}

    def ps_f():
        i = rr["f"] % 6
        rr["f"] += 1
        return psf[i], psf_t[i]

    def ps_b():
        i = rr["b"] % 2
        rr["b"] += 1
        return psb[i], psb_t[i]

    def mm(out, lhsT, rhs, start, stop, reads, writes):
        P.op("pe", lambda e: e.matmul(out, lhsT=lhsT, rhs=rhs, start=start, stop=stop), reads=reads, writes=writes)

    def tr(out, in_, ident, reads, writes):
        P.op("pe", lambda e: e.transpose(out, in_, ident), reads=reads, writes=writes)

    def act(out, in_, func, reads, writes, **kw):
        P.op("act", lambda e: e.activation(out=out, in_=in_, func=func, **kw), reads=reads, writes=writes)

    class Rot:
        def __init__(self, arena, shape, dt, n):
            self.b = [arena.alloc(shape, dt) for _ in range(n)]
            self.t = [Trk() for _ in range(n)]
            self.i = 0

        def get(self):
            k = self.i % len(self.b)
            self.i += 1
            return self.b[k], self.t[k]

    A1 = Arena(nc, A0.off)
    hT = A1.alloc([128, 16, NTOK], BF16)
    hT_t = [Trk() for _ in range(16)]
    R_START = A1.off
    xc = Rot(A1, [128, NTOK], F32, 4)
    sqb = Rot(A1, [128, NTOK], BF16, 2)
    rB = A1.alloc([128, NTOK], F32)
    rB_t = Trk()
    wblk = Rot(A1, [128, 4096], BF16, 3)
    AT = A1.alloc([128, 16, NTOK], BF16)
    AT_t = [Trk() for _ in range(16)]
    P1C_START = A1.off
    glrT = A1.alloc([32, NTOK], F32)
    glrT_t = Trk()
    wgk = A1.alloc([32, 512], F32)
    Ub = A1.alloc([16, 16], F32)
    smask = A1.alloc([16, 4], F32)
    for dst, src in ((wgk[0:17, :], wgk_d), (Ub[:], Ub_d), (smask[:], sm_d)):
        dma("sp", dst, src, writes=[ct])
    A1_mark = A1.off

    def ssq_norm(src_chunks, parts, nw_col, dst, dst_t, div):
        N = parts[-1][0] + parts[-1][1]
        pss = [ps_f() for _ in parts]
        for kc in range(16):
            s_ap, s_t = src_chunks(kc)
            q, qt = sqb.get()
            act(q[:, 0:N], s_ap, AF.Square, reads=s_t, writes=[qt])
            P.op("dve", lambda e, kc=kc, s_ap=s_ap: e.tensor_scalar_mul(
                out=dst[:, kc, 0:N], in0=s_ap, scalar1=nw[:, nw_col + kc:nw_col + kc + 1]),
                reads=list(s_t) + [ct], writes=[dst_t[kc]])
            for (c0, n), (ps, pt) in zip(parts, pss):
                mm(ps[:, 0:n], onesb[:], q[:, c0:c0 + n], kc == 0, kc == 15, reads=[qt, ct], writes=[pt])
        for (c0, n), (ps, pt) in zip(parts, pss):
            act(rB[:, c0:c0 + n], ps[:, 0:n], AF.Ln, reads=[pt, ct], writes=[rB_t], scale=1.0 / div, bias=epsb[:, 0:1])
        act(rB[:, 0:N], rB[:, 0:N], AF.Exp, reads=[rB_t], writes=[rB_t], scale=-0.5)
        for kc in range(16):
            eng = "dve" if kc % 2 == 0 else "pool"
            P.op(eng, lambda e, kc=kc: e.tensor_mul(out=dst[:, kc, 0:N], in0=dst[:, kc, 0:N], in1=rB[:, 0:N]),
                 reads=[dst_t[kc], rB_t], writes=[dst_t[kc]])

    P.op("pool", lambda e: e.memset(epsb[:, 0:1], EPS), writes=[ct])
    P.op("pool", lambda e: e.memset(epsb[:, 1:2], float(np.log(SCALE))), writes=[ct])
    P.barrier()

    wstate = {"i": 0, "issued": []}

    def _wissue(col_specs):
        b, t = wblk.get()
        W = sum(n for (_, _, n) in col_specs)
        off = 0
        for (dc, sc, n) in col_specs:
            assert dc == off
            off += n
        key = tuple((sc, n) for (_, sc, n) in col_specs)
        view = b[:, 0:16 * W].rearrange("p (kc n) -> p kc n", kc=16)
        if WLIST is None:
            WOFF.setdefault(key, None)
            src = wflat[:, 0:16]
            dma("pool", b[:, 0:16], src, writes=[t])
        else:
            o = WOFF[key]
            dma("pool", b[:, 0:16 * W], wflat[:, o:o + 16 * W], writes=[t])
        return view, t

    def wload(col_specs, src=None, rows=None):
        i = wstate["i"]
        wstate["i"] += 1
        if WLIST is None:
            WLOG.append(col_specs)
            return _wissue(col_specs)
        assert WLIST[i] == col_specs
        while len(wstate["issued"]) <= min(i + 1, len(WLIST) - 1):
            wstate["issued"].append(_wissue(WLIST[len(wstate["issued"])]))
        return wstate["issued"][i]

    def proj_fm(wb, wt, wc, m, parts, evac):
        for (c0, n) in parts:
            ps, pt = ps_f()
            for kc in range(16):
                mm(ps[0:m, 0:n], wb[:, kc, wc:wc + m], hT[:, kc, c0:c0 + n], kc == 0, kc == 15,
                   reads=[wt, hT_t[kc]], writes=[pt])
            evac(ps, pt, c0, n)

    def proj_tm(wb, wt, ncols, t0, m, evac):
        ps, pt = ps_f()
        for kc in range(16):
            mm(ps[0:m, 0:ncols], hT[:, kc, t0:t0 + m], wb[:, kc, 0:ncols], kc == 0, kc == 15,
               reads=[wt, hT_t[kc]], writes=[pt])
        evac(ps, pt)

    A2 = Arena(nc, A1_mark)
    GB = []
    for _ in range(2):
        GB.append(dict(gkT=A2.alloc([128, NTOK], F32), gqT=A2.alloc([128, NTOK], F32), ggs=A2.alloc([128, 2, NTOK], BF16),
                       gv=A2.alloc([128, 9, 256], BF16), gkT_t=Trk(), gqT_t=Trk(), ggs_t=Trk(), gv_t=[Trk() for _ in range(9)]))
    o_h = A2.alloc([128, 2, NTOK], F32)
    o_t = Trk()
    tmpf = Rot(A2, [128, 128], F32, 6)
    f512 = Rot(A2, [128, 512], F32, 5)
    xq = Rot(A2, [128, 512], BF16, 2)
    xk = Rot(A2, [128, 512], BF16, 2)
    xa = Rot(A2, [128, 4, 128], BF16, 2)
    xh = Rot(A2, [128, 4, 128], BF16, 2)
    xhT = Rot(A2, [128, 512], BF16, 2)
    tmpb = Rot(A2, [128, 128], BF16, 8)
    khat4 = A2.alloc([128, 4, 16], BF16)
    khat4_t = Trk()
    blast = Rot(A2, [128, 4], F32, 6)
    sgate, sgate_t = rB, rB_t
    S0 = Rot(A2, [128, 256], F32, 4)
    S0b = Rot(A2, [128, 256], BF16, 2)
    Sout = Rot(A2, [128, 256], F32, 2)

    def copy_evac(dst_ap, dst_t, eng="act"):
        def f(ps, pt, c0=None, n=None):
            pass
        return f

    def gla_tile(h, c0, nt, own, sample, G):
        gkT, gkT_t, gqT, gqT_t, gv, gv_t = G["gkT"], G["gkT_t"], G["gqT"], G["gqT_t"], G["gv"], G["gv_t"]
        Umask = Ub[:] if sample else Uf[:]
        ti = c0 // 128
        ps, pt = ps_f()
        mm(ps[0:nt, 0:128], glrT[0:17, c0:c0 + nt], wgk[0:17, h * 128:(h + 1) * 128], True, True,
           reads=[glrT_t, ct], writes=[pt])
        e1, e1t = tmpf.get()
        act(e1[0:nt, :], ps[0:nt, 0:128], AF.Exp, reads=[pt], writes=[e1t], scale=-1.0)
        lan, lant = tmpf.get()
        act(lan[0:nt, :], e1[0:nt, :], AF.Ln, reads=[e1t], writes=[lant], bias=1.0)
        psB, ptB = ps_f()
        mm(psB[:, 0:nt], lan[0:nt, :], Umask[0:nt, 0:nt], True, True, reads=[lant, ct], writes=[ptB])
        enB, enBt = tmpf.get()
        act(enB[:, 0:nt], psB[:, 0:nt], AF.Exp, reads=[ptB], writes=[enBt], scale=1.0 / 16)
        kt, ktt = tmpf.get()
        P.op("dve", lambda e: e.tensor_mul(out=kt[:, 0:nt], in0=gkT[:, c0:c0 + nt], in1=enB[:, 0:nt]),
             reads=[gkT_t, enBt], writes=[ktt])
        bl, blt = blast.get()
        if sample:
            for j in range(4):
                act(bl[:, j:j + 1], psB[:, 4 * j + 3:4 * j + 4], AF.Exp, reads=[ptB], writes=[blt], scale=-1.0 / 16)
        else:
            act(bl[:, 0:1], psB[:, nt - 1:nt], AF.Exp, reads=[ptB], writes=[blt], scale=-1.0 / 16)
        if own:
            eB, eBt = tmpf.get()
            act(eB[:, 0:nt], psB[:, 0:nt], AF.Exp, reads=[ptB, ct], writes=[eBt], scale=-1.0 / 16, bias=epsb[:, 1:2])
            qt_, qtt = tmpb.get()
            P.op("dve", lambda e: e.tensor_mul(out=qt_[:, 0:nt], in0=gqT[:, c0:c0 + nt], in1=eB[:, 0:nt]),
                 reads=[gqT_t, eBt], writes=[qtt])
            ktb, ktbt = tmpb.get()
            P.op("pool", lambda e: e.tensor_copy(out=ktb[:, 0:nt], in_=kt[:, 0:nt]), reads=[ktt], writes=[ktbt])
            psA, ptA = ps_f()
            mm(psA[0:nt, 0:nt], ktb[:, 0:nt], qt_[:, 0:nt], True, True, reads=[ktbt, qtt], writes=[ptA])
            Am, Amt = tmpb.get()
            P.op("dve", lambda e: e.tensor_mul(out=Am[0:nt, 0:nt], in0=psA[0:nt, 0:nt], in1=Umask[0:nt, 0:nt]),
                 reads=[ptA, ct], writes=[Amt])
            if sample:
                psO, ptO = ps_f()
                for vh in range(2):
                    mm(psO[:, vh * 128:vh * 128 + nt], gv[0:nt, ti, vh * 128:(vh + 1) * 128], Am[0:nt, 0:nt], True, True,
                       reads=[gv_t[ti], Amt], writes=[ptO])
        if not sample:
            kh, kht = tmpb.get()
            P.op("pool", lambda e: e.tensor_scalar_mul(out=kh[:, 0:nt], in0=kt[:, 0:nt], scalar1=bl[:, 0:1]),
                 reads=[ktt, blt], writes=[kht])
            pT, pTt = ps_b()
            tr(pT[0:nt, 0:128], kh[:, 0:nt], identb[:], reads=[kht, ct], writes=[pTt])
            khT, khTt = tmpb.get()
            act(khT[0:nt, :], pT[0:nt, 0:128], AF.Copy, reads=[pTt], writes=[khTt])
            def stageB():
                if own:
                    psO, ptO = ps_f()
                    for vh in range(2):
                        mm(psO[:, vh * 128:vh * 128 + nt], gv[0:nt, ti, vh * 128:(vh + 1) * 128], Am[0:nt, 0:nt], True, False,
                           reads=[gv_t[ti], Amt], writes=[ptO])
                        mm(psO[:, vh * 128:vh * 128 + nt], Sb[:, h, vh * 128:(vh + 1) * 128], qt_[:, 0:nt], False, True,
                           reads=[Sb_t[h], qtt], writes=[ptO])
                psS, ptS = ps_f()
                mm(psS[:, 0:256], khT[0:nt, :], gv[0:nt, ti, :], True, True, reads=[khTt, gv_t[ti]], writes=[ptS])
                if own:
                    P.op("act", lambda e: e.activation(out=o_h[:, :, c0:c0 + nt],
                                                       in_=psO[:, 0:256].rearrange("p (a b) -> p a b", a=2)[:, :, 0:nt],
                                                       func=AF.Copy), reads=[ptO], writes=[o_t])
                P.op("dve", lambda e: e.scalar_tensor_tensor(out=Sf[:, h, :], in0=Sf[:, h, :], scalar=bl[:, 0:1],
                                                             in1=psS[:, 0:256], op0=ALU.mult, op1=ALU.add),
                     reads=[Sf_t[h], blt, ptS], writes=[Sf_t[h]])
                P.op("pool", lambda e: e.tensor_copy(out=Sb[:, h, :], in_=Sf[:, h, :]), reads=[Sf_t[h]], writes=[Sb_t[h]])
            return stageB
        else:
            P.op("pool", lambda e: e.memset(khat4[:], 0.0), writes=[khat4_t])
            seqst = []
            for j in range(4):
                s0, s0t = S0.get()
                dma("sp", s0[:], st_d[j, h], writes=[s0t])
                s0b, s0bt = S0b.get()
                P.op("pool", lambda e, s0=s0, s0b=s0b: e.tensor_copy(out=s0b[:], in_=s0[:]), reads=[s0t], writes=[s0bt])
                for vh in range(2):
                    mm(psO[:, 256 + vh * 128 + 4 * j:256 + vh * 128 + 4 * j + 4], s0b[:, vh * 128:(vh + 1) * 128],
                       qt_[:, 4 * j:4 * j + 4], True, True, reads=[s0bt, qtt], writes=[ptO])
                P.op("pool", lambda e, j=j: e.tensor_scalar_mul(out=khat4[:, j, 4 * j:4 * j + 4], in0=kt[:, 4 * j:4 * j + 4],
                                                                scalar1=bl[:, j:j + 1]),
                     reads=[ktt, blt, khat4_t], writes=[khat4_t])
                seqst.append((s0, s0t))
            P.op("act", lambda e: e.activation(out=o_h[:, :, c0:c0 + nt],
                                               in_=psO[:, 0:256].rearrange("p (a b) -> p a b", a=2)[:, :, 0:nt],
                                               func=AF.Copy), reads=[ptO], writes=[o_t])
            P.op("dve", lambda e: e.tensor_add(out=o_h[:, :, c0:c0 + nt], in0=o_h[:, :, c0:c0 + nt],
                                               in1=psO[:, 256:512].rearrange("p (a b) -> p a b", a=2)[:, :, 0:nt]),
                 reads=[ptO, o_t], writes=[o_t])
            for j in range(4):
                s0, s0t = seqst[j]
                pT, pTt = ps_b()
                tr(pT[0:16, 0:128], khat4[:, j, :], identb[:], reads=[khat4_t, ct], writes=[pTt])
                khT, khTt = tmpb.get()
                act(khT[0:16, :], pT[0:16, 0:128], AF.Copy, reads=[pTt], writes=[khTt])
                psS, ptS = ps_f()
                mm(psS[:, 0:256], khT[0:16, :], gv[0:16, ti, :], True, True, reads=[khTt, gv_t[ti]], writes=[ptS])
                so, sot = Sout.get()
                P.op("dve", lambda e, so=so, s0=s0, j=j, psS=psS: e.scalar_tensor_tensor(
                    out=so[:], in0=s0[:], scalar=bl[:, j:j + 1], in1=psS[:, 0:256], op0=ALU.mult, op1=ALU.add),
                    reads=[s0t, blt, ptS], writes=[sot])
                dma("sp", gs_o[j, h], so[:], reads=[sot])

    def gla_batch(h, b, own, G):
        gkT, gkT_t, gqT, gqT_t, gv, gv_t = G["gkT"], G["gkT_t"], G["gqT"], G["gqT_t"], G["gv"], G["gv_t"]
        c0 = b * 512
        X = {}

        def A1():
            psZ, ptZ = ps_f()
            for i in range(4):
                mm(psZ[:, i * 128:(i + 1) * 128], glrT[0:17, c0 + i * 128:c0 + (i + 1) * 128], wgk[0:17, h * 128:(h + 1) * 128],
                   True, True, reads=[glrT_t, ct], writes=[ptZ])
            e1, e1t = f512.get()
            act(e1[:], psZ[:], AF.Exp, reads=[ptZ], writes=[e1t], scale=-1.0)
            lan, lant = f512.get()
            act(lan[:], e1[:], AF.Ln, reads=[e1t], writes=[lant], bias=1.0)
            X["lan"], X["lant"] = lan, lant

        def A2():
            lan, lant = X["lan"], X["lant"]
            psB, ptB = ps_f()
            for i in range(4):
                mm(psB[:, i * 128:(i + 1) * 128], lan[:, i * 128:(i + 1) * 128], Uf[:], True, True, reads=[lant, ct], writes=[ptB])
            enB, enBt = f512.get()
            act(enB[:], psB[:], AF.Exp, reads=[ptB], writes=[enBt], scale=1.0 / 16)
            bl, blt = blast.get()
            act(bl[:, 0:4], psB[:, 127:512:128], AF.Exp, reads=[ptB], writes=[blt], scale=-1.0 / 16)
            kt, ktt = f512.get()
            P.op("dve", lambda e: e.tensor_mul(out=kt[:], in0=gkT[:, c0:c0 + 512], in1=enB[:]), reads=[gkT_t, enBt], writes=[ktt])
            kh, kht = xh.get()
            P.op("pool", lambda e: e.tensor_mul(out=kh[:], in0=kt[:].rearrange("p (a b) -> p a b", a=4),
                                                in1=bl[:, 0:4].unsqueeze(2).broadcast_to([128, 4, 128])),
                 reads=[ktt, blt], writes=[kht])
            X.update(bl=bl, blt=blt, kh=kh, kht=kht)
            if own:
                eB, eBt = f512.get()
                act(eB[:], psB[:], AF.Exp, reads=[ptB, ct], writes=[eBt], scale=-1.0 / 16, bias=epsb[:, 1:2])
                qt_, qtt = xq.get()
                P.op("dve", lambda e: e.tensor_mul(out=qt_[:], in0=gqT[:, c0:c0 + 512], in1=eB[:]), reads=[gqT_t, eBt], writes=[qtt])
                ktb, ktbt = xk.get()
                P.op("pool", lambda e: e.tensor_copy(out=ktb[:], in_=kt[:]), reads=[ktt], writes=[ktbt])
                X.update(qt_=qt_, qtt=qtt, ktb=ktb, ktbt=ktbt)

        def A3():
            kh, kht = X["kh"], X["kht"]
            pT, pTt = ps_b()
            for i in range(4):
                tr(pT[:, i * 128:(i + 1) * 128], kh[:, i, :], identb[:], reads=[kht, ct], writes=[pTt])
            khT, khTt = xhT.get()
            act(khT[:], pT[:, 0:512], AF.Copy, reads=[pTt], writes=[khTt])
            X.update(khT=khT, khTt=khTt)
            if own:
                qt_, qtt, ktb, ktbt = X["qt_"], X["qtt"], X["ktb"], X["ktbt"]
                psA, ptA = ps_f()
                for i in range(4):
                    mm(psA[:, i * 128:(i + 1) * 128], ktb[:, i * 128:(i + 1) * 128], qt_[:, i * 128:(i + 1) * 128], True, True,
                       reads=[ktbt, qtt], writes=[ptA])
                Am, Amt = xa.get()
                P.op("dve", lambda e: e.tensor_mul(out=Am[:], in0=psA[:].rearrange("p (a b) -> p a b", a=4),
                                                   in1=Uf[:].unsqueeze(1).broadcast_to([128, 4, 128])), reads=[ptA, ct], writes=[Amt])
                X.update(Am=Am, Amt=Amt)

        def mk(i):
            ti = b * 4 + i
            t0 = ti * 128

            def stageB():
                bl, blt, khT, khTt = X["bl"], X["blt"], X["khT"], X["khTt"]
                if own:
                    Am, Amt, qt_, qtt = X["Am"], X["Amt"], X["qt_"], X["qtt"]
                    psO, ptO = ps_f()
                    for vh in range(2):
                        mm(psO[:, vh * 128:(vh + 1) * 128], gv[:, ti, vh * 128:(vh + 1) * 128], Am[:, i, :], True, False,
                           reads=[gv_t[ti], Amt], writes=[ptO])
                        mm(psO[:, vh * 128:(vh + 1) * 128], Sb[:, h, vh * 128:(vh + 1) * 128], qt_[:, i * 128:(i + 1) * 128],
                           False, True, reads=[Sb_t[h], qtt], writes=[ptO])
                psS, ptS = ps_f()
                mm(psS[:, 0:256], khT[:, i * 128:(i + 1) * 128], gv[:, ti, :], True, True, reads=[khTt, gv_t[ti]], writes=[ptS])
                if own:
                    P.op("act", lambda e: e.activation(out=o_h[:, :, t0:t0 + 128],
                                                       in_=psO[:, 0:256].rearrange("p (a b) -> p a b", a=2),
                                                       func=AF.Copy), reads=[ptO], writes=[o_t])
                P.op("dve", lambda e: e.scalar_tensor_tensor(out=Sf[:, h, :], in0=Sf[:, h, :], scalar=bl[:, i:i + 1],
                                                             in1=psS[:, 0:256], op0=ALU.mult, op1=ALU.add),
                     reads=[Sf_t[h], blt, ptS], writes=[Sf_t[h]])
                P.op("pool", lambda e: e.tensor_copy(out=Sb[:, h, :], in_=Sf[:, h, :]), reads=[Sf_t[h]], writes=[Sb_t[h]])
            return stageB
        return [A1, A2, A3], [mk(i) for i in range(4)]

    def gla_group(g, own):
        parts = PARTS_OWN if own else PARTS_PRE
        N = NTOK if own else 1024
        P.op("pool", lambda e: e.memset(glrT[:, 0:N], 1.0), writes=[glrT_t])
        wb, wt = wload([(0, C_GLR, 16)])

        def ev_glr(ps, pt, c0, n):
            act(glrT[0:16, c0:c0 + n], ps[0:16, 0:n], AF.Copy, reads=[pt], writes=[glrT_t])
        proj_fm(wb, wt, 0, 16, parts, ev_glr)
        ntile = 9 if own else 8

        def proj_items(h, G):
            gkT, gkT_t, gqT, gqT_t, gv, gv_t, ggs, ggs_t = (G["gkT"], G["gkT_t"], G["gqT"], G["gqT_t"], G["gv"], G["gv_t"],
                                                            G["ggs"], G["ggs_t"])
            cols = [(0, C_GK + h * 128, 128)] + ([(128, C_GQ + h * 128, 128)] if own else [])
            wb, wt = wload(cols)

            def ev_k(ps, pt, c0, n):
                act(gkT[:, c0:c0 + n], ps[:, 0:n], AF.Copy, reads=[pt], writes=[gkT_t])

            def ev_q(ps, pt, c0, n):
                P.op("dve", lambda e: e.tensor_copy(out=gqT[:, c0:c0 + n], in_=ps[:, 0:n]), reads=[pt], writes=[gqT_t])
            for p_ in parts:
                proj_fm(wb, wt, 0, 128, [p_], ev_k)
                yield
            if own:
                for p_ in parts:
                    proj_fm(wb, wt, 128, 128, [p_], ev_q)
                    yield
            wb, wt = wload([(0, C_GV + h * 256, 256)])
            for ti in range(ntile):
                m = 128 if ti < 8 else 16

                def ev_v(ps, pt, ti=ti, m=m):
                    P.op("dve", lambda e: e.tensor_copy(out=gv[0:m, ti, :], in_=ps[0:m, 0:256]), reads=[pt], writes=[gv_t[ti]])
                proj_tm(wb, wt, 256, ti * 128, m, ev_v)
                yield
            if own:
                wb, wt = wload([(0, C_GG + h * 256, 256)])
                for vh in range(2):
                    def ev_g(ps, pt, c0, n, vh=vh):
                        act(ggs[:, vh, c0:c0 + n], ps[:, 0:n], AF.Silu, reads=[pt], writes=[ggs_t])
                    for p_ in parts:
                        proj_fm(wb, wt, vh * 128, 128, [p_], ev_g)
                        yield

        def head_norm(h, G):
            ggs, ggs_t = G["ggs"], G["ggs_t"]
            pss = [ps_f() for _ in parts]
            for vh in range(2):
                q, qt = sqb.get()
                act(q[:, 0:N], o_h[:, vh, 0:N], AF.Square, reads=[o_t], writes=[qt])
                for (c0, n), (ps, pt) in zip(parts, pss):
                    mm(ps[:, 0:n], onesb[:], q[:, c0:c0 + n], vh == 0, vh == 1, reads=[qt, ct], writes=[pt])
            for (c0, n), (ps, pt) in zip(parts, pss):
                act(sgate[:, c0:c0 + n], ps[:, 0:n], AF.Ln, reads=[pt, ct], writes=[sgate_t], scale=1.0 / 256,
                    bias=epsb[:, 0:1])
            act(sgate[:, 0:N], sgate[:, 0:N], AF.Exp, reads=[sgate_t], writes=[sgate_t], scale=-0.5)
            for vh in range(2):
                P.op("dve", lambda e, vh=vh: e.scalar_tensor_tensor(
                    out=o_h[:, vh, 0:N], in0=o_h[:, vh, 0:N], scalar=nw[:, NW_GLA + vh:NW_GLA + vh + 1],
                    in1=sgate[:, 0:N], op0=ALU.mult, op1=ALU.mult), reads=[o_t, sgate_t, ct], writes=[o_t])
                P.op("pool", lambda e, vh=vh: e.tensor_mul(out=AT[:, 2 * h + vh, 0:N], in0=o_h[:, vh, 0:N],
                                                          in1=ggs[:, vh, 0:N]),
                     reads=[o_t, ggs_t], writes=[AT_t[2 * h + vh]])

        for _ in proj_items(0, GB[0]):
            pass
        for h in range(4):
            G = GB[h % 2]
            nxt = proj_items(h + 1, GB[(h + 1) % 2]) if h < 3 else None
            (a0, b0), (a1, b1) = gla_batch(h, 0, own, G), gla_batch(h, 1, own, G)
            steps = [a0[0], a1[0], a0[1], a1[1], a0[2], a1[2]] + b0 + b1
            if own:
                steps.append(lambda h=h, G=G: gla_tile(h, 1024, 16, own, True, G))
                steps.append(lambda h=h, G=G: head_norm(h, G))
            nit = (24 if own else 10)
            per = -(-nit // len(steps))
            for st in steps:
                st()
                if nxt is not None:
                    for _ in range(per):
                        if next(nxt, "done") == "done":
                            nxt = None
                            break
            if nxt is not None:
                for _ in nxt:
                    pass

    A3 = Arena(nc, A1_mark)
    ks_s = A3.alloc([16, 1024], F32)
    vs_s = A3.alloc([16, 1024], F32)
    qs_s = A3.alloc([16, 1024], BF16)
    ks_t, vs_t, qs_t = Trk(), Trk(), Trk()
    amask = A3.alloc([128, 3 * 8 * 256], BF16)
    kb = A3.alloc([128, NU], F32)
    selb = A3.alloc([128, 256], BF16)
    selq = A3.alloc([16, 16 * 128], BF16)
    sbias = A3.alloc([128, 24], F32)
    anwb = A3.alloc([16, 1024], F32)

    def load_attn_consts():
        for dst, src in ((amask[:], amask_d), (kb[:], kb_d), (selb[:], selb_d), (selq[:], selq_d),
                         (sbias[:], sbias_d), (anwb[:], anwb_d)):
            dma("sp", dst, src, writes=[ct])
    A3_mark = A3.off
    aqT = A3.alloc([128, NTOK], BF16)
    akT = A3.alloc([128, 3072 + 16], BF16)
    avT = A3.alloc([128, 3072 + 16], BF16)
    aqT_t, akT_t, avT_t = Trk(), Trk(), Trk()
    acc = A3.alloc([128, 2, 1024], F32)
    acc_t = Trk()
    ssatt = A3.alloc([128, NTOK], F32)
    ssatt_t = Trk()
    et = Rot(A3, [128, 256], F32, 4)
    ptl = Rot(A3, [128, 256], BF16, 6)
    vbl = Rot(A3, [128, 128], BF16, 8)
    kvst = Rot(A3, [128, 1024], BF16, 3)
    kout = Rot(A3, [128, 256], F32, 3)
    attf = A3.alloc([128, 1024], F32)
    attf_t = Trk()
    A3 = Arena(nc, A3_mark)
    Kt = Rot(A3, [128, 1024], BF16, 4)
    Vt = Rot(A3, [128, 1024], BF16, 5)
    prod = Rot(A3, [128, 1024], F32, 2)
    pvb = Rot(A3, [128, 1024], BF16, 3)
    sc8 = Rot(A3, [128, 8], F32, 10)
    p8b = Rot(A3, [128, 8], BF16, 4)
    sm16 = Rot(A3, [16, 1024], F32, 2)
    sm8 = Rot(A3, [16, 8], F32, 4)
    atts_b = A3.alloc([16, 1024], BF16)
    atts_bt = Trk()

    def kv_prefix_group(g):
        for h in range(8):
            wb, wt = wload([(0, C_AK + h * 128, 128), (128, C_AV + h * 128, 128)])
            for which in range(2):
                st_, stt = kvst.get()

                def ev(ps, pt, c0, n, st_=st_, stt=stt):
                    act(st_[:, c0:c0 + n], ps[:, 0:n], AF.Copy, reads=[pt], writes=[stt])
                proj_fm(wb, wt, which * 128, 128, PARTS_PRE, ev)
                dma("sp", kvscr[which, h, :, (g - 1) * 1024:g * 1024], st_[:], reads=[stt], writes=[kvscr_t[which][h]])

    def kv_outputs():
        for which, (cb, outd, outs, sdst, sdt) in enumerate(((C_AK, k_o, ks_o, ks_s, ks_t), (C_AV, v_o, vs_o, vs_s, vs_t))):
            for cbk in range(4):
                wb, wt = wload([(0, cb + cbk * 256, 256)])
                for ti in range(9):
                    m = 128 if ti < 8 else 16
                    ko, kot = kout.get()

                    def ev(ps, pt, ko=ko, kot=kot, m=m):
                        act(ko[0:m, :], ps[0:m, 0:256], AF.Copy, reads=[pt], writes=[kot])
                    proj_tm(wb, wt, 256, ti * 128, m, ev)
                    if ti < 8:
                        dma("sp", outd[ti * 128:(ti + 1) * 128, cbk * 256:(cbk + 1) * 256], ko[:], reads=[kot])
                    else:
                        dma("sp", outs[:, cbk * 256:(cbk + 1) * 256], ko[0:16, :], reads=[kot], writes=[ksvs_t[which]])
                        P.op("pool", lambda e, ko=ko, sdst=sdst, cbk=cbk: e.tensor_copy(
                            out=sdst[:, cbk * 256:(cbk + 1) * 256], in_=ko[0:16, :]), reads=[kot], writes=[sdt])

    def attn_head(h):
        wb, wt = wload([(0, C_AQ + h * 128, 128), (128, C_AK + h * 128, 128)])
        wb2, wt2 = wload([(0, C_AV + h * 128, 128)])
        dma("sp", akT[:, 0:2048], kvscr[0, h], reads=[kvscr_t[0][h]], writes=[akT_t])
        dma("sp", avT[:, 0:2048], kvscr[1, h], reads=[kvscr_t[1][h]], writes=[avT_t])

        def ev_q(ps, pt, c0, n):
            act(aqT[:, c0:c0 + n], ps[:, 0:n], AF.Copy, reads=[pt], writes=[aqT_t])

        def ev_k(ps, pt, c0, n):
            P.op("dve", lambda e: e.tensor_copy(out=akT[:, 2048 + c0:2048 + c0 + n], in_=ps[:, 0:n]), reads=[pt], writes=[akT_t])

        def ev_v(ps, pt, c0, n):
            act(avT[:, 2048 + c0:2048 + c0 + n], ps[:, 0:n], AF.Copy, reads=[pt], writes=[avT_t])
        proj_fm(wb, wt, 0, 128, PARTS_OWN, ev_q)
        proj_fm(wb, wt, 128, 128, PARTS_OWN, ev_k)
        proj_fm(wb2, wt2, 0, 128, PARTS_OWN, ev_v)
        P.op("pool", lambda e: e.memset(acc[:], 0.0), writes=[acc_t])
        def S1(ui):
            u = UNITS[ui]
            D, nk, nq = u["D"], u["nk"], u["nq"]
            ksl = slice(u["kc0"], u["kc0"] + (nk - 1) * D + 1, D)
            qsl = slice(u["qc0"], u["qc0"] + (nq - 1) * D + 1, D)
            pT, pTt = ps_b()
            tr(pT[0:nk, 0:128], avT[:, ksl], identb[:], reads=[avT_t, ct], writes=[pTt])
            vb, vbt = vbl.get()
            act(vb[0:nk, :], pT[0:nk, 0:128], AF.Copy, reads=[pTt], writes=[vbt])
            ps, pt = ps_f()
            mm(ps[0:nk, 0:nq], akT[:, ksl], aqT[:, qsl], True, True, reads=[akT_t, aqT_t], writes=[pt])
            return dict(ui=ui, u=u, nk=nk, nq=nq, qsl=qsl, vb=vb, vbt=vbt, ps=ps, pt=pt)

        def S2(c):
            nk, nq, ui, u = c["nk"], c["nq"], c["ui"], c["u"]
            e_, e_t = et.get()
            act(e_[0:nk, 0:nq], c["ps"][0:nk, 0:nq], AF.Exp, reads=[c["pt"], ct], writes=[e_t], scale=SCALE,
                bias=kb[0:nk, ui:ui + 1])
            p_, p_t = ptl.get()
            mbase = (u["di"] * 8 + h) * 256 + u["m0"]
            P.op("pool", lambda e, p_=p_, e_=e_, nk=nk, nq=nq, mbase=mbase: e.tensor_mul(
                out=p_[0:nk, 0:nq], in0=e_[0:nk, 0:nq], in1=amask[0:nk, mbase:mbase + nq]), reads=[e_t, ct], writes=[p_t])
            c["p_"], c["p_t"] = p_, p_t

        def S3(c):
            nk, nq, qsl, p_, p_t = c["nk"], c["nq"], c["qsl"], c["p_"], c["p_t"]
            ps2, pt2 = ps_f()
            mm(ps2[:, 0:nq], c["vb"][0:nk, :], p_[0:nk, 0:nq], True, True, reads=[c["vbt"], p_t], writes=[pt2])
            mm(ps2[:, 256:256 + nq], onesb[0:nk, :], p_[0:nk, 0:nq], True, True, reads=[p_t, ct], writes=[pt2])
            P.op("dve", lambda e, ps2=ps2, qsl=qsl, nq=nq: e.tensor_add(
                out=acc[:, :, qsl], in0=acc[:, :, qsl], in1=ps2[:, 0:512].rearrange("p (a b) -> p a b", a=2)[:, :, 0:nq]),
                reads=[acc_t, pt2], writes=[acc_t])
        ctxs = {}
        for i in range((NU + 1) // 2 + 2):
            for u_ in (2 * i, 2 * i + 1):
                if u_ < NU:
                    ctxs[u_] = S1(u_)
            for u_ in (2 * i - 2, 2 * i - 1):
                if 0 <= u_ < NU:
                    S2(ctxs[u_])
            for u_ in (2 * i - 4, 2 * i - 3):
                if 0 <= u_ < NU:
                    S3(ctxs.pop(u_))
        assert not ctxs
        act(acc[:, 1, :], acc[:, 1, :], AF.Ln, reads=[acc_t], writes=[acc_t])
        act(acc[:, 1, :], acc[:, 1, :], AF.Exp, reads=[acc_t], writes=[acc_t], scale=-1.0)
        P.op("dve", lambda e: e.tensor_mul(out=attf[:], in0=acc[:, 0, :], in1=acc[:, 1, :]),
             reads=[acc_t], writes=[attf_t])
        P.op("pool", lambda e: e.tensor_copy(out=AT[:, 8 + h, 0:1024], in_=attf[:]), reads=[attf_t], writes=[AT_t[8 + h]])
        q, qt = sqb.get()
        act(q[:, 0:1024], attf[:], AF.Square, reads=[attf_t], writes=[qt])
        for (c0, n) in PARTS_PRE:
            ps, pt = ps_f()
            mm(ps[:, 0:n], onesb[:], q[:, c0:c0 + n], True, True, reads=[qt, ct], writes=[pt])
            if h == 0:
                act(ssatt[:, c0:c0 + n], ps[:, 0:n], AF.Copy, reads=[pt], writes=[ssatt_t])
            else:
                P.op("dve", lambda e, ps=ps, c0=c0, n=n: e.tensor_add(out=ssatt[:, c0:c0 + n], in0=ssatt[:, c0:c0 + n],
                                                                       in1=ps[:, 0:n]), reads=[pt, ssatt_t], writes=[ssatt_t])
        pT, pTt = ps_b()
        tr(pT[0:16, 0:128], aqT[:, 1024:1040], identb[:], reads=[aqT_t, ct], writes=[pTt])
        act(qs_s[:, h * 128:(h + 1) * 128], pT[0:16, 0:128], AF.Copy, reads=[pTt], writes=[qs_t])

    def attn_finish():
        act(ssatt[:, 0:1024], ssatt[:, 0:1024], AF.Ln, reads=[ssatt_t, ct], writes=[ssatt_t], scale=1.0 / 1024, bias=epsb[:, 0:1])
        act(ssatt[:, 0:1024], ssatt[:, 0:1024], AF.Exp, reads=[ssatt_t], writes=[ssatt_t], scale=-0.5)
        for h in range(8):
            P.op("dve", lambda e, h=h: e.scalar_tensor_tensor(
                out=AT[:, 8 + h, 0:1024], in0=AT[:, 8 + h, 0:1024], scalar=nw[:, NW_ATT + h:NW_ATT + h + 1],
                in1=ssatt[:, 0:1024], op0=ALU.mult, op1=ALU.mult), reads=[AT_t[8 + h], ssatt_t, ct], writes=[AT_t[8 + h]])

    def sample_attention():
        psN = [(psf[0], psf_t[0]), (psf[1], psf_t[1])]
        psD = (psf[2], psf_t[2])
        psQ = [(psf[3], psf_t[3]), (psf[4], psf_t[4])]
        nunit = 4 * 3 * 4
        ulist = [(j, di, D, i) for j in range(4) for di, D in enumerate((1, 4, 16)) for i in range(4)]

        def SA(k):
            j, di, D, i = ulist[k]
            Kb, Kbt = Kt.get()
            Vb, Vbt = Vt.get()
            r0 = 2048 + i - D * 128
            ncache = 128 if D > 1 else 128 - i
            for (buf, bt, cd, sd, sdt) in ((Kb, Kbt, ck_d, ks_o, ksvs_t[0]), (Vb, Vbt, cv_d, vs_o, ksvs_t[1])):
                if D == 1:
                    dma("pool", buf[0:112, :], cd[j, r0:r0 + 112, :], writes=[bt])
                    dma("pool", buf[112:ncache, :], cd[j, r0 + 112:r0 + ncache, :], writes=[bt])
                    if i > 0:
                        dma("pool", buf[ncache:128, :], sd[4 * j:4 * j + i, :], reads=[sdt], writes=[bt])
                else:
                    dma("pool", buf[:, :], cd[j, r0:r0 + 127 * D + 1:D, :], writes=[bt])
            return dict(k=k, tok=4 * j + i, di=di, Kb=Kb, Kbt=Kbt, Vb=Vb, Vbt=Vbt)

        def SB(c):
            tok, di, Kb, Kbt = c["tok"], c["di"], c["Kb"], c["Kbt"]
            for hh in range(2):
                mm(psQ[hh][0][:, :], selq[0:16, tok * 128:(tok + 1) * 128], qs_s[:, hh * 512:(hh + 1) * 512], True, True,
                   reads=[qs_t, ct], writes=[psQ[hh][1]])
            pr, prt = prod.get()
            for hh in range(2):
                P.op("dve", lambda e, pr=pr, Kb=Kb, hh=hh: e.tensor_mul(
                    out=pr[:, hh * 512:(hh + 1) * 512], in0=Kb[:, hh * 512:(hh + 1) * 512], in1=psQ[hh][0][:, :]),
                    reads=[Kbt, psQ[hh][1]], writes=[prt])
            s8, s8t = sc8.get()
            P.op("dve", lambda e, s8=s8, pr=pr: e.tensor_reduce(
                out=s8[:], in_=pr[:].rearrange("p (h e) -> p h e", h=8), axis=AX.X, op=ALU.add), reads=[prt], writes=[s8t])
            s9, s9t = sc8.get()
            P.op("dve", lambda e, s8=s8, s9=s9, di=di: e.scalar_tensor_tensor(
                out=s9[:], in0=s8[:], scalar=SCALE, in1=sbias[:, di * 8:(di + 1) * 8], op0=ALU.mult, op1=ALU.add),
                reads=[s8t, ct], writes=[s9t])
            c["s9"], c["s9t"] = s9, s9t

        def SC(c):
            Vb, Vbt, s9, s9t = c["Vb"], c["Vbt"], c["s9"], c["s9t"]
            pe_, pet = sc8.get()
            act(pe_[:], s9[:], AF.Exp, reads=[s9t], writes=[pet])
            p8, p8t = p8b.get()
            act(p8[:], pe_[:], AF.Copy, reads=[pet], writes=[p8t])
            pv, pvt = pvb.get()
            for hd in range(4):
                act(pv[:, hd * 128:(hd + 1) * 128], Vb[:, hd * 128:(hd + 1) * 128], AF.Copy, reads=[Vbt, pet], writes=[pvt],
                    scale=pe_[:, hd:hd + 1])
            P.op("pool", lambda e, pv=pv, Vb=Vb, pe_=pe_: e.tensor_mul(
                out=pv[:, 512:1024].rearrange("p (h e) -> p h e", h=4), in0=Vb[:, 512:1024].rearrange("p (h e) -> p h e", h=4),
                in1=pe_[:, 4:8].unsqueeze(2).broadcast_to([128, 4, 128])), reads=[Vbt, pet], writes=[pvt])
            c["p8"], c["p8t"], c["pv"], c["pvt"] = p8, p8t, pv, pvt

        def SD(c):
            k, tok, pv, pvt, p8, p8t = c["k"], c["tok"], c["pv"], c["pvt"], c["p8"], c["p8t"]
            for hh in range(2):
                mm(psN[hh][0][0:16, :], selb[:, tok * 16:(tok + 1) * 16], pv[:, hh * 512:(hh + 1) * 512],
                   k == 0, k == nunit - 1, reads=[pvt, ct], writes=[psN[hh][1]])
            mm(psD[0][0:16, 0:8], selb[:, tok * 16:(tok + 1) * 16], p8[:], k == 0, k == nunit - 1,
               reads=[p8t, ct], writes=[psD[1]])
        cx = {}
        for it in range(nunit + 4):
            if it < nunit:
                cx[it] = SA(it)
            if 0 <= it - 2 < nunit:
                SB(cx[it - 2])
            if 0 <= it - 3 < nunit:
                SC(cx[it - 3])
            if 0 <= it - 4 < nunit:
                SD(cx.pop(it - 4))
        t1, t1t = sm16.get()
        P.op("dve", lambda e: e.tensor_mul(out=t1[:], in0=qs_s[:], in1=ks_s[:]), reads=[qs_t, ks_t], writes=[t1t])
        s1, s1t = sm8.get()
        P.op("dve", lambda e: e.tensor_reduce(out=s1[:], in_=t1[:].rearrange("p (h e) -> p h e", h=8), axis=AX.X, op=ALU.add),
             reads=[t1t], writes=[s1t])
        p1, p1t = sm8.get()
        act(p1[:], s1[:], AF.Exp, reads=[s1t], writes=[p1t], scale=SCALE)
        den, dent = sm8.get()
        P.op("dve", lambda e: e.scalar_tensor_tensor(out=den[:], in0=p1[:], scalar=3.0, in1=psD[0][0:16, 0:8],
                                                     op0=ALU.mult, op1=ALU.add), reads=[p1t, psD[1]], writes=[dent])
        P.op("dve", lambda e: e.reciprocal(out=den[:], in_=den[:]), reads=[dent], writes=[dent])
        t2, t2t = sm16.get()
        P.op("pool", lambda e: e.tensor_mul(out=t2[:].rearrange("p (h e) -> p h e", h=8),
                                            in0=vs_s[:].rearrange("p (h e) -> p h e", h=8),
                                            in1=p1[:].unsqueeze(2).broadcast_to([16, 8, 128])), reads=[vs_t, p1t], writes=[t2t])
        t3, t3t = sm16.get()
        for hh in range(2):
            P.op("dve", lambda e, hh=hh: e.scalar_tensor_tensor(
                out=t3[:, hh * 512:(hh + 1) * 512], in0=t2[:, hh * 512:(hh + 1) * 512], scalar=3.0, in1=psN[hh][0][0:16, :],
                op0=ALU.mult, op1=ALU.add), reads=[t2t, psN[hh][1]], writes=[t3t])
        P.op("dve", lambda e: e.tensor_mul(out=t3[:].rearrange("p (h e) -> p h e", h=8),
                                           in0=t3[:].rearrange("p (h e) -> p h e", h=8),
                                           in1=den[:].unsqueeze(2).broadcast_to([16, 8, 128])), reads=[t3t, dent], writes=[t3t])
        t4, t4t = sm16.get()
        ss1, ss1t = sm8.get()
        act(t4[:], t3[:], AF.Square, reads=[t3t], writes=[t4t, ss1t], accum_out=ss1[:, 0:1])
        act(ss1[:, 0:1], ss1[:, 0:1], AF.Sqrt, reads=[ss1t, ct], writes=[ss1t], scale=1.0 / 1024, bias=epsb[0:16, 0:1])
        P.op("dve", lambda e: e.reciprocal(out=ss1[:, 0:1], in_=ss1[:, 0:1]), reads=[ss1t], writes=[ss1t])
        P.op("dve", lambda e: e.scalar_tensor_tensor(out=atts_b[:], in0=t3[:], scalar=ss1[:, 0:1], in1=anwb[:],
                                                     op0=ALU.mult, op1=ALU.mult), reads=[t3t, ss1t, ct], writes=[atts_bt])
        for h in range(8):
            pT, pTt = ps_b()
            tr(pT[:, 0:16], atts_b[:, h * 128:(h + 1) * 128], identb[0:16, 0:16], reads=[atts_bt, ct], writes=[pTt])
            act(AT[:, 8 + h, 1024:1040], pT[:, 0:16], AF.Copy, reads=[pTt], writes=[AT_t[8 + h]])

    xTv = xT.rearrange("(kc p) n -> kc p n", p=128)
    xsTv = xsT.rearrange("(kc p) n -> kc p n", p=128)
    for g in range(4):
        own = g == 3
        if DEBUG == "gla0" and not own:
            continue
        parts = PARTS_OWN if own else PARTS_PRE

        def src_chunks(kc, g=g, own=own):
            b, t = xc.get()
            dma("sp" if kc % 2 == 0 else "pool", b[:, 0:1024], xTv[kc, :, g * 1024:(g + 1) * 1024], writes=[t])
            if own:
                P.op("pool", lambda e, b=b, kc=kc: e.tensor_copy(out=b[:, 1024:1040], in_=xs_all[:, kc, :]), reads=[xs_t], writes=[t])
            return b[:, 0:(NTOK if own else 1024)], [t]
        ssq_norm(src_chunks, parts, NW_ATTN, hT, hT_t, 2048.0)
        dumpf(0, rB[:, :], [rB_t])
        dumpb(0, hT[:, 0, :], [hT_t[0]])
        dumpb(1, hT[:, 15, :], [hT_t[15]])
        gla_group(g, own)
        if g in (1, 2):
            P.barrier()
            kv_prefix_group(g)
            P.barrier()
    for h in range(4):
        dma("sp", gst_o[h], Sf[:, h, :], reads=[Sf_t[h]])
    P.barrier()
    load_attn_consts()
    kv_outputs()
    if DEBUG == "A":
        raise StopBuild()
    for h in range(8):
        attn_head(h)
    attn_finish()
    P.barrier()
    if DEBUG == "B":
        raise StopBuild()
    wo_issued = []

    def wo_issue(cb):
        b_, t_ = wblk.get()
        dma("pool", b_[:], wo_flat[:, cb * 4096:(cb + 1) * 4096], writes=[t_])
        wo_issued.append((b_, t_))
    for cb_ in range(3):
        wo_issue(cb_)
    sample_attention()
    P.barrier()
    if DEBUG == "C":
        raise StopBuild()
    if DEBUG == "F":
        for sl, c in zip(range(2, 8), (0, 1, 7, 8, 9, 15)):
            dumpb(sl, AT[:, c, :], [AT_t[c]])

    A4 = Arena(nc, P1C_START)
    x1T = A4.alloc([128, 16, NTOK], F32)
    x1T_t = [Trk() for _ in range(16)]
    x1T_end = A4.off
    xres = Rot(A4, [128, NTOK], F32, 2)
    wo = wblk
    for cb in range(8):
        if cb >= 1 and cb + 2 < 8:
            wo_issue(cb + 2)
        b, t = wo_issued[cb]
        b = b[:].rearrange("p (kc n) -> p kc n", kc=16)
        for sub in range(2):
            dc = cb * 2 + sub
            xr, xrt = xres.get()
            dma("sp", xr[:, 0:1024], xTv[dc, :, 3072:4096], writes=[xrt])
            P.op("pool", lambda e, xr=xr, dc=dc: e.tensor_copy(out=xr[:, 1024:1040], in_=xs_all[:, dc, :]), reads=[xs_t], writes=[xrt])
            for (c0, n) in PARTS_MLP:
                ps, pt = ps_f()
                for ac in range(16):
                    mm(ps[:, 0:n], b[:, ac, sub * 128:(sub + 1) * 128], AT[:, ac, c0:c0 + n], ac == 0, ac == 15,
                       reads=[t, AT_t[ac]], writes=[pt])
                P.op("dve", lambda e, ps=ps, xr=xr, dc=dc, c0=c0, n=n: e.tensor_add(
                    out=x1T[:, dc, c0:c0 + n], in0=ps[:, 0:n], in1=xr[:, c0:c0 + n]), reads=[pt, xrt], writes=[x1T_t[dc]])
    if DEBUG == "F":
        dumpf(1, x1T[:, 0, :], [x1T_t[0]])
        dumpf(2, x1T[:, 15, :], [x1T_t[15]])
        P.barrier()
    h2T, h2T_t = hT, hT_t

    def src2(kc):
        return x1T[:, kc, :], [x1T_t[kc]]
    ssq_norm(src2, PARTS_MLP, NW_FFN, h2T, h2T_t, 2048.0)
    P.barrier()
    if DEBUG == "D":
        raise StopBuild()

    A5 = Arena(nc, R_START)
    wu = Rot(A5, [128, 16, 512], BF16, 2)
    wd = Rot(A5, [128, 4, 2048], BF16, 2)
    HT = Rot(A5, [128, 4, NTOK], BF16, 1)
    assert A5.off <= P1C_START, (A5.off, P1C_START)
    A5b = Arena(nc, x1T_end)
    HT.b.append(A5b.alloc([128, 4, NTOK], BF16)); HT.t.append(Trk())
    rl = Rot(A5b, [128, 512], F32, 3)
    for g in range(16):
        ub, ut = wu.get()
        dma("pool", ub[:].rearrange("p a b -> p (a b)"), wu_flat[:, g * 8192:(g + 1) * 8192], writes=[ut])
        db, dt_ = wd.get()
        dma("pool", db[:], w_down[g * 512:(g + 1) * 512, :].rearrange("(fc p) n -> p fc n", p=128), writes=[dt_])
        hb, hbt = HT.get()
        for fc in range(4):
            for (c0, n) in PARTS_MLP:
                ps, pt = ps_f()
                for kc in range(16):
                    mm(ps[:, 0:n], ub[:, kc, fc * 128:(fc + 1) * 128], h2T[:, kc, c0:c0 + n], kc == 0, kc == 15,
                       reads=[ut, h2T_t[kc]], writes=[pt])
                r_, r_t = rl.get()
                act(r_[:, 0:n], ps[:, 0:n], AF.Relu, reads=[pt], writes=[r_t])
                P.op("pool", lambda e, hb=hb, r_=r_, fc=fc, c0=c0, n=n: e.tensor_mul(
                    out=hb[:, fc, c0:c0 + n], in0=r_[:, 0:n], in1=r_[:, 0:n]), reads=[r_t], writes=[hbt])
        for dc in range(16):
            for (c0, n) in PARTS_MLP:
                ps, pt = ps_f()
                for fc in range(4):
                    mm(ps[:, 0:n], db[:, fc, dc * 128:(dc + 1) * 128], hb[:, fc, c0:c0 + n], fc == 0, fc == 3,
                       reads=[dt_, hbt], writes=[pt])
                P.op("dve", lambda e, ps=ps, dc=dc, c0=c0, n=n: e.tensor_add(
                    out=x1T[:, dc, c0:c0 + n], in0=x1T[:, dc, c0:c0 + n], in1=ps[:, 0:n]), reads=[pt, x1T_t[dc]], writes=[x1T_t[dc]])
    P.barrier()

    if DEBUG == "E":
        raise StopBuild()
    if DEBUG == "F":
        dumpf(3, x1T[:, 0, :], [x1T_t[0]])
        dumpf(4, x1T[:, 15, :], [x1T_t[15]])
        P.barrier()
    A6 = Arena(nc, R_START)
    sqb4 = Rot(A6, [128, NTOK], BF16, 2)
    rB4 = A6.alloc([128, NTOK], F32)
    rB4_t = Trk()
    yT, yT_t = x1T, x1T_t
    ytok = Rot(A6, [128, 2048], F32, 2)
    pss = [ps_f() for _ in PARTS_OWN]
    for kc in range(16):
        q, qt = sqb4.get()
        act(q[:, 0:NTOK], x1T[:, kc, :], AF.Square, reads=[x1T_t[kc]], writes=[qt])
        for (c0, n), (ps, pt) in zip(PARTS_OWN, pss):
            mm(ps[:, 0:n], onesb[:], q[:, c0:c0 + n], kc == 0, kc == 15, reads=[qt, ct], writes=[pt])
    for (c0, n), (ps, pt) in zip(PARTS_OWN, pss):
        act(rB4[:, c0:c0 + n], ps[:, 0:n], AF.Ln, reads=[pt, ct], writes=[rB4_t], scale=1.0 / 2048, bias=epsb[:, 0:1])
    act(rB4[:, :], rB4[:, :], AF.Exp, reads=[rB4_t], writes=[rB4_t], scale=-0.5)
    for kc in range(16):
        P.op("dve", lambda e, kc=kc: e.scalar_tensor_tensor(
            out=yT[:, kc, :], in0=x1T[:, kc, :], scalar=nw[:, NW_FIN + kc:NW_FIN + kc + 1], in1=rB4[:, :],
            op0=ALU.mult, op1=ALU.mult), reads=[x1T_t[kc], rB4_t, ct], writes=[yT_t[kc]])
    for ti in range(9):
        m = 128 if ti < 8 else 16
        yb, ybt = ytok.get()
        for q4 in range(4):
            ps, pt = ps_f()
            for s in range(4):
                kc = q4 * 4 + s
                tr(ps[0:m, s * 128:(s + 1) * 128], yT[:, kc, ti * 128:ti * 128 + m], identf[:], reads=[yT_t[kc], ct], writes=[pt])
            if q4 % 2 == 0:
                act(yb[0:m, q4 * 512:(q4 + 1) * 512], ps[0:m, :], AF.Copy, reads=[pt], writes=[ybt])
            else:
                P.op("dve", lambda e, yb=yb, ps=ps, q4=q4, m=m: e.tensor_copy(out=yb[0:m, q4 * 512:(q4 + 1) * 512], in_=ps[0:m, :]),
                     reads=[pt], writes=[ybt])
        if ti < 8:
            dma("sp", y_o[ti * 128:(ti + 1) * 128, :], yb[:], reads=[ybt])
        else:
            dma("sp", ys_o[:, :], yb[0:16, :], reads=[ybt])


_CACHE = {}


def _consts():
    import ml_dtypes
    bf = ml_dtypes.bfloat16
    c = {}
    c["identf"] = np.eye(128, dtype=np.float32)
    c["identb"] = np.eye(128, dtype=np.float32).astype(bf)
    c["onesb"] = np.ones((128, 128), np.float32).astype(bf)
    s = np.arange(128)
    c["U"] = (s[:, None] <= s[None, :]).astype(np.float32)
    t = np.arange(16)
    c["Ub"] = ((t[:, None] <= t[None, :]) & (t[:, None] // 4 == t[None, :] // 4)).astype(np.float32)
    slopes = np.exp2(-8.0 * np.arange(1, 9, dtype=np.float32) / 8).astype(np.float32)
    kk = np.arange(128)[:, None]
    qq = np.arange(256)[None, :]
    dist = qq - kk
    valid = (dist >= 0) & (dist <= 128)
    am = np.zeros((128, 3, 8, 256), np.float32)
    for di, D in enumerate((1, 4, 16)):
        for h in range(8):
            am[:, di, h, :] = np.where(valid, np.exp(-slopes[h] * D * np.maximum(dist, 0).astype(np.float32)), 0.0)
    c["amask"] = am.reshape(128, -1).astype(bf)
    selb = np.zeros((128, 16, 16), np.float32)
    for tok in range(16):
        selb[:, tok, tok] = 1.0
    c["selb"] = selb.reshape(128, 256).astype(bf)
    selq = np.zeros((16, 16, 128), np.float32)
    for tok in range(16):
        selq[tok, tok, :] = 1.0
    c["selq"] = selq.reshape(16, -1).astype(bf)
    m = np.arange(128)
    sb = np.zeros((128, 3, 8), np.float32)
    for di, D in enumerate((1, 4, 16)):
        sb[:, di, :] = -slopes[None, :] * (D * (128 - m))[:, None]
    c["sbias"] = sb.reshape(128, 24)
    c["smask"] = (t[:, None] // 4 == np.arange(4)[None, :]).astype(np.float32)
    return c


def _kb(ci):
    s = ci * 1024
    kb = np.zeros((128, NU), np.float32)
    for ui, u in enumerate(UNITS):
        kk = np.arange(128)
        uloc = (128 * u["j"] + kk) * u["D"] + u["r"]
        tglob = s - 2048 + uloc
        kb[:, ui] = np.where(tglob >= 0, 0.0, -30000.0)
    return kb


def kernel(x_prompt, x_sample, cache_k_win, cache_v_win, state_gla, attn_norm_w, w_in, w_gk_up, b_gk,
           gla_norm_w, att_out_norm_w, w_out, ffn_norm_w, w_up, w_down, final_norm_w):
    f = lambda a: np.ascontiguousarray(np.asarray(a, dtype=np.float32))
    x_prompt, x_sample = f(x_prompt), f(x_sample)
    if "nc" not in _CACHE:
        _CACHE["nc"] = build()
    nc = _CACHE["nc"]
    consts = _consts()
    col = lambda w, n: f(w).reshape(n, 128).T
    nwm = np.concatenate([col(attn_norm_w[0], 16), col(ffn_norm_w[0], 16), col(final_norm_w, 16),
                          col(gla_norm_w[0], 2), col(att_out_norm_w[0], 8)], axis=1)
    def tile_cols(w, c0, n):
        return w[:, c0:c0 + n].reshape(16, 128, n).transpose(1, 0, 2).reshape(128, 16 * n)
    w_in0, w_out0, w_up0 = f(w_in[0]), f(w_out[0]), f(w_up[0])
    wflat = np.empty((128, WTOT[0]), np.float32)
    for key, o in WOFF.items():
        blk = np.concatenate([w_in0[:, sc:sc + n] for (sc, n) in key], axis=1)
        W = blk.shape[1]
        wflat[:, o:o + 16 * W] = tile_cols(blk, 0, W)
    wo_flat = np.concatenate([tile_cols(w_out0, cb * 256, 256) for cb in range(8)], axis=1)
    wu_flat = np.concatenate([tile_cols(w_up0, g * 512, 512) for g in range(16)], axis=1)
    shared = dict(wflat=wflat, wo_flat=np.ascontiguousarray(wo_flat), wu_flat=np.ascontiguousarray(wu_flat), w_down=f(w_down[0]),
                  wgk=np.concatenate([f(w_gk_up[0]), f(b_gk[0])[None, :]], axis=0), nw=np.ascontiguousarray(nwm),
                  anwb=np.ascontiguousarray(np.broadcast_to(f(att_out_norm_w[0])[None, :], (16, 1024))), **consts)
    in_maps = []
    for c in range(8):
        b, ci = c // 4, c % 4
        s = ci * 1024
        xt = np.zeros((2048, 4096), np.float32)
        lo = max(0, s - 3072)
        seg = x_prompt[b, lo:s + 1024, :]
        xt[:, 4096 - seg.shape[0]:] = seg.T
        m = dict(shared)
        m["xT"] = xt
        m["xsT"] = np.ascontiguousarray(x_sample[4 * c:4 * c + 4].reshape(16, 2048).T)
        m["ck"] = np.ascontiguousarray(f(cache_k_win[0, 4 * c:4 * c + 4]).reshape(4, 2048, 1024))
        m["cv"] = np.ascontiguousarray(f(cache_v_win[0, 4 * c:4 * c + 4]).reshape(4, 2048, 1024))
        m["st"] = f(state_gla[0, 4 * c:4 * c + 4])
        m["kb"] = _kb(ci)
        in_maps.append(m)
    res = run_bass_kernel_spmd(nc, in_maps, core_ids=list(range(8)))
    R = res.results
    if DEBUG == "gla0":
        return R
    if DEBUG:
        z16 = np.zeros((16, 2048), np.float32)
        for r in R:
            r.setdefault("y_own", np.zeros((1024, 2048), np.float32))
            r.setdefault("y_s", z16)
    y_prompt = np.zeros((2, 4096, 2048), np.float32)
    y_sample = np.zeros((32, 4, 2048), np.float32)
    k_win = np.zeros((1, 2, 2048, 8, 128), np.float32)
    v_win = np.zeros((1, 2, 2048, 8, 128), np.float32)
    gla_p = np.zeros((1, 2, 4, 128, 256), np.float32)
    k_new = np.zeros((1, 32, 4, 8, 128), np.float32)
    v_new = np.zeros((1, 32, 4, 8, 128), np.float32)
    gla_s = np.zeros((1, 32, 4, 128, 256), np.float32)
    for c in range(8):
        b, ci = c // 4, c % 4
        s = ci * 1024
        r = R[c]
        y_prompt[b, s:s + 1024] = r["y_own"]
        y_sample[4 * c:4 * c + 4] = r["y_s"].reshape(4, 4, 2048)
        if ci >= 2:
            k_win[0, b, s - 2048:s - 1024] = r["k_own"].reshape(1024, 8, 128)
            v_win[0, b, s - 2048:s - 1024] = r["v_own"].reshape(1024, 8, 128)
        if ci == 3:
            gla_p[0, b] = r["gla_st"]
        k_new[0, 4 * c:4 * c + 4] = r["k_s"].reshape(4, 4, 8, 128)
        v_new[0, 4 * c:4 * c + 4] = r["v_s"].reshape(4, 4, 8, 128)
        gla_s[0, 4 * c:4 * c + 4] = r["gla_s"]
    return (y_prompt, y_sample, k_win, v_win, gla_p, k_new, v_new, gla_s)
```

```python
import numpy as np
import concourse.bass as bass
import concourse.mybir as mybir
from concourse.bass_utils import run_bass_kernel_spmd

F32, BF16 = mybir.dt.float32, mybir.dt.bfloat16
AF = mybir.ActivationFunctionType
ALU = mybir.AluOpType
AX = mybir.AxisListType

NQ = 24


class Trk:
    __slots__ = ("w", "r")

    def __init__(self):
        self.w = None
        self.r = []


class Op:
    __slots__ = ("eng", "fn", "deps", "dma", "slot", "inc", "val")

    def __init__(self, eng, fn, deps, dma):
        self.eng, self.fn, self.deps, self.dma = eng, fn, deps, dma
        self.slot, self.inc, self.val = None, False, 0


class Prog:
    ENGS = ["pe", "act", "dve", "pool", "sp"]

    def __init__(self):
        self.ops = []
        self.slot_last = {"sp": [None] * NQ, "pool": [None] * NQ}
        self.dma_cnt = {"sp": 0, "pool": 0}
        self.last = {e: None for e in self.ENGS}

    def op(self, eng, fn, reads=(), writes=(), dma=False):
        deps = set()
        for t in reads:
            if t.w is not None:
                deps.add(t.w)
        for t in writes:
            if t.w is not None:
                deps.add(t.w)
            deps.update(t.r)
        if eng == "pe":
            deps = {d for d in deps if self.ops[d].eng != "pe"}
        i = len(self.ops)
        o = Op(eng, fn, deps, dma)
        if dma:
            k = self.dma_cnt[eng] % NQ
            self.dma_cnt[eng] += 1
            if self.slot_last[eng][k] is not None:
                deps.add(self.slot_last[eng][k])
            self.slot_last[eng][k] = i
            o.slot = k
        self.ops.append(o)
        for t in reads:
            t.r.append(i)
        for t in writes:
            t.w = i
            t.r = []
        self.last[eng] = i
        return i

    def barrier(self):
        allp = set(x for x in self.last.values() if x is not None)
        for q in self.slot_last.values():
            allp.update(x for x in q if x is not None)
        for e in self.ENGS:
            self.ops.append(Op(e, None, set(allp), False))

    def emit(self, nc):
        ops = self.ops
        for o in ops:
            for d in o.deps:
                ops[d].inc = True
        cnt = {e: 0 for e in self.ENGS}
        slotcnt = {"sp": [0] * NQ, "pool": [0] * NQ}
        for o in ops:
            if o.fn is None:
                continue
            if o.dma:
                slotcnt[o.eng][o.slot] += 16
                o.val = slotcnt[o.eng][o.slot]
            elif o.inc:
                cnt[o.eng] += 1
                o.val = cnt[o.eng]
        import contextlib

        with contextlib.ExitStack() as es:
            csem = {e: es.enter_context(nc.semaphore("c_" + e)) for e in self.ENGS}
            dsem = {q: [es.enter_context(nc.semaphore(f"d_{q}{k}")) for k in range(NQ)] for q in ("sp", "pool")}
            block = es.enter_context(nc.Block())

            def run(ename, eng):
                waited = {}
                for o in ops:
                    if o.eng != ename:
                        continue
                    for d in sorted(o.deps):
                        do = ops[d]
                        if do.dma:
                            sem, key = dsem[do.eng][do.slot], ("d", do.eng, do.slot)
                        else:
                            sem, key = csem[do.eng], ("c", do.eng)
                        if waited.get(key, 0) < do.val:
                            eng.wait_ge(sem, do.val)
                            waited[key] = do.val
                    if o.fn is None:
                        continue
                    ins = o.fn(eng)
                    if o.dma:
                        ins.then_inc(dsem[ename][o.slot], 16)
                    elif o.inc:
                        ins.then_inc(csem[ename], 1)

            @block.tensor
            def _(e):
                run("pe", e)

            @block.scalar
            def _(e):
                run("act", e)

            @block.vector
            def _(e):
                run("dve", e)

            @block.gpsimd
            def _(e):
                run("pool", e)

            @block.sync
            def _(e):
                run("sp", e)


class Arena:
    def __init__(self, nc, base=0):
        self.nc, self.off, self.n = nc, base, 0

    def alloc(self, shape, dt):
        per = int(np.prod(shape[1:])) * (4 if dt == F32 else 2)
        per = (per + 31) // 32 * 32
        h = self.nc.alloc_sbuf_tensor_at(f"sb{self.n}", list(shape), dt, offset=self.off)
        self.n += 1
        self.off += per
        assert self.off <= 229376, self.off
        return h


NTOK = 1040
PARTS_OWN = [(0, 512), (512, 512), (1024, 16)]
PARTS_PRE = [(0, 512), (512, 512)]
PARTS_MLP = [(0, 347), (347, 347), (694, 346)]
C_GQ, C_GK, C_GV, C_GG, C_GLR, C_AQ, C_AK, C_AV = 0, 512, 1024, 2048, 3072, 3088, 4112, 5136
SCALE = 128 ** -0.5
EPS = 1e-6


def attn_units():
    units = []
    for di, D in enumerate((1, 4, 16)):
        lq0, lq1 = 2048 // D, 3072 // D
        for r in range(D):
            j = 0
            while 128 * j < lq1:
                if 128 * (j + 1) > lq0 - 128:
                    nk = min(128, lq1 - 128 * j)
                    q0, q1 = max(128 * j, lq0), min(128 * j + 256, lq1)
                    if q1 > q0:
                        units.append(dict(di=di, D=D, r=r, j=j, nk=nk, kc0=128 * j * D + r, q0=q0, nq=q1 - q0,
                                          qc0=q0 * D + r - 2048, m0=q0 - 128 * j))
                j += 1
    return units


UNITS = attn_units()
NU = len(UNITS)


DEBUG = None


class StopBuild(Exception):
    pass


WLIST = None
WLOG = []
WOFF = {}
WTOT = [0]


def build():
    global WLIST
    WLIST = None
    del WLOG[:]
    WOFF.clear()
    WTOT[0] = 0
    try:
        _build(bass.Bass("TRN2", target_bir_lowering=False), Prog())
    except StopBuild:
        pass
    WLIST = list(WLOG)
    o = 0
    for key in WOFF:
        WOFF[key] = o
        o += 16 * sum(n for (_, n) in key)
    WTOT[0] = o
    nc = bass.Bass("TRN2", target_bir_lowering=False)
    P = Prog()
    try:
        _build(nc, P)
    except StopBuild:
        pass
    P.barrier()
    P.emit(nc)
    return nc


def _build(nc, P):

    def din(name, shape, dt=F32):
        return nc.dram_tensor(name, list(shape), dt, kind="ExternalInput").ap()

    def dout(name, shape):
        return nc.dram_tensor(name, list(shape), F32, kind="ExternalOutput").ap()

    xT = din("xT", [2048, 4096])
    xsT = din("xsT", [2048, 16])
    wflat = din("wflat", [128, max(WTOT[0], 16)])
    wo_flat = din("wo_flat", [128, 8 * 4096])
    wu_flat = din("wu_flat", [128, 16 * 8192])
    w_down = din("w_down", [8192, 2048])
    wgk_d = din("wgk", [17, 512])
    nw_d = din("nw", [128, 58])
    anwb_d = din("anwb", [16, 1024])
    ck_d = din("ck", [4, 2048, 1024])
    cv_d = din("cv", [4, 2048, 1024])
    st_d = din("st", [4, 4, 128, 256])
    identf_d = din("identf", [128, 128])
    identb_d = din("identb", [128, 128], BF16)
    onesb_d = din("onesb", [128, 128], BF16)
    U_d = din("U", [128, 128])
    Ub_d = din("Ub", [16, 16])
    amask_d = din("amask", [128, 3 * 8 * 256], BF16)
    kb_d = din("kb", [128, NU])
    selb_d = din("selb", [128, 256], BF16)
    selq_d = din("selq", [16, 16 * 128], BF16)
    sbias_d = din("sbias", [128, 24])
    sm_d = din("smask", [16, 4])

    y_o = dout("y_own", [1024, 2048])
    ys_o = dout("y_s", [16, 2048])
    k_o = dout("k_own", [1024, 1024])
    v_o = dout("v_own", [1024, 1024])
    gst_o = dout("gla_st", [4, 128, 256])
    ks_o = dout("k_s", [16, 1024])
    vs_o = dout("v_s", [16, 1024])
    gs_o = dout("gla_s", [4, 4, 128, 256])
    if DEBUG:
        dbgf = nc.dram_tensor("dbgf", [128, 8, NTOK], F32, kind="ExternalOutput").ap()
        dbgb = nc.dram_tensor("dbgb", [128, 8, NTOK], BF16, kind="ExternalOutput").ap()

    def dumpf(slot, ap, trks, p=128, n=NTOK):
        if DEBUG:
            P.op("sp", lambda e: e.dma_start(out=dbgf[0:p, slot, 0:n], in_=ap), reads=trks, dma=True)

    def dumpb(slot, ap, trks, p=128, n=NTOK):
        if DEBUG:
            P.op("sp", lambda e: e.dma_start(out=dbgb[0:p, slot, 0:n], in_=ap), reads=trks, dma=True)
    kvscr = nc.dram_tensor("kvscr", [2, 8, 128, 2048], BF16).ap()
    kvscr_t = [[Trk() for _ in range(8)] for _ in range(2)]
    ksvs_t = [Trk(), Trk()]

    A0 = Arena(nc, 16640)
    identf, identb, onesb, Uf = (A0.alloc([128, 128], F32), A0.alloc([128, 128], BF16),
                                 A0.alloc([128, 128], BF16), A0.alloc([128, 128], F32))
    nw = A0.alloc([128, 58], F32)
    Sf = A0.alloc([128, 4, 256], F32)
    Sb = A0.alloc([128, 4, 256], BF16)
    ct = Trk()
    Sf_t = [Trk() for _ in range(4)]
    Sb_t = [Trk() for _ in range(4)]

    def dma(q, out, in_, reads=(), writes=()):
        return P.op(q, lambda e: e.dma_start(out=out, in_=in_), reads=reads, writes=writes, dma=True)

    epsb = A0.alloc([128, 2], F32)
    xs_all = A0.alloc([128, 16, 16], F32)
    xs_t = Trk()
    for dst, src in ((identf, identf_d), (identb, identb_d), (onesb, onesb_d), (Uf, U_d), (nw, nw_d)):
        dma("sp", dst[:], src, writes=[ct])
    dma("sp", xs_all[:], xsT.rearrange("(kc p) n -> p kc n", p=128), writes=[xs_t])
    P.op("pool", lambda e: e.memset(Sf[:], 0.0), writes=Sf_t)
    P.op("pool", lambda e: e.memset(Sb[:], 0.0), writes=Sb_t)
    NW_ATTN, NW_FFN, NW_FIN, NW_GLA, NW_ATT = 0, 16, 32, 48, 50

    psf = [nc.alloc_psum_tensor(f"psf{i}", [128, 512], F32) for i in range(6)]
    psf_t = [Trk() for _ in range(6)]
    psb = [nc.alloc_psum_tensor(f"psb{i}", [128, 1024], BF16) for i in range(2)]
    psb_t = [Trk() for _ in range(2)]
    rr = {"f": 0, "b": 0}

    def ps_f():
        i = rr["f"] % 6
        rr["f"] += 1
        return psf[i], psf_t[i]

    def ps_b():
        i = rr["b"] % 2
        rr["b"] += 1
        return psb[i], psb_t[i]

    def mm(out, lhsT, rhs, start, stop, reads, writes):
        P.op("pe", lambda e: e.matmul(out, lhsT=lhsT, rhs=rhs, start=start, stop=stop), reads=reads, writes=writes)

    def tr(out, in_, ident, reads, writes):
        P.op("pe", lambda e: e.transpose(out, in_, ident), reads=reads, writes=writes)

    def act(out, in_, func, reads, writes, **kw):
        P.op("act", lambda e: e.activation(out=out, in_=in_, func=func, **kw), reads=reads, writes=writes)

    class Rot:
        def __init__(self, arena, shape, dt, n):
            self.b = [arena.alloc(shape, dt) for _ in range(n)]
            self.t = [Trk() for _ in range(n)]
            self.i = 0

        def get(self):
            k = self.i % len(self.b)
            self.i += 1
            return self.b[k], self.t[k]

    A1 = Arena(nc, A0.off)
    hT = A1.alloc([128, 16, NTOK], BF16)
    hT_t = [Trk() for _ in range(16)]
    R_START = A1.off
    xc = Rot(A1, [128, NTOK], F32, 4)
    sqb = Rot(A1, [128, NTOK], BF16, 2)
    rB = A1.alloc([128, NTOK], F32)
    rB_t = Trk()
    wblk = Rot(A1, [128, 4096], BF16, 3)
    AT = A1.alloc([128, 16, NTOK], BF16)
    AT_t = [Trk() for _ in range(16)]
    P1C_START = A1.off
    glrT = A1.alloc([32, NTOK], F32)
    glrT_t = Trk()
    wgk = A1.alloc([32, 512], F32)
    Ub = A1.alloc([16, 16], F32)
    smask = A1.alloc([16, 4], F32)
    for dst, src in ((wgk[0:17, :], wgk_d), (Ub[:], Ub_d), (smask[:], sm_d)):
        dma("sp", dst, src, writes=[ct])
    A1_mark = A1.off

    def ssq_norm(src_chunks, parts, nw_col, dst, dst_t, div):
        N = parts[-1][0] + parts[-1][1]
        pss = [ps_f() for _ in parts]
        for kc in range(16):
            s_ap, s_t = src_chunks(kc)
            q, qt = sqb.get()
            act(q[:, 0:N], s_ap, AF.Square, reads=s_t, writes=[qt])
            P.op("dve", lambda e, kc=kc, s_ap=s_ap: e.tensor_scalar_mul(
                out=dst[:, kc, 0:N], in0=s_ap, scalar1=nw[:, nw_col + kc:nw_col + kc + 1]),
                reads=list(s_t) + [ct], writes=[dst_t[kc]])
            for (c0, n), (ps, pt) in zip(parts, pss):
                mm(ps[:, 0:n], onesb[:], q[:, c0:c0 + n], kc == 0, kc == 15, reads=[qt, ct], writes=[pt])
        for (c0, n), (ps, pt) in zip(parts, pss):
            act(rB[:, c0:c0 + n], ps[:, 0:n], AF.Ln, reads=[pt, ct], writes=[rB_t], scale=1.0 / div, bias=epsb[:, 0:1])
        act(rB[:, 0:N], rB[:, 0:N], AF.Exp, reads=[rB_t], writes=[rB_t], scale=-0.5)
        for kc in range(16):
            eng = "dve" if kc % 2 == 0 else "pool"
            P.op(eng, lambda e, kc=kc: e.tensor_mul(out=dst[:, kc, 0:N], in0=dst[:, kc, 0:N], in1=rB[:, 0:N]),
                 reads=[dst_t[kc], rB_t], writes=[dst_t[kc]])

    P.op("pool", lambda e: e.memset(epsb[:, 0:1], EPS), writes=[ct])
    P.op("pool", lambda e: e.memset(epsb[:, 1:2], float(np.log(SCALE))), writes=[ct])
    P.barrier()

    wstate = {"i": 0, "issued": []}

    def _wissue(col_specs):
        b, t = wblk.get()
        W = sum(n for (_, _, n) in col_specs)
        off = 0
        for (dc, sc, n) in col_specs:
            assert dc == off
            off += n
        key = tuple((sc, n) for (_, sc, n) in col_specs)
        view = b[:, 0:16 * W].rearrange("p (kc n) -> p kc n", kc=16)
        if WLIST is None:
            WOFF.setdefault(key, None)
            src = wflat[:, 0:16]
            dma("pool", b[:, 0:16], src, writes=[t])
        else:
            o = WOFF[key]
            dma("pool", b[:, 0:16 * W], wflat[:, o:o + 16 * W], writes=[t])
        return view, t

    def wload(col_specs, src=None, rows=None):
        i = wstate["i"]
        wstate["i"] += 1
        if WLIST is None:
            WLOG.append(col_specs)
            return _wissue(col_specs)
        assert WLIST[i] == col_specs
        while len(wstate["issued"]) <= min(i + 1, len(WLIST) - 1):
            wstate["issued"].append(_wissue(WLIST[len(wstate["issued"])]))
        return wstate["issued"][i]

    def proj_fm(wb, wt, wc, m, parts, evac):
        for (c0, n) in parts:
            ps, pt = ps_f()
            for kc in range(16):
                mm(ps[0:m, 0:n], wb[:, kc, wc:wc + m], hT[:, kc, c0:c0 + n], kc == 0, kc == 15,
                   reads=[wt, hT_t[kc]], writes=[pt])
            evac(ps, pt, c0, n)

    def proj_tm(wb, wt, ncols, t0, m, evac):
        ps, pt = ps_f()
        for kc in range(16):
            mm(ps[0:m, 0:ncols], hT[:, kc, t0:t0 + m], wb[:, kc, 0:ncols], kc == 0, kc == 15,
               reads=[wt, hT_t[kc]], writes=[pt])
        evac(ps, pt)

    A2 = Arena(nc, A1_mark)
    GB = []
    for _ in range(2):
        GB.append(dict(gkT=A2.alloc([128, NTOK], F32), gqT=A2.alloc([128, NTOK], F32), ggs=A2.alloc([128, 2, NTOK], BF16),
                       gv=A2.alloc([128, 9, 256], BF16), gkT_t=Trk(), gqT_t=Trk(), ggs_t=Trk(), gv_t=[Trk() for _ in range(9)]))
    o_h = A2.alloc([128, 2, NTOK], F32)
    o_t = Trk()
    tmpf = Rot(A2, [128, 128], F32, 6)
    f512 = Rot(A2, [128, 512], F32, 5)
    xq = Rot(A2, [128, 512], BF16, 2)
    xk = Rot(A2, [128, 512], BF16, 2)
    xa = Rot(A2, [128, 4, 128], BF16, 2)
    xh = Rot(A2, [128, 4, 128], BF16, 2)
    xhT = Rot(A2, [128, 512], BF16, 2)
    tmpb = Rot(A2, [128, 128], BF16, 8)
    khat4 = A2.alloc([128, 4, 16], BF16)
    khat4_t = Trk()
    blast = Rot(A2, [128, 4], F32, 6)
    sgate, sgate_t = rB, rB_t
    S0 = Rot(A2, [128, 256], F32, 4)
    S0b = Rot(A2, [128, 256], BF16, 2)
    Sout = Rot(A2, [128, 256], F32, 2)

    def copy_evac(dst_ap, dst_t, eng="act"):
        def f(ps, pt, c0=None, n=None):
            pass
        return f

    def gla_tile(h, c0, nt, own, sample, G):
        gkT, gkT_t, gqT, gqT_t, gv, gv_t = G["gkT"], G["gkT_t"], G["gqT"], G["gqT_t"], G["gv"], G["gv_t"]
        Umask = Ub[:] if sample else Uf[:]
        ti = c0 // 128
        ps, pt = ps_f()
        mm(ps[0:nt, 0:128], glrT[0:17, c0:c0 + nt], wgk[0:17, h * 128:(h + 1) * 128], True, True,
           reads=[glrT_t, ct], writes=[pt])
        e1, e1t = tmpf.get()
        act(e1[0:nt, :], ps[0:nt, 0:128], AF.Exp, reads=[pt], writes=[e1t], scale=-1.0)
        lan, lant = tmpf.get()
        act(lan[0:nt, :], e1[0:nt, :], AF.Ln, reads=[e1t], writes=[lant], bias=1.0)
        psB, ptB = ps_f()
        mm(psB[:, 0:nt], lan[0:nt, :], Umask[0:nt, 0:nt], True, True, reads=[lant, ct], writes=[ptB])
        enB, enBt = tmpf.get()
        act(enB[:, 0:nt], psB[:, 0:nt], AF.Exp, reads=[ptB], writes=[enBt], scale=1.0 / 16)
        kt, ktt = tmpf.get()
        P.op("dve", lambda e: e.tensor_mul(out=kt[:, 0:nt], in0=gkT[:, c0:c0 + nt], in1=enB[:, 0:nt]),
             reads=[gkT_t, enBt], writes=[ktt])
        bl, blt = blast.get()
        if sample:
            for j in range(4):
                act(bl[:, j:j + 1], psB[:, 4 * j + 3:4 * j + 4], AF.Exp, reads=[ptB], writes=[blt], scale=-1.0 / 16)
        else:
            act(bl[:, 0:1], psB[:, nt - 1:nt], AF.Exp, reads=[ptB], writes=[blt], scale=-1.0 / 16)
        if own:
            eB, eBt = tmpf.get()
            act(eB[:, 0:nt], psB[:, 0:nt], AF.Exp, reads=[ptB, ct], writes=[eBt], scale=-1.0 / 16, bias=epsb[:, 1:2])
            qt_, qtt = tmpb.get()
            P.op("dve", lambda e: e.tensor_mul(out=qt_[:, 0:nt], in0=gqT[:, c0:c0 + nt], in1=eB[:, 0:nt]),
                 reads=[gqT_t, eBt], writes=[qtt])
            ktb, ktbt = tmpb.get()
            P.op("pool", lambda e: e.tensor_copy(out=ktb[:, 0:nt], in_=kt[:, 0:nt]), reads=[ktt], writes=[ktbt])
            psA, ptA = ps_f()
            mm(psA[0:nt, 0:nt], ktb[:, 0:nt], qt_[:, 0:nt], True, True, reads=[ktbt, qtt], writes=[ptA])
            Am, Amt = tmpb.get()
            P.op("dve", lambda e: e.tensor_mul(out=Am[0:nt, 0:nt], in0=psA[0:nt, 0:nt], in1=Umask[0:nt, 0:nt]),
                 reads=[ptA, ct], writes=[Amt])
            if sample:
                psO, ptO = ps_f()
                for vh in range(2):
                    mm(psO[:, vh * 128:vh * 128 + nt], gv[0:nt, ti, vh * 128:(vh + 1) * 128], Am[0:nt, 0:nt], True, True,
                       reads=[gv_t[ti], Amt], writes=[ptO])
        if not sample:
            kh, kht = tmpb.get()
            P.op("pool", lambda e: e.tensor_scalar_mul(out=kh[:, 0:nt], in0=kt[:, 0:nt], scalar1=bl[:, 0:1]),
                 reads=[ktt, blt], writes=[kht])
            pT, pTt = ps_b()
            tr(pT[0:nt, 0:128], kh[:, 0:nt], identb[:], reads=[kht, ct], writes=[pTt])
            khT, khTt = tmpb.get()
            act(khT[0:nt, :], pT[0:nt, 0:128], AF.Copy, reads=[pTt], writes=[khTt])
            def stageB():
                if own:
                    psO, ptO = ps_f()
                    for vh in range(2):
                        mm(psO[:, vh * 128:vh * 128 + nt], gv[0:nt, ti, vh * 128:(vh + 1) * 128], Am[0:nt, 0:nt], True, False,
                           reads=[gv_t[ti], Amt], writes=[ptO])
                        mm(psO[:, vh * 128:vh * 128 + nt], Sb[:, h, vh * 128:(vh + 1) * 128], qt_[:, 0:nt], False, True,
                           reads=[Sb_t[h], qtt], writes=[ptO])
                psS, ptS = ps_f()
                mm(psS[:, 0:256], khT[0:nt, :], gv[0:nt, ti, :], True, True, reads=[khTt, gv_t[ti]], writes=[ptS])
                if own:
                    P.op("act", lambda e: e.activation(out=o_h[:, :, c0:c0 + nt],
                                                       in_=psO[:, 0:256].rearrange("p (a b) -> p a b", a=2)[:, :, 0:nt],
                                                       func=AF.Copy), reads=[ptO], writes=[o_t])
                P.op("dve", lambda e: e.scalar_tensor_tensor(out=Sf[:, h, :], in0=Sf[:, h, :], scalar=bl[:, 0:1],
                                                             in1=psS[:, 0:256], op0=ALU.mult, op1=ALU.add),
                     reads=[Sf_t[h], blt, ptS], writes=[Sf_t[h]])
                P.op("pool", lambda e: e.tensor_copy(out=Sb[:, h, :], in_=Sf[:, h, :]), reads=[Sf_t[h]], writes=[Sb_t[h]])
            return stageB
        else:
            P.op("pool", lambda e: e.memset(khat4[:], 0.0), writes=[khat4_t])
            seqst = []
            for j in range(4):
                s0, s0t = S0.get()
                dma("sp", s0[:], st_d[j, h], writes=[s0t])
                s0b, s0bt = S0b.get()
                P.op("pool", lambda e, s0=s0, s0b=s0b: e.tensor_copy(out=s0b[:], in_=s0[:]), reads=[s0t], writes=[s0bt])
                for vh in range(2):
                    mm(psO[:, 256 + vh * 128 + 4 * j:256 + vh * 128 + 4 * j + 4], s0b[:, vh * 128:(vh + 1) * 128],
                       qt_[:, 4 * j:4 * j + 4], True, True, reads=[s0bt, qtt], writes=[ptO])
                P.op("pool", lambda e, j=j: e.tensor_scalar_mul(out=khat4[:, j, 4 * j:4 * j + 4], in0=kt[:, 4 * j:4 * j + 4],
                                                                scalar1=bl[:, j:j + 1]),
                     reads=[ktt, blt, khat4_t], writes=[khat4_t])
                seqst.append((s0, s0t))
            P.op("act", lambda e: e.activation(out=o_h[:, :, c0:c0 + nt],
                                               in_=psO[:, 0:256].rearrange("p (a b) -> p a b", a=2)[:, :, 0:nt],
                                               func=AF.Copy), reads=[ptO], writes=[o_t])
            P.op("dve", lambda e: e.tensor_add(out=o_h[:, :, c0:c0 + nt], in0=o_h[:, :, c0:c0 + nt],
                                               in1=psO[:, 256:512].rearrange("p (a b) -> p a b", a=2)[:, :, 0:nt]),
                 reads=[ptO, o_t], writes=[o_t])
            for j in range(4):
                s0, s0t = seqst[j]
                pT, pTt = ps_b()
                tr(pT[0:16, 0:128], khat4[:, j, :], identb[:], reads=[khat4_t, ct], writes=[pTt])
                khT, khTt = tmpb.get()
                act(khT[0:16, :], pT[0:16, 0:128], AF.Copy, reads=[pTt], writes=[khTt])
                psS, ptS = ps_f()
                mm(psS[:, 0:256], khT[0:16, :], gv[0:16, ti, :], True, True, reads=[khTt, gv_t[ti]], writes=[ptS])
                so, sot = Sout.get()
                P.op("dve", lambda e, so=so, s0=s0, j=j, psS=psS: e.scalar_tensor_tensor(
                    out=so[:], in0=s0[:], scalar=bl[:, j:j + 1], in1=psS[:, 0:256], op0=ALU.mult, op1=ALU.add),
                    reads=[s0t, blt, ptS], writes=[sot])
                dma("sp", gs_o[j, h], so[:], reads=[sot])

    def gla_batch(h, b, own, G):
        gkT, gkT_t, gqT, gqT_t, gv, gv_t = G["gkT"], G["gkT_t"], G["gqT"], G["gqT_t"], G["gv"], G["gv_t"]
        c0 = b * 512
        X = {}

        def A1():
            psZ, ptZ = ps_f()
            for i in range(4):
                mm(psZ[:, i * 128:(i + 1) * 128], glrT[0:17, c0 + i * 128:c0 + (i + 1) * 128], wgk[0:17, h * 128:(h + 1) * 128],
                   True, True, reads=[glrT_t, ct], writes=[ptZ])
            e1, e1t = f512.get()
            act(e1[:], psZ[:], AF.Exp, reads=[ptZ], writes=[e1t], scale=-1.0)
            lan, lant = f512.get()
            act(lan[:], e1[:], AF.Ln, reads=[e1t], writes=[lant], bias=1.0)
            X["lan"], X["lant"] = lan, lant

        def A2():
            lan, lant = X["lan"], X["lant"]
            psB, ptB = ps_f()
            for i in range(4):
                mm(psB[:, i * 128:(i + 1) * 128], lan[:, i * 128:(i + 1) * 128], Uf[:], True, True, reads=[lant, ct], writes=[ptB])
            enB, enBt = f512.get()
            act(enB[:], psB[:], AF.Exp, reads=[ptB], writes=[enBt], scale=1.0 / 16)
            bl, blt = blast.get()
            act(bl[:, 0:4], psB[:, 127:512:128], AF.Exp, reads=[ptB], writes=[blt], scale=-1.0 / 16)
            kt, ktt = f512.get()
            P.op("dve", lambda e: e.tensor_mul(out=kt[:], in0=gkT[:, c0:c0 + 512], in1=enB[:]), reads=[gkT_t, enBt], writes=[ktt])
            kh, kht = xh.get()
            P.op("pool", lambda e: e.tensor_mul(out=kh[:], in0=kt[:].rearrange("p (a b) -> p a b", a=4),
                                                in1=bl[:, 0:4].unsqueeze(2).broadcast_to([128, 4, 128])),
                 reads=[ktt, blt], writes=[kht])
            X.update(bl=bl, blt=blt, kh=kh, kht=kht)
            if own:
                eB, eBt = f512.get()
                act(eB[:], psB[:], AF.Exp, reads=[ptB, ct], writes=[eBt], scale=-1.0 / 16, bias=epsb[:, 1:2])
                qt_, qtt = xq.get()
                P.op("dve", lambda e: e.tensor_mul(out=qt_[:], in0=gqT[:, c0:c0 + 512], in1=eB[:]), reads=[gqT_t, eBt], writes=[qtt])
                ktb, ktbt = xk.get()
                P.op("pool", lambda e: e.tensor_copy(out=ktb[:], in_=kt[:]), reads=[ktt], writes=[ktbt])
                X.update(qt_=qt_, qtt=qtt, ktb=ktb, ktbt=ktbt)

        def A3():
            kh, kht = X["kh"], X["kht"]
            pT, pTt = ps_b()
            for i in range(4):
                tr(pT[:, i * 128:(i + 1) * 128], kh[:, i, :], identb[:], reads=[kht, ct], writes=[pTt])
            khT, khTt = xhT.get()
            act(khT[:], pT[:, 0:512], AF.Copy, reads=[pTt], writes=[khTt])
            X.update(khT=khT, khTt=khTt)
            if own:
                qt_, qtt, ktb, ktbt = X["qt_"], X["qtt"], X["ktb"], X["ktbt"]
                psA, ptA = ps_f()
                for i in range(4):
                    mm(psA[:, i * 128:(i + 1) * 128], ktb[:, i * 128:(i + 1) * 128], qt_[:, i * 128:(i + 1) * 128], True, True,
                       reads=[ktbt, qtt], writes=[ptA])
                Am, Amt = xa.get()
                P.op("dve", lambda e: e.tensor_mul(out=Am[:], in0=psA[:].rearrange("p (a b) -> p a b", a=4),
                                                   in1=Uf[:].unsqueeze(1).broadcast_to([128, 4, 128])), reads=[ptA, ct], writes=[Amt])
                X.update(Am=Am, Amt=Amt)

        def mk(i):
            ti = b * 4 + i
            t0 = ti * 128

            def stageB():
                bl, blt, khT, khTt = X["bl"], X["blt"], X["khT"], X["khTt"]
                if own:
                    Am, Amt, qt_, qtt = X["Am"], X["Amt"], X["qt_"], X["qtt"]
                    psO, ptO = ps_f()
                    for vh in range(2):
                        mm(psO[:, vh * 128:(vh + 1) * 128], gv[:, ti, vh * 128:(vh + 1) * 128], Am[:, i, :], True, False,
                           reads=[gv_t[ti], Amt], writes=[ptO])
                        mm(psO[:, vh * 128:(vh + 1) * 128], Sb[:, h, vh * 128:(vh + 1) * 128], qt_[:, i * 128:(i + 1) * 128],
                           False, True, reads=[Sb_t[h], qtt], writes=[ptO])
                psS, ptS = ps_f()
                mm(psS[:, 0:256], khT[:, i * 128:(i + 1) * 128], gv[:, ti, :], True, True, reads=[khTt, gv_t[ti]], writes=[ptS])
                if own:
                    P.op("act", lambda e: e.activation(out=o_h[:, :, t0:t0 + 128],
                                                       in_=psO[:, 0:256].rearrange("p (a b) -> p a b", a=2),
                                                       func=AF.Copy), reads=[ptO], writes=[o_t])
                P.op("dve", lambda e: e.scalar_tensor_tensor(out=Sf[:, h, :], in0=Sf[:, h, :], scalar=bl[:, i:i + 1],
                                                             in1=psS[:, 0:256], op0=ALU.mult, op1=ALU.add),
                     reads=[Sf_t[h], blt, ptS], writes=[Sf_t[h]])
                P.op("pool", lambda e: e.tensor_copy(out=Sb[:, h, :], in_=Sf[:, h, :]), reads=[Sf_t[h]], writes=[Sb_t[h]])
            return stageB
        return [A1, A2, A3], [mk(i) for i in range(4)]

    def gla_group(g, own):
        parts = PARTS_OWN if own else PARTS_PRE
        N = NTOK if own else 1024
        P.op("pool", lambda e: e.memset(glrT[:, 0:N], 1.0), writes=[glrT_t])
        wb, wt = wload([(0, C_GLR, 16)])

        def ev_glr(ps, pt, c0, n):
            act(glrT[0:16, c0:c0 + n], ps[0:16, 0:n], AF.Copy, reads=[pt], writes=[glrT_t])
        proj_fm(wb, wt, 0, 16, parts, ev_glr)
        ntile = 9 if own else 8

        def proj_items(h, G):
            gkT, gkT_t, gqT, gqT_t, gv, gv_t, ggs, ggs_t = (G["gkT"], G["gkT_t"], G["gqT"], G["gqT_t"], G["gv"], G["gv_t"],
                                                            G["ggs"], G["ggs_t"])
            cols = [(0, C_GK + h * 128, 128)] + ([(128, C_GQ + h * 128, 128)] if own else [])
            wb, wt = wload(cols)

            def ev_k(ps, pt, c0, n):
                act(gkT[:, c0:c0 + n], ps[:, 0:n], AF.Copy, reads=[pt], writes=[gkT_t])

            def ev_q(ps, pt, c0, n):
                P.op("dve", lambda e: e.tensor_copy(out=gqT[:, c0:c0 + n], in_=ps[:, 0:n]), reads=[pt], writes=[gqT_t])
            for p_ in parts:
                proj_fm(wb, wt, 0, 128, [p_], ev_k)
                yield
            if own:
                for p_ in parts:
                    proj_fm(wb, wt, 128, 128, [p_], ev_q)
                    yield
            wb, wt = wload([(0, C_GV + h * 256, 256)])
            for ti in range(ntile):
                m = 128 if ti < 8 else 16

                def ev_v(ps, pt, ti=ti, m=m):
                    P.op("dve", lambda e: e.tensor_copy(out=gv[0:m, ti, :], in_=ps[0:m, 0:256]), reads=[pt], writes=[gv_t[ti]])
                proj_tm(wb, wt, 256, ti * 128, m, ev_v)
                yield
            if own:
                wb, wt = wload([(0, C_GG + h * 256, 256)])
                for vh in range(2):
                    def ev_g(ps, pt, c0, n, vh=vh):
                        act(ggs[:, vh, c0:c0 + n], ps[:, 0:n], AF.Silu, reads=[pt], writes=[ggs_t])
                    for p_ in parts:
                        proj_fm(wb, wt, vh * 128, 128, [p_], ev_g)
                        yield

        def head_norm(h, G):
            ggs, ggs_t = G["ggs"], G["ggs_t"]
            pss = [ps_f() for _ in parts]
            for vh in range(2):
                q, qt = sqb.get()
                act(q[:, 0:N], o_h[:, vh, 0:N], AF.Square, reads=[o_t], writes=[qt])
                for (c0, n), (ps, pt) in zip(parts, pss):
                    mm(ps[:, 0:n], onesb[:], q[:, c0:c0 + n], vh == 0, vh == 1, reads=[qt, ct], writes=[pt])
            for (c0, n), (ps, pt) in zip(parts, pss):
                act(sgate[:, c0:c0 + n], ps[:, 0:n], AF.Ln, reads=[pt, ct], writes=[sgate_t], scale=1.0 / 256,
                    bias=epsb[:, 0:1])
            act(sgate[:, 0:N], sgate[:, 0:N], AF.Exp, reads=[sgate_t], writes=[sgate_t], scale=-0.5)
            for vh in range(2):
                P.op("dve", lambda e, vh=vh: e.scalar_tensor_tensor(
                    out=o_h[:, vh, 0:N], in0=o_h[:, vh, 0:N], scalar=nw[:, NW_GLA + vh:NW_GLA + vh + 1],
                    in1=sgate[:, 0:N], op0=ALU.mult, op1=ALU.mult), reads=[o_t, sgate_t, ct], writes=[o_t])
                P.op("pool", lambda e, vh=vh: e.tensor_mul(out=AT[:, 2 * h + vh, 0:N], in0=o_h[:, vh, 0:N],
                                                          in1=ggs[:, vh, 0:N]),
                     reads=[o_t, ggs_t], writes=[AT_t[2 * h + vh]])

        for _ in proj_items(0, GB[0]):
            pass
        for h in range(4):
            G = GB[h % 2]
            nxt = proj_items(h + 1, GB[(h + 1) % 2]) if h < 3 else None
            (a0, b0), (a1, b1) = gla_batch(h, 0, own, G), gla_batch(h, 1, own, G)
            steps = [a0[0], a1[0], a0[1], a1[1], a0[2], a1[2]] + b0 + b1
            if own:
                steps.append(lambda h=h, G=G: gla_tile(h, 1024, 16, own, True, G))
                steps.append(lambda h=h, G=G: head_norm(h, G))
            nit = (24 if own else 10)
            per = -(-nit // len(steps))
            for st in steps:
                st()
                if nxt is not None:
                    for _ in range(per):
                        if next(nxt, "done") == "done":
                            nxt = None
                            break
            if nxt is not None:
                for _ in nxt:
                    pass

    A3 = Arena(nc, A1_mark)
    ks_s = A3.alloc([16, 1024], F32)
    vs_s = A3.alloc([16, 1024], F32)
    qs_s = A3.alloc([16, 1024], BF16)
    ks_t, vs_t, qs_t = Trk(), Trk(), Trk()
    amask = A3.alloc([128, 3 * 8 * 256], BF16)
    kb = A3.alloc([128, NU], F32)
    selb = A3.alloc([128, 256], BF16)
    selq = A3.alloc([16, 16 * 128], BF16)
    sbias = A3.alloc([128, 24], F32)
    anwb = A3.alloc([16, 1024], F32)

    def load_attn_consts():
        for dst, src in ((amask[:], amask_d), (kb[:], kb_d), (selb[:], selb_d), (selq[:], selq_d),
                         (sbias[:], sbias_d), (anwb[:], anwb_d)):
            dma("sp", dst, src, writes=[ct])
    A3_mark = A3.off
    aqT = A3.alloc([128, NTOK], BF16)
    akT = A3.alloc([128, 3072 + 16], BF16)
    avT = A3.alloc([128, 3072 + 16], BF16)
    aqT_t, akT_t, avT_t = Trk(), Trk(), Trk()
    acc = A3.alloc([128, 2, 1024], F32)
    acc_t = Trk()
    ssatt = A3.alloc([128, NTOK], F32)
    ssatt_t = Trk()
    et = Rot(A3, [128, 256], F32, 4)
    ptl = Rot(A3, [128, 256], BF16, 6)
    vbl = Rot(A3, [128, 128], BF16, 8)
    kvst = Rot(A3, [128, 1024], BF16, 3)
    kout = Rot(A3, [128, 256], F32, 3)
    attf = A3.alloc([128, 1024], F32)
    attf_t = Trk()
    A3 = Arena(nc, A3_mark)
    Kt = Rot(A3, [128, 1024], BF16, 5)
    Vt = Rot(A3, [128, 1024], BF16, 6)
    prod = Rot(A3, [128, 1024], F32, 2)
    pvb = Rot(A3, [128, 1024], BF16, 3)
    sc8 = Rot(A3, [128, 8], F32, 10)
    p8b = Rot(A3, [128, 8], BF16, 4)
    sm16 = Rot(A3, [16, 1024], F32, 2)
    sm8 = Rot(A3, [16, 8], F32, 4)
    atts_b = A3.alloc([16, 1024], BF16)
    atts_bt = Trk()

    def kv_prefix_group(g):
        for h in range(8):
            wb, wt = wload([(0, C_AK + h * 128, 128), (128, C_AV + h * 128, 128)])
            for which in range(2):
                st_, stt = kvst.get()

                def ev(ps, pt, c0, n, st_=st_, stt=stt):
                    act(st_[:, c0:c0 + n], ps[:, 0:n], AF.Copy, reads=[pt], writes=[stt])
                proj_fm(wb, wt, which * 128, 128, PARTS_PRE, ev)
                dma("sp", kvscr[which, h, :, (g - 1) * 1024:g * 1024], st_[:], reads=[stt], writes=[kvscr_t[which][h]])

    def kv_outputs():
        for which, (cb, outd, outs, sdst, sdt) in enumerate(((C_AK, k_o, ks_o, ks_s, ks_t), (C_AV, v_o, vs_o, vs_s, vs_t))):
            for cbk in range(4):
                wb, wt = wload([(0, cb + cbk * 256, 256)])
                for ti in range(9):
                    m = 128 if ti < 8 else 16
                    ko, kot = kout.get()

                    def ev(ps, pt, ko=ko, kot=kot, m=m):
                        act(ko[0:m, :], ps[0:m, 0:256], AF.Copy, reads=[pt], writes=[kot])
                    proj_tm(wb, wt, 256, ti * 128, m, ev)
                    if ti < 8:
                        dma("sp", outd[ti * 128:(ti + 1) * 128, cbk * 256:(cbk + 1) * 256], ko[:], reads=[kot])
                    else:
                        dma("sp", outs[:, cbk * 256:(cbk + 1) * 256], ko[0:16, :], reads=[kot], writes=[ksvs_t[which]])
                        P.op("pool", lambda e, ko=ko, sdst=sdst, cbk=cbk: e.tensor_copy(
                            out=sdst[:, cbk * 256:(cbk + 1) * 256], in_=ko[0:16, :]), reads=[kot], writes=[sdt])

    def attn_head(h):
        wb, wt = wload([(0, C_AQ + h * 128, 128), (128, C_AK + h * 128, 128)])
        wb2, wt2 = wload([(0, C_AV + h * 128, 128)])
        dma("sp", akT[:, 0:2048], kvscr[0, h], reads=[kvscr_t[0][h]], writes=[akT_t])
        dma("sp", avT[:, 0:2048], kvscr[1, h], reads=[kvscr_t[1][h]], writes=[avT_t])

        def ev_q(ps, pt, c0, n):
            act(aqT[:, c0:c0 + n], ps[:, 0:n], AF.Copy, reads=[pt], writes=[aqT_t])

        def ev_k(ps, pt, c0, n):
            P.op("dve", lambda e: e.tensor_copy(out=akT[:, 2048 + c0:2048 + c0 + n], in_=ps[:, 0:n]), reads=[pt], writes=[akT_t])

        def ev_v(ps, pt, c0, n):
            act(avT[:, 2048 + c0:2048 + c0 + n], ps[:, 0:n], AF.Copy, reads=[pt], writes=[avT_t])
        proj_fm(wb, wt, 0, 128, PARTS_OWN, ev_q)
        proj_fm(wb, wt, 128, 128, PARTS_OWN, ev_k)
        proj_fm(wb2, wt2, 0, 128, PARTS_OWN, ev_v)
        P.op("pool", lambda e: e.memset(acc[:], 0.0), writes=[acc_t])
        def S1(ui):
            u = UNITS[ui]
            D, nk, nq = u["D"], u["nk"], u["nq"]
            ksl = slice(u["kc0"], u["kc0"] + (nk - 1) * D + 1, D)
            qsl = slice(u["qc0"], u["qc0"] + (nq - 1) * D + 1, D)
            pT, pTt = ps_b()
            tr(pT[0:nk, 0:128], avT[:, ksl], identb[:], reads=[avT_t, ct], writes=[pTt])
            vb, vbt = vbl.get()
            act(vb[0:nk, :], pT[0:nk, 0:128], AF.Copy, reads=[pTt], writes=[vbt])
            ps, pt = ps_f()
            mm(ps[0:nk, 0:nq], akT[:, ksl], aqT[:, qsl], True, True, reads=[akT_t, aqT_t], writes=[pt])
            return dict(ui=ui, u=u, nk=nk, nq=nq, qsl=qsl, vb=vb, vbt=vbt, ps=ps, pt=pt)

        def S2(c):
            nk, nq, ui, u = c["nk"], c["nq"], c["ui"], c["u"]
            e_, e_t = et.get()
            act(e_[0:nk, 0:nq], c["ps"][0:nk, 0:nq], AF.Exp, reads=[c["pt"], ct], writes=[e_t], scale=SCALE,
                bias=kb[0:nk, ui:ui + 1])
            p_, p_t = ptl.get()
            mbase = (u["di"] * 8 + h) * 256 + u["m0"]
            P.op("pool", lambda e, p_=p_, e_=e_, nk=nk, nq=nq, mbase=mbase: e.tensor_mul(
                out=p_[0:nk, 0:nq], in0=e_[0:nk, 0:nq], in1=amask[0:nk, mbase:mbase + nq]), reads=[e_t, ct], writes=[p_t])
            c["p_"], c["p_t"] = p_, p_t

        def S3(c):
            nk, nq, qsl, p_, p_t = c["nk"], c["nq"], c["qsl"], c["p_"], c["p_t"]
            ps2, pt2 = ps_f()
            mm(ps2[:, 0:nq], c["vb"][0:nk, :], p_[0:nk, 0:nq], True, True, reads=[c["vbt"], p_t], writes=[pt2])
            mm(ps2[:, 256:256 + nq], onesb[0:nk, :], p_[0:nk, 0:nq], True, True, reads=[p_t, ct], writes=[pt2])
            P.op("dve", lambda e, ps2=ps2, qsl=qsl, nq=nq: e.tensor_add(
                out=acc[:, :, qsl], in0=acc[:, :, qsl], in1=ps2[:, 0:512].rearrange("p (a b) -> p a b", a=2)[:, :, 0:nq]),
                reads=[acc_t, pt2], writes=[acc_t])
        ctxs = {}
        for i in range((NU + 1) // 2 + 2):
            for u_ in (2 * i, 2 * i + 1):
                if u_ < NU:
                    ctxs[u_] = S1(u_)
            for u_ in (2 * i - 2, 2 * i - 1):
                if 0 <= u_ < NU:
                    S2(ctxs[u_])
            for u_ in (2 * i - 4, 2 * i - 3):
                if 0 <= u_ < NU:
                    S3(ctxs.pop(u_))
        assert not ctxs
        act(acc[:, 1, :], acc[:, 1, :], AF.Ln, reads=[acc_t], writes=[acc_t])
        act(acc[:, 1, :], acc[:, 1, :], AF.Exp, reads=[acc_t], writes=[acc_t], scale=-1.0)
        P.op("dve", lambda e: e.tensor_mul(out=attf[:], in0=acc[:, 0, :], in1=acc[:, 1, :]),
             reads=[acc_t], writes=[attf_t])
        P.op("pool", lambda e: e.tensor_copy(out=AT[:, 8 + h, 0:1024], in_=attf[:]), reads=[attf_t], writes=[AT_t[8 + h]])
        q, qt = sqb.get()
        act(q[:, 0:1024], attf[:], AF.Square, reads=[attf_t], writes=[qt])
        for (c0, n) in PARTS_PRE:
            ps, pt = ps_f()
            mm(ps[:, 0:n], onesb[:], q[:, c0:c0 + n], True, True, reads=[qt, ct], writes=[pt])
            if h == 0:
                act(ssatt[:, c0:c0 + n], ps[:, 0:n], AF.Copy, reads=[pt], writes=[ssatt_t])
            else:
                P.op("dve", lambda e, ps=ps, c0=c0, n=n: e.tensor_add(out=ssatt[:, c0:c0 + n], in0=ssatt[:, c0:c0 + n],
                                                                       in1=ps[:, 0:n]), reads=[pt, ssatt_t], writes=[ssatt_t])
        pT, pTt = ps_b()
        tr(pT[0:16, 0:128], aqT[:, 1024:1040], identb[:], reads=[aqT_t, ct], writes=[pTt])
        act(qs_s[:, h * 128:(h + 1) * 128], pT[0:16, 0:128], AF.Copy, reads=[pTt], writes=[qs_t])

    def attn_finish():
        act(ssatt[:, 0:1024], ssatt[:, 0:1024], AF.Ln, reads=[ssatt_t, ct], writes=[ssatt_t], scale=1.0 / 1024, bias=epsb[:, 0:1])
        act(ssatt[:, 0:1024], ssatt[:, 0:1024], AF.Exp, reads=[ssatt_t], writes=[ssatt_t], scale=-0.5)
        for h in range(8):
            P.op("dve", lambda e, h=h: e.scalar_tensor_tensor(
                out=AT[:, 8 + h, 0:1024], in0=AT[:, 8 + h, 0:1024], scalar=nw[:, NW_ATT + h:NW_ATT + h + 1],
                in1=ssatt[:, 0:1024], op0=ALU.mult, op1=ALU.mult), reads=[AT_t[8 + h], ssatt_t, ct], writes=[AT_t[8 + h]])

    def sample_attention():
        psN = [(psf[0], psf_t[0]), (psf[1], psf_t[1])]
        psD = (psf[2], psf_t[2])
        psQ = [(psf[3], psf_t[3]), (psf[4], psf_t[4])]
        nunit = 4 * 3 * 4
        ulist = [(j, di, D, i) for j in range(4) for di, D in enumerate((1, 4, 16)) for i in range(4)]

        def SA(k):
            j, di, D, i = ulist[k]
            Kb, Kbt = Kt.get()
            Vb, Vbt = Vt.get()
            r0 = 2048 + i - D * 128
            ncache = 128 if D > 1 else 128 - i
            for (buf, bt, cd, sd, sdt) in ((Kb, Kbt, ck_d, ks_o, ksvs_t[0]), (Vb, Vbt, cv_d, vs_o, ksvs_t[1])):
                if D == 1:
                    dma("pool", buf[0:112, :], cd[j, r0:r0 + 112, :], writes=[bt])
                    dma("pool", buf[112:ncache, :], cd[j, r0 + 112:r0 + ncache, :], writes=[bt])
                    if i > 0:
                        dma("pool", buf[ncache:128, :], sd[4 * j:4 * j + i, :], reads=[sdt], writes=[bt])
                else:
                    dma("pool", buf[:, :], cd[j, r0:r0 + 127 * D + 1:D, :], writes=[bt])
            return dict(k=k, tok=4 * j + i, di=di, Kb=Kb, Kbt=Kbt, Vb=Vb, Vbt=Vbt)

        def SB(c):
            tok, di, Kb, Kbt = c["tok"], c["di"], c["Kb"], c["Kbt"]
            for hh in range(2):
                mm(psQ[hh][0][:, :], selq[0:16, tok * 128:(tok + 1) * 128], qs_s[:, hh * 512:(hh + 1) * 512], True, True,
                   reads=[qs_t, ct], writes=[psQ[hh][1]])
            pr, prt = prod.get()
            for hh in range(2):
                P.op("dve", lambda e, pr=pr, Kb=Kb, hh=hh: e.tensor_mul(
                    out=pr[:, hh * 512:(hh + 1) * 512], in0=Kb[:, hh * 512:(hh + 1) * 512], in1=psQ[hh][0][:, :]),
                    reads=[Kbt, psQ[hh][1]], writes=[prt])
            s8, s8t = sc8.get()
            P.op("dve", lambda e, s8=s8, pr=pr: e.tensor_reduce(
                out=s8[:], in_=pr[:].rearrange("p (h e) -> p h e", h=8), axis=AX.X, op=ALU.add), reads=[prt], writes=[s8t])
            s9, s9t = sc8.get()
            P.op("dve", lambda e, s8=s8, s9=s9, di=di: e.scalar_tensor_tensor(
                out=s9[:], in0=s8[:], scalar=SCALE, in1=sbias[:, di * 8:(di + 1) * 8], op0=ALU.mult, op1=ALU.add),
                reads=[s8t, ct], writes=[s9t])
            c["s9"], c["s9t"] = s9, s9t

        def SC(c):
            Vb, Vbt, s9, s9t = c["Vb"], c["Vbt"], c["s9"], c["s9t"]
            pe_, pet = sc8.get()
            act(pe_[:], s9[:], AF.Exp, reads=[s9t], writes=[pet])
            p8, p8t = p8b.get()
            act(p8[:], pe_[:], AF.Copy, reads=[pet], writes=[p8t])
            pv, pvt = pvb.get()
            for hd in range(4):
                act(pv[:, hd * 128:(hd + 1) * 128], Vb[:, hd * 128:(hd + 1) * 128], AF.Copy, reads=[Vbt, pet], writes=[pvt],
                    scale=pe_[:, hd:hd + 1])
            P.op("pool", lambda e, pv=pv, Vb=Vb, pe_=pe_: e.tensor_mul(
                out=pv[:, 512:1024].rearrange("p (h e) -> p h e", h=4), in0=Vb[:, 512:1024].rearrange("p (h e) -> p h e", h=4),
                in1=pe_[:, 4:8].unsqueeze(2).broadcast_to([128, 4, 128])), reads=[Vbt, pet], writes=[pvt])
            c["p8"], c["p8t"], c["pv"], c["pvt"] = p8, p8t, pv, pvt

        def SD(c):
            k, tok, pv, pvt, p8, p8t = c["k"], c["tok"], c["pv"], c["pvt"], c["p8"], c["p8t"]
            for hh in range(2):
                mm(psN[hh][0][0:16, :], selb[:, tok * 16:(tok + 1) * 16], pv[:, hh * 512:(hh + 1) * 512],
                   k == 0, k == nunit - 1, reads=[pvt, ct], writes=[psN[hh][1]])
            mm(psD[0][0:16, 0:8], selb[:, tok * 16:(tok + 1) * 16], p8[:], k == 0, k == nunit - 1,
               reads=[p8t, ct], writes=[psD[1]])
        cx = {}
        for it in range(nunit + 5):
            if it < nunit:
                cx[it] = SA(it)
            if 0 <= it - 3 < nunit:
                SB(cx[it - 3])
            if 0 <= it - 4 < nunit:
                SC(cx[it - 4])
            if 0 <= it - 5 < nunit:
                SD(cx.pop(it - 5))
        t1, t1t = sm16.get()
        P.op("dve", lambda e: e.tensor_mul(out=t1[:], in0=qs_s[:], in1=ks_s[:]), reads=[qs_t, ks_t], writes=[t1t])
        s1, s1t = sm8.get()
        P.op("dve", lambda e: e.tensor_reduce(out=s1[:], in_=t1[:].rearrange("p (h e) -> p h e", h=8), axis=AX.X, op=ALU.add),
             reads=[t1t], writes=[s1t])
        p1, p1t = sm8.get()
        act(p1[:], s1[:], AF.Exp, reads=[s1t], writes=[p1t], scale=SCALE)
        den, dent = sm8.get()
        P.op("dve", lambda e: e.scalar_tensor_tensor(out=den[:], in0=p1[:], scalar=3.0, in1=psD[0][0:16, 0:8],
                                                     op0=ALU.mult, op1=ALU.add), reads=[p1t, psD[1]], writes=[dent])
        P.op("dve", lambda e: e.reciprocal(out=den[:], in_=den[:]), reads=[dent], writes=[dent])
        t2, t2t = sm16.get()
        P.op("pool", lambda e: e.tensor_mul(out=t2[:].rearrange("p (h e) -> p h e", h=8),
                                            in0=vs_s[:].rearrange("p (h e) -> p h e", h=8),
                                            in1=p1[:].unsqueeze(2).broadcast_to([16, 8, 128])), reads=[vs_t, p1t], writes=[t2t])
        t3, t3t = sm16.get()
        for hh in range(2):
            P.op("dve", lambda e, hh=hh: e.scalar_tensor_tensor(
                out=t3[:, hh * 512:(hh + 1) * 512], in0=t2[:, hh * 512:(hh + 1) * 512], scalar=3.0, in1=psN[hh][0][0:16, :],
                op0=ALU.mult, op1=ALU.add), reads=[t2t, psN[hh][1]], writes=[t3t])
        P.op("dve", lambda e: e.tensor_mul(out=t3[:].rearrange("p (h e) -> p h e", h=8),
                                           in0=t3[:].rearrange("p (h e) -> p h e", h=8),
                                           in1=den[:].unsqueeze(2).broadcast_to([16, 8, 128])), reads=[t3t, dent], writes=[t3t])
        t4, t4t = sm16.get()
        ss1, ss1t = sm8.get()
        act(t4[:], t3[:], AF.Square, reads=[t3t], writes=[t4t, ss1t], accum_out=ss1[:, 0:1])
        act(ss1[:, 0:1], ss1[:, 0:1], AF.Sqrt, reads=[ss1t, ct], writes=[ss1t], scale=1.0 / 1024, bias=epsb[0:16, 0:1])
        P.op("dve", lambda e: e.reciprocal(out=ss1[:, 0:1], in_=ss1[:, 0:1]), reads=[ss1t], writes=[ss1t])
        P.op("dve", lambda e: e.scalar_tensor_tensor(out=atts_b[:], in0=t3[:], scalar=ss1[:, 0:1], in1=anwb[:],
                                                     op0=ALU.mult, op1=ALU.mult), reads=[t3t, ss1t, ct], writes=[atts_bt])
        for h in range(8):
            pT, pTt = ps_b()
            tr(pT[:, 0:16], atts_b[:, h * 128:(h + 1) * 128], identb[0:16, 0:16], reads=[atts_bt, ct], writes=[pTt])
            act(AT[:, 8 + h, 1024:1040], pT[:, 0:16], AF.Copy, reads=[pTt], writes=[AT_t[8 + h]])

    xTv = xT.rearrange("(kc p) n -> kc p n", p=128)
    xsTv = xsT.rearrange("(kc p) n -> kc p n", p=128)
    for g in range(4):
        own = g == 3
        if DEBUG == "gla0" and not own:
            continue
        parts = PARTS_OWN if own else PARTS_PRE

        def src_chunks(kc, g=g, own=own):
            b, t = xc.get()
            dma("sp" if kc % 2 == 0 else "pool", b[:, 0:1024], xTv[kc, :, g * 1024:(g + 1) * 1024], writes=[t])
            if own:
                P.op("pool", lambda e, b=b, kc=kc: e.tensor_copy(out=b[:, 1024:1040], in_=xs_all[:, kc, :]), reads=[xs_t], writes=[t])
            return b[:, 0:(NTOK if own else 1024)], [t]
        ssq_norm(src_chunks, parts, NW_ATTN, hT, hT_t, 2048.0)
        dumpf(0, rB[:, :], [rB_t])
        dumpb(0, hT[:, 0, :], [hT_t[0]])
        dumpb(1, hT[:, 15, :], [hT_t[15]])
        gla_group(g, own)
        if g in (1, 2):
            P.barrier()
            kv_prefix_group(g)
            P.barrier()
    for h in range(4):
        dma("sp", gst_o[h], Sf[:, h, :], reads=[Sf_t[h]])
    P.barrier()
    load_attn_consts()
    kv_outputs()
    if DEBUG == "A":
        raise StopBuild()
    for h in range(8):
        attn_head(h)
    attn_finish()
    P.barrier()
    if DEBUG == "B":
        raise StopBuild()
    wo_issued = []

    def wo_issue(cb):
        b_, t_ = wblk.get()
        dma("pool", b_[:], wo_flat[:, cb * 4096:(cb + 1) * 4096], writes=[t_])
        wo_issued.append((b_, t_))
    for cb_ in range(3):
        wo_issue(cb_)
    sample_attention()
    P.barrier()
    if DEBUG == "C":
        raise StopBuild()
    if DEBUG == "F":
        for sl, c in zip(range(2, 8), (0, 1, 7, 8, 9, 15)):
            dumpb(sl, AT[:, c, :], [AT_t[c]])

    A4 = Arena(nc, P1C_START)
    x1T = A4.alloc([128, 16, NTOK], F32)
    x1T_t = [Trk() for _ in range(16)]
    x1T_end = A4.off
    xres = Rot(A4, [128, NTOK], F32, 2)
    wo = wblk
    for cb in range(8):
        if cb >= 1 and cb + 2 < 8:
            wo_issue(cb + 2)
        b, t = wo_issued[cb]
        b = b[:].rearrange("p (kc n) -> p kc n", kc=16)
        for sub in range(2):
            dc = cb * 2 + sub
            xr, xrt = xres.get()
            dma("sp", xr[:, 0:1024], xTv[dc, :, 3072:4096], writes=[xrt])
            P.op("pool", lambda e, xr=xr, dc=dc: e.tensor_copy(out=xr[:, 1024:1040], in_=xs_all[:, dc, :]), reads=[xs_t], writes=[xrt])
            for (c0, n) in PARTS_MLP:
                ps, pt = ps_f()
                for ac in range(16):
                    mm(ps[:, 0:n], b[:, ac, sub * 128:(sub + 1) * 128], AT[:, ac, c0:c0 + n], ac == 0, ac == 15,
                       reads=[t, AT_t[ac]], writes=[pt])
                P.op("dve", lambda e, ps=ps, xr=xr, dc=dc, c0=c0, n=n: e.tensor_add(
                    out=x1T[:, dc, c0:c0 + n], in0=ps[:, 0:n], in1=xr[:, c0:c0 + n]), reads=[pt, xrt], writes=[x1T_t[dc]])
    if DEBUG == "F":
        dumpf(1, x1T[:, 0, :], [x1T_t[0]])
        dumpf(2, x1T[:, 15, :], [x1T_t[15]])
        P.barrier()
    h2T, h2T_t = hT, hT_t

    def src2(kc):
        return x1T[:, kc, :], [x1T_t[kc]]
    ssq_norm(src2, PARTS_MLP, NW_FFN, h2T, h2T_t, 2048.0)
    P.barrier()
    if DEBUG == "D":
        raise StopBuild()

    A5 = Arena(nc, R_START)
    wu = Rot(A5, [128, 16, 512], BF16, 2)
    wd = Rot(A5, [128, 4, 2048], BF16, 2)
    HT = Rot(A5, [128, 4, NTOK], BF16, 1)
    assert A5.off <= P1C_START, (A5.off, P1C_START)
    A5b = Arena(nc, x1T_end)
    HT.b.append(A5b.alloc([128, 4, NTOK], BF16)); HT.t.append(Trk())
    rl = Rot(A5b, [128, 512], F32, 3)
    for g in range(16):
        ub, ut = wu.get()
        dma("pool", ub[:].rearrange("p a b -> p (a b)"), wu_flat[:, g * 8192:(g + 1) * 8192], writes=[ut])
        db, dt_ = wd.get()
        dma("pool", db[:], w_down[g * 512:(g + 1) * 512, :].rearrange("(fc p) n -> p fc n", p=128), writes=[dt_])
        hb, hbt = HT.get()
        for fc in range(4):
            for (c0, n) in PARTS_MLP:
                ps, pt = ps_f()
                for kc in range(16):
                    mm(ps[:, 0:n], ub[:, kc, fc * 128:(fc + 1) * 128], h2T[:, kc, c0:c0 + n], kc == 0, kc == 15,
                       reads=[ut, h2T_t[kc]], writes=[pt])
                r_, r_t = rl.get()
                act(r_[:, 0:n], ps[:, 0:n], AF.Relu, reads=[pt], writes=[r_t])
                P.op("pool", lambda e, hb=hb, r_=r_, fc=fc, c0=c0, n=n: e.tensor_mul(
                    out=hb[:, fc, c0:c0 + n], in0=r_[:, 0:n], in1=r_[:, 0:n]), reads=[r_t], writes=[hbt])
        for dc in range(16):
            for (c0, n) in PARTS_MLP:
                ps, pt = ps_f()
                for fc in range(4):
                    mm(ps[:, 0:n], db[:, fc, dc * 128:(dc + 1) * 128], hb[:, fc, c0:c0 + n], fc == 0, fc == 3,
                       reads=[dt_, hbt], writes=[pt])
                P.op("dve", lambda e, ps=ps, dc=dc, c0=c0, n=n: e.tensor_add(
                    out=x1T[:, dc, c0:c0 + n], in0=x1T[:, dc, c0:c0 + n], in1=ps[:, 0:n]), reads=[pt, x1T_t[dc]], writes=[x1T_t[dc]])
    P.barrier()

    if DEBUG == "E":
        raise StopBuild()
    if DEBUG == "F":
        dumpf(3, x1T[:, 0, :], [x1T_t[0]])
        dumpf(4, x1T[:, 15, :], [x1T_t[15]])
        P.barrier()
    A6 = Arena(nc, R_START)
    sqb4 = Rot(A6, [128, NTOK], BF16, 2)
    rB4 = A6.alloc([128, NTOK], F32)
    rB4_t = Trk()
    yT, yT_t = x1T, x1T_t
    ytok = Rot(A6, [128, 2048], F32, 2)
    pss = [ps_f() for _ in PARTS_OWN]
    for kc in range(16):
        q, qt = sqb4.get()
        act(q[:, 0:NTOK], x1T[:, kc, :], AF.Square, reads=[x1T_t[kc]], writes=[qt])
        for (c0, n), (ps, pt) in zip(PARTS_OWN, pss):
            mm(ps[:, 0:n], onesb[:], q[:, c0:c0 + n], kc == 0, kc == 15, reads=[qt, ct], writes=[pt])
    for (c0, n), (ps, pt) in zip(PARTS_OWN, pss):
        act(rB4[:, c0:c0 + n], ps[:, 0:n], AF.Ln, reads=[pt, ct], writes=[rB4_t], scale=1.0 / 2048, bias=epsb[:, 0:1])
    act(rB4[:, :], rB4[:, :], AF.Exp, reads=[rB4_t], writes=[rB4_t], scale=-0.5)
    for kc in range(16):
        P.op("dve", lambda e, kc=kc: e.scalar_tensor_tensor(
            out=yT[:, kc, :], in0=x1T[:, kc, :], scalar=nw[:, NW_FIN + kc:NW_FIN + kc + 1], in1=rB4[:, :],
            op0=ALU.mult, op1=ALU.mult), reads=[x1T_t[kc], rB4_t, ct], writes=[yT_t[kc]])
    for ti in range(9):
        m = 128 if ti < 8 else 16
        yb, ybt = ytok.get()
        for q4 in range(4):
            ps, pt = ps_f()
            for s in range(4):
                kc = q4 * 4 + s
                tr(ps[0:m, s * 128:(s + 1) * 128], yT[:, kc, ti * 128:ti * 128 + m], identf[:], reads=[yT_t[kc], ct], writes=[pt])
            if q4 % 2 == 0:
                act(yb[0:m, q4 * 512:(q4 + 1) * 512], ps[0:m, :], AF.Copy, reads=[pt], writes=[ybt])
            else:
                P.op("dve", lambda e, yb=yb, ps=ps, q4=q4, m=m: e.tensor_copy(out=yb[0:m, q4 * 512:(q4 + 1) * 512], in_=ps[0:m, :]),
                     reads=[pt], writes=[ybt])
        if ti < 8:
            dma("sp", y_o[ti * 128:(ti + 1) * 128, :], yb[:], reads=[ybt])
        else:
            dma("sp", ys_o[:, :], yb[0:16, :], reads=[ybt])


_CACHE = {}


def _consts():
    import ml_dtypes
    bf = ml_dtypes.bfloat16
    c = {}
    c["identf"] = np.eye(128, dtype=np.float32)
    c["identb"] = np.eye(128, dtype=np.float32).astype(bf)
    c["onesb"] = np.ones((128, 128), np.float32).astype(bf)
    s = np.arange(128)
    c["U"] = (s[:, None] <= s[None, :]).astype(np.float32)
    t = np.arange(16)
    c["Ub"] = ((t[:, None] <= t[None, :]) & (t[:, None] // 4 == t[None, :] // 4)).astype(np.float32)
    slopes = np.exp2(-8.0 * np.arange(1, 9, dtype=np.float32) / 8).astype(np.float32)
    kk = np.arange(128)[:, None]
    qq = np.arange(256)[None, :]
    dist = qq - kk
    valid = (dist >= 0) & (dist <= 128)
    am = np.zeros((128, 3, 8, 256), np.float32)
    for di, D in enumerate((1, 4, 16)):
        for h in range(8):
            am[:, di, h, :] = np.where(valid, np.exp(-slopes[h] * D * np.maximum(dist, 0).astype(np.float32)), 0.0)
    c["amask"] = am.reshape(128, -1).astype(bf)
    selb = np.zeros((128, 16, 16), np.float32)
    for tok in range(16):
        selb[:, tok, tok] = 1.0
    c["selb"] = selb.reshape(128, 256).astype(bf)
    selq = np.zeros((16, 16, 128), np.float32)
    for tok in range(16):
        selq[tok, tok, :] = 1.0
    c["selq"] = selq.reshape(16, -1).astype(bf)
    m = np.arange(128)
    sb = np.zeros((128, 3, 8), np.float32)
    for di, D in enumerate((1, 4, 16)):
        sb[:, di, :] = -slopes[None, :] * (D * (128 - m))[:, None]
    c["sbias"] = sb.reshape(128, 24)
    c["smask"] = (t[:, None] // 4 == np.arange(4)[None, :]).astype(np.float32)
    return c


def _kb(ci):
    s = ci * 1024
    kb = np.zeros((128, NU), np.float32)
    for ui, u in enumerate(UNITS):
        kk = np.arange(128)
        uloc = (128 * u["j"] + kk) * u["D"] + u["r"]
        tglob = s - 2048 + uloc
        kb[:, ui] = np.where(tglob >= 0, 0.0, -30000.0)
    return kb


def kernel(x_prompt, x_sample, cache_k_win, cache_v_win, state_gla, attn_norm_w, w_in, w_gk_up, b_gk,
           gla_norm_w, att_out_norm_w, w_out, ffn_norm_w, w_up, w_down, final_norm_w):
    f = lambda a: np.ascontiguousarray(np.asarray(a, dtype=np.float32))
    x_prompt, x_sample = f(x_prompt), f(x_sample)
    if "nc" not in _CACHE:
        _CACHE["nc"] = build()
    nc = _CACHE["nc"]
    consts = _consts()
    col = lambda w, n: f(w).reshape(n, 128).T
    nwm = np.concatenate([col(attn_norm_w[0], 16), col(ffn_norm_w[0], 16), col(final_norm_w, 16),
                          col(gla_norm_w[0], 2), col(att_out_norm_w[0], 8)], axis=1)
    def tile_cols(w, c0, n):
        return w[:, c0:c0 + n].reshape(16, 128, n).transpose(1, 0, 2).reshape(128, 16 * n)
    w_in0, w_out0, w_up0 = f(w_in[0]), f(w_out[0]), f(w_up[0])
    wflat = np.empty((128, WTOT[0]), np.float32)
    for key, o in WOFF.items():
        blk = np.concatenate([w_in0[:, sc:sc + n] for (sc, n) in key], axis=1)
        W = blk.shape[1]
        wflat[:, o:o + 16 * W] = tile_cols(blk, 0, W)
    wo_flat = np.concatenate([tile_cols(w_out0, cb * 256, 256) for cb in range(8)], axis=1)
    wu_flat = np.concatenate([tile_cols(w_up0, g * 512, 512) for g in range(16)], axis=1)
    shared = dict(wflat=wflat, wo_flat=np.ascontiguousarray(wo_flat), wu_flat=np.ascontiguousarray(wu_flat), w_down=f(w_down[0]),
                  wgk=np.concatenate([f(w_gk_up[0]), f(b_gk[0])[None, :]], axis=0), nw=np.ascontiguousarray(nwm),
                  anwb=np.ascontiguousarray(np.broadcast_to(f(att_out_norm_w[0])[None, :], (16, 1024))), **consts)
    in_maps = []
    for c in range(8):
        b, ci = c // 4, c % 4
        s = ci * 1024
        xt = np.zeros((2048, 4096), np.float32)
        lo = max(0, s - 3072)
        seg = x_prompt[b, lo:s + 1024, :]
        xt[:, 4096 - seg.shape[0]:] = seg.T
        m = dict(shared)
        m["xT"] = xt
        m["xsT"] = np.ascontiguousarray(x_sample[4 * c:4 * c + 4].reshape(16, 2048).T)
        m["ck"] = np.ascontiguousarray(f(cache_k_win[0, 4 * c:4 * c + 4]).reshape(4, 2048, 1024))
        m["cv"] = np.ascontiguousarray(f(cache_v_win[0, 4 * c:4 * c + 4]).reshape(4, 2048, 1024))
        m["st"] = f(state_gla[0, 4 * c:4 * c + 4])
        m["kb"] = _kb(ci)
        in_maps.append(m)
    res = run_bass_kernel_spmd(nc, in_maps, core_ids=list(range(8)))
    R = res.results
    if DEBUG == "gla0":
        return R
    if DEBUG:
        z16 = np.zeros((16, 2048), np.float32)
        for r in R:
            r.setdefault("y_own", np.zeros((1024, 2048), np.float32))
            r.setdefault("y_s", z16)
    y_prompt = np.zeros((2, 4096, 2048), np.float32)
    y_sample = np.zeros((32, 4, 2048), np.float32)
    k_win = np.zeros((1, 2, 2048, 8, 128), np.float32)
    v_win = np.zeros((1, 2, 2048, 8, 128), np.float32)
    gla_p = np.zeros((1, 2, 4, 128, 256), np.float32)
    k_new = np.zeros((1, 32, 4, 8, 128), np.float32)
    v_new = np.zeros((1, 32, 4, 8, 128), np.float32)
    gla_s = np.zeros((1, 32, 4, 128, 256), np.float32)
    for c in range(8):
        b, ci = c // 4, c % 4
        s = ci * 1024
        r = R[c]
        y_prompt[b, s:s + 1024] = r["y_own"]
        y_sample[4 * c:4 * c + 4] = r["y_s"].reshape(4, 4, 2048)
        if ci >= 2:
            k_win[0, b, s - 2048:s - 1024] = r["k_own"].reshape(1024, 8, 128)
            v_win[0, b, s - 2048:s - 1024] = r["v_own"].reshape(1024, 8, 128)
        if ci == 3:
            gla_p[0, b] = r["gla_st"]
        k_new[0, 4 * c:4 * c + 4] = r["k_s"].reshape(4, 4, 8, 128)
        v_new[0, 4 * c:4 * c + 4] = r["v_s"].reshape(4, 4, 8, 128)
        gla_s[0, 4 * c:4 * c + 4] = r["gla_s"]
    return (y_prompt, y_sample, k_win, v_win, gla_p, k_new, v_new, gla_s)
```

```python
import numpy as np
import concourse.bass as bass
import concourse.mybir as mybir
from concourse.bass_utils import run_bass_kernel_spmd

F32, BF16 = mybir.dt.float32, mybir.dt.bfloat16
AF = mybir.ActivationFunctionType
ALU = mybir.AluOpType
AX = mybir.AxisListType

NQ = 24


class Trk:
    __slots__ = ("w", "r")

    def __init__(self):
        self.w = None
        self.r = []


class Op:
    __slots__ = ("eng", "fn", "deps", "dma", "slot", "inc", "val")

    def __init__(self, eng, fn, deps, dma):
        self.eng, self.fn, self.deps, self.dma = eng, fn, deps, dma
        self.slot, self.inc, self.val = None, False, 0


class Prog:
    ENGS = ["pe", "act", "dve", "pool", "sp"]

    def __init__(self):
        self.ops = []
        self.slot_last = {"sp": [None] * NQ, "pool": [None] * NQ}
        self.dma_cnt = {"sp": 0, "pool": 0}
        self.last = {e: None for e in self.ENGS}

    def op(self, eng, fn, reads=(), writes=(), dma=False):
        deps = set()
        for t in reads:
            if t.w is not None:
                deps.add(t.w)
        for t in writes:
            if t.w is not None:
                deps.add(t.w)
            last = {}
            for r in t.r:
                ro = self.ops[r]
                if ro.dma:
                    deps.add(r)
                else:
                    last[ro.eng] = r
            deps.update(last.values())
        if eng == "pe":
            deps = {d for d in deps if self.ops[d].eng != "pe"}
        i = len(self.ops)
        o = Op(eng, fn, deps, dma)
        if dma:
            k = self.dma_cnt[eng] % NQ
            self.dma_cnt[eng] += 1
            if self.slot_last[eng][k] is not None:
                deps.add(self.slot_last[eng][k])
            self.slot_last[eng][k] = i
            o.slot = k
        self.ops.append(o)
        for t in reads:
            t.r.append(i)
        for t in writes:
            t.w = i
            t.r = []
        self.last[eng] = i
        return i

    def barrier(self):
        allp = set(x for x in self.last.values() if x is not None)
        for q in self.slot_last.values():
            allp.update(x for x in q if x is not None)
        for e in self.ENGS:
            self.ops.append(Op(e, None, set(allp), False))

    def emit(self, nc):
        ops = self.ops
        for o in ops:
            for d in o.deps:
                ops[d].inc = True
        cnt = {e: 0 for e in self.ENGS}
        slotcnt = {"sp": [0] * NQ, "pool": [0] * NQ}
        for o in ops:
            if o.fn is None:
                continue
            if o.dma:
                slotcnt[o.eng][o.slot] += 16
                o.val = slotcnt[o.eng][o.slot]
            elif o.inc:
                cnt[o.eng] += 1
                o.val = cnt[o.eng]
        import contextlib

        with contextlib.ExitStack() as es:
            csem = {e: es.enter_context(nc.semaphore("c_" + e)) for e in self.ENGS}
            dsem = {q: [es.enter_context(nc.semaphore(f"d_{q}{k}")) for k in range(NQ)] for q in ("sp", "pool")}
            block = es.enter_context(nc.Block())

            def run(ename, eng):
                waited = {}
                for o in ops:
                    if o.eng != ename:
                        continue
                    for d in sorted(o.deps):
                        do = ops[d]
                        if do.dma:
                            sem, key = dsem[do.eng][do.slot], ("d", do.eng, do.slot)
                        else:
                            sem, key = csem[do.eng], ("c", do.eng)
                        if waited.get(key, 0) < do.val:
                            eng.wait_ge(sem, do.val)
                            waited[key] = do.val
                    if o.fn is None:
                        continue
                    ins = o.fn(eng)
                    if o.dma:
                        ins.then_inc(dsem[ename][o.slot], 16)
                    elif o.inc:
                        ins.then_inc(csem[ename], 1)

            @block.tensor
            def _(e):
                run("pe", e)

            @block.scalar
            def _(e):
                run("act", e)

            @block.vector
            def _(e):
                run("dve", e)

            @block.gpsimd
            def _(e):
                run("pool", e)

            @block.sync
            def _(e):
                run("sp", e)


class Arena:
    def __init__(self, nc, base=0):
        self.nc, self.off, self.n = nc, base, 0

    def alloc(self, shape, dt):
        per = int(np.prod(shape[1:])) * (4 if dt == F32 else 2)
        per = (per + 31) // 32 * 32
        h = self.nc.alloc_sbuf_tensor_at(f"sb{self.n}", list(shape), dt, offset=self.off)
        self.n += 1
        self.off += per
        assert self.off <= 229376, self.off
        return h


NTOK = 1040
PARTS_OWN = [(0, 512), (512, 512), (1024, 16)]
PARTS_PRE = [(0, 512), (512, 512)]
PARTS_MLP = [(0, 347), (347, 347), (694, 346)]
C_GQ, C_GK, C_GV, C_GG, C_GLR, C_AQ, C_AK, C_AV = 0, 512, 1024, 2048, 3072, 3088, 4112, 5136
SCALE = 128 ** -0.5
EPS = 1e-6


def attn_units():
    units = []
    for di, D in enumerate((1, 4, 16)):
        lq0, lq1 = 2048 // D, 3072 // D
        for r in range(D):
            j = 0
            while 128 * j < lq1:
                if 128 * (j + 1) > lq0 - 128:
                    nk = min(128, lq1 - 128 * j)
                    q0, q1 = max(128 * j, lq0), min(128 * j + 256, lq1)
                    if q1 > q0:
                        units.append(dict(di=di, D=D, r=r, j=j, nk=nk, kc0=128 * j * D + r, q0=q0, nq=q1 - q0,
                                          qc0=q0 * D + r - 2048, m0=q0 - 128 * j))
                j += 1
    return units


UNITS = attn_units()
NU = len(UNITS)


DEBUG = None


class StopBuild(Exception):
    pass


WLIST = None
WLOG = []
WOFF = {}
WTOT = [0]


def build():
    global WLIST
    WLIST = None
    del WLOG[:]
    WOFF.clear()
    WTOT[0] = 0
    try:
        _build(bass.Bass("TRN2", target_bir_lowering=False), Prog())
    except StopBuild:
        pass
    WLIST = list(WLOG)
    o = 0
    for key in WOFF:
        WOFF[key] = o
        o += 16 * sum(n for (_, n) in key)
    WTOT[0] = o
    nc = bass.Bass("TRN2", target_bir_lowering=False)
    P = Prog()
    try:
        _build(nc, P)
    except StopBuild:
        pass
    P.barrier()
    P.emit(nc)
    return nc


def _build(nc, P):

    def din(name, shape, dt=F32):
        return nc.dram_tensor(name, list(shape), dt, kind="ExternalInput").ap()

    def dout(name, shape):
        return nc.dram_tensor(name, list(shape), F32, kind="ExternalOutput").ap()

    xT = din("xT", [2048, 4096])
    xsT = din("xsT", [2048, 16])
    wflat = din("wflat", [128, max(WTOT[0], 16)])
    wo_flat = din("wo_flat", [128, 8 * 4096])
    wu_flat = din("wu_flat", [128, 16 * 8192])
    w_down = din("w_down", [8192, 2048])
    wgk_d = din("wgk", [17, 512])
    nw_d = din("nw", [128, 58])
    anwb_d = din("anwb", [16, 1024])
    ck_d = din("ck", [4, 2048, 1024])
    cv_d = din("cv", [4, 2048, 1024])
    st_d = din("st", [4, 4, 128, 256])
    identf_d = din("identf", [128, 128])
    identb_d = din("identb", [128, 128], BF16)
    onesb_d = din("onesb", [128, 128], BF16)
    U_d = din("U", [128, 128])
    Ub_d = din("Ub", [16, 16])
    amask_d = din("amask", [128, 3 * 8 * 256], BF16)
    kb_d = din("kb", [128, NU])
    selb_d = din("selb", [128, 256], BF16)
    selq_d = din("selq", [16, 16 * 128], BF16)
    sbias_d = din("sbias", [128, 24])
    sm_d = din("smask", [16, 4])

    y_o = dout("y_own", [1024, 2048])
    ys_o = dout("y_s", [16, 2048])
    k_o = dout("k_own", [1024, 1024])
    v_o = dout("v_own", [1024, 1024])
    gst_o = dout("gla_st", [4, 128, 256])
    ks_o = dout("k_s", [16, 1024])
    vs_o = dout("v_s", [16, 1024])
    gs_o = dout("gla_s", [4, 4, 128, 256])
    if DEBUG:
        dbgf = nc.dram_tensor("dbgf", [128, 8, NTOK], F32, kind="ExternalOutput").ap()
        dbgb = nc.dram_tensor("dbgb", [128, 8, NTOK], BF16, kind="ExternalOutput").ap()

    def dumpf(slot, ap, trks, p=128, n=NTOK):
        if DEBUG:
            P.op("sp", lambda e: e.dma_start(out=dbgf[0:p, slot, 0:n], in_=ap), reads=trks, dma=True)

    def dumpb(slot, ap, trks, p=128, n=NTOK):
        if DEBUG:
            P.op("sp", lambda e: e.dma_start(out=dbgb[0:p, slot, 0:n], in_=ap), reads=trks, dma=True)
    kvscr = nc.dram_tensor("kvscr", [2, 8, 128, 2048], BF16).ap()
    kvscr_t = [[Trk() for _ in range(8)] for _ in range(2)]
    ksvs_t = [Trk(), Trk()]

    A0 = Arena(nc, 16640)
    identf, identb, onesb, Uf = (A0.alloc([128, 128], F32), A0.alloc([128, 128], BF16),
                                 A0.alloc([128, 128], BF16), A0.alloc([128, 128], F32))
    nw = A0.alloc([128, 58], F32)
    Sf = A0.alloc([128, 4, 256], F32)
    Sb = A0.alloc([128, 4, 256], BF16)
    ct = Trk()
    Sf_t = [Trk() for _ in range(4)]
    Sb_t = [Trk() for _ in range(4)]

    def dma(q, out, in_, reads=(), writes=()):
        return P.op(q, lambda e: e.dma_start(out=out, in_=in_), reads=reads, writes=writes, dma=True)

    epsb = A0.alloc([128, 2], F32)
    xs_all = A0.alloc([128, 16, 16], F32)
    xs_t = Trk()
    for dst, src in ((identf, identf_d), (identb, identb_d), (onesb, onesb_d), (Uf, U_d), (nw, nw_d)):
        dma("sp", dst[:], src, writes=[ct])
    dma("sp", xs_all[:], xsT.rearrange("(kc p) n -> p kc n", p=128), writes=[xs_t])
    P.op("pool", lambda e: e.memset(Sf[:], 0.0), writes=Sf_t)
    P.op("pool", lambda e: e.memset(Sb[:], 0.0), writes=Sb_t)
    NW_ATTN, NW_FFN, NW_FIN, NW_GLA, NW_ATT = 0, 16, 32, 48, 50

    psf = [nc.alloc_psum_tensor(f"psf{i}", [128, 512], F32) for i in range(6)]
    psf_t = [Trk() for _ in range(6)]
    psb = [nc.alloc_psum_tensor(f"psb{i}", [128, 1024], BF16) for i in range(2)]
    psb_t = [Trk() for _ in range(2)]
    rr = {"f": 0, "b": 0}

    def ps_f():
        i = rr["f"] % 6
        rr["f"] += 1
        return psf[i], psf_t[i]

    def ps_b():
        i = rr["b"] % 2
        rr["b"] += 1
        return psb[i], psb_t[i]

    def mm(out, lhsT, rhs, start, stop, reads, writes):
        P.op("pe", lambda e: e.matmul(out, lhsT=lhsT, rhs=rhs, start=start, stop=stop), reads=reads, writes=writes)

    def tr(out, in_, ident, reads, writes):
        P.op("pe", lambda e: e.transpose(out, in_, ident), reads=reads, writes=writes)

    def act(out, in_, func, reads, writes, **kw):
        P.op("act", lambda e: e.activation(out=out, in_=in_, func=func, **kw), reads=reads, writes=writes)

    class Rot:
        def __init__(self, arena, shape, dt, n):
            self.b = [arena.alloc(shape, dt) for _ in range(n)]
            self.t = [Trk() for _ in range(n)]
            self.i = 0

        def get(self):
            k = self.i % len(self.b)
            self.i += 1
            return self.b[k], self.t[k]

    A1 = Arena(nc, A0.off)
    hT = A1.alloc([128, 16, NTOK], BF16)
    hT_t = [Trk() for _ in range(16)]
    R_START = A1.off
    xc = Rot(A1, [128, NTOK], F32, 4)
    sqb = Rot(A1, [128, NTOK], BF16, 2)
    rB = A1.alloc([128, NTOK], F32)
    rB_t = Trk()
    wblk = Rot(A1, [128, 4096], BF16, 3)
    AT = A1.alloc([128, 16, NTOK], BF16)
    AT_t = [Trk() for _ in range(16)]
    P1C_START = A1.off
    glrT = A1.alloc([32, NTOK], F32)
    glrT_t = Trk()
    wgk = A1.alloc([32, 512], F32)
    Ub = A1.alloc([16, 16], F32)
    smask = A1.alloc([16, 4], F32)
    for dst, src in ((wgk[0:17, :], wgk_d), (Ub[:], Ub_d), (smask[:], sm_d)):
        dma("sp", dst, src, writes=[ct])
    A1_mark = A1.off

    def ssq_norm(src_chunks, parts, nw_col, dst, dst_t, div):
        N = parts[-1][0] + parts[-1][1]
        pss = [ps_f() for _ in parts]
        for kc in range(16):
            s_ap, s_t = src_chunks(kc)
            q, qt = sqb.get()
            act(q[:, 0:N], s_ap, AF.Square, reads=s_t, writes=[qt])
            P.op("dve", lambda e, kc=kc, s_ap=s_ap: e.tensor_scalar_mul(
                out=dst[:, kc, 0:N], in0=s_ap, scalar1=nw[:, nw_col + kc:nw_col + kc + 1]),
                reads=list(s_t) + [ct], writes=[dst_t[kc]])
            for (c0, n), (ps, pt) in zip(parts, pss):
                mm(ps[:, 0:n], onesb[:], q[:, c0:c0 + n], kc == 0, kc == 15, reads=[qt, ct], writes=[pt])
        for (c0, n), (ps, pt) in zip(parts, pss):
            act(rB[:, c0:c0 + n], ps[:, 0:n], AF.Ln, reads=[pt, ct], writes=[rB_t], scale=1.0 / div, bias=epsb[:, 0:1])
        act(rB[:, 0:N], rB[:, 0:N], AF.Exp, reads=[rB_t], writes=[rB_t], scale=-0.5)
        for kc in range(16):
            eng = "dve" if kc % 2 == 0 else "pool"
            P.op(eng, lambda e, kc=kc: e.tensor_mul(out=dst[:, kc, 0:N], in0=dst[:, kc, 0:N], in1=rB[:, 0:N]),
                 reads=[dst_t[kc], rB_t], writes=[dst_t[kc]])

    P.op("pool", lambda e: e.memset(epsb[:, 0:1], EPS), writes=[ct])
    P.op("pool", lambda e: e.memset(epsb[:, 1:2], float(np.log(SCALE))), writes=[ct])
    P.barrier()

    wstate = {"i": 0, "issued": []}

    def _wissue(col_specs):
        b, t = wblk.get()
        W = sum(n for (_, _, n) in col_specs)
        off = 0
        for (dc, sc, n) in col_specs:
            assert dc == off
            off += n
        key = tuple((sc, n) for (_, sc, n) in col_specs)
        view = b[:, 0:16 * W].rearrange("p (kc n) -> p kc n", kc=16)
        if WLIST is None:
            WOFF.setdefault(key, None)
            src = wflat[:, 0:16]
            dma("pool", b[:, 0:16], src, writes=[t])
        else:
            o = WOFF[key]
            dma("pool", b[:, 0:16 * W], wflat[:, o:o + 16 * W], writes=[t])
        return view, t

    def wload(col_specs, src=None, rows=None):
        i = wstate["i"]
        wstate["i"] += 1
        if WLIST is None:
            WLOG.append(col_specs)
            return _wissue(col_specs)
        assert WLIST[i] == col_specs
        while len(wstate["issued"]) <= min(i + 1, len(WLIST) - 1):
            wstate["issued"].append(_wissue(WLIST[len(wstate["issued"])]))
        return wstate["issued"][i]

    def proj_fm(wb, wt, wc, m, parts, evac):
        for (c0, n) in parts:
            ps, pt = ps_f()
            for kc in range(16):
                mm(ps[0:m, 0:n], wb[:, kc, wc:wc + m], hT[:, kc, c0:c0 + n], kc == 0, kc == 15,
                   reads=[wt, hT_t[kc]], writes=[pt])
            evac(ps, pt, c0, n)

    def proj_tm(wb, wt, ncols, t0, m, evac):
        ps, pt = ps_f()
        for kc in range(16):
            mm(ps[0:m, 0:ncols], hT[:, kc, t0:t0 + m], wb[:, kc, 0:ncols], kc == 0, kc == 15,
               reads=[wt, hT_t[kc]], writes=[pt])
        evac(ps, pt)

    A2 = Arena(nc, A1_mark)
    GB = []
    for _ in range(2):
        GB.append(dict(gkT=A2.alloc([128, NTOK], F32), gqT=A2.alloc([128, NTOK], F32), ggs=A2.alloc([128, 2, NTOK], BF16),
                       gv=A2.alloc([128, 9, 256], BF16), gkT_t=Trk(), gqT_t=Trk(), ggs_t=Trk(), gv_t=[Trk() for _ in range(9)]))
    o_h = A2.alloc([128, 2, NTOK], F32)
    o_t = Trk()
    tmpf = Rot(A2, [128, 128], F32, 6)
    f512 = Rot(A2, [128, 512], F32, 5)
    xq = Rot(A2, [128, 512], BF16, 2)
    xk = Rot(A2, [128, 512], BF16, 2)
    xa = Rot(A2, [128, 4, 128], BF16, 2)
    xh = Rot(A2, [128, 4, 128], BF16, 2)
    xhT = Rot(A2, [128, 512], BF16, 2)
    tmpb = Rot(A2, [128, 128], BF16, 8)
    khat4 = A2.alloc([128, 4, 16], BF16)
    khat4_t = Trk()
    blast = Rot(A2, [128, 4], F32, 6)
    sgate, sgate_t = rB, rB_t
    S0 = Rot(A2, [128, 256], F32, 4)
    S0b = Rot(A2, [128, 256], BF16, 2)
    Sout = Rot(A2, [128, 256], F32, 2)

    def copy_evac(dst_ap, dst_t, eng="act"):
        def f(ps, pt, c0=None, n=None):
            pass
        return f

    def gla_tile(h, c0, nt, own, sample, G):
        gkT, gkT_t, gqT, gqT_t, gv, gv_t = G["gkT"], G["gkT_t"], G["gqT"], G["gqT_t"], G["gv"], G["gv_t"]
        Umask = Ub[:] if sample else Uf[:]
        ti = c0 // 128
        ps, pt = ps_f()
        mm(ps[0:nt, 0:128], glrT[0:17, c0:c0 + nt], wgk[0:17, h * 128:(h + 1) * 128], True, True,
           reads=[glrT_t, ct], writes=[pt])
        e1, e1t = tmpf.get()
        act(e1[0:nt, :], ps[0:nt, 0:128], AF.Exp, reads=[pt], writes=[e1t], scale=-1.0)
        lan, lant = tmpf.get()
        act(lan[0:nt, :], e1[0:nt, :], AF.Ln, reads=[e1t], writes=[lant], bias=1.0)
        psB, ptB = ps_f()
        mm(psB[:, 0:nt], lan[0:nt, :], Umask[0:nt, 0:nt], True, True, reads=[lant, ct], writes=[ptB])
        enB, enBt = tmpf.get()
        act(enB[:, 0:nt], psB[:, 0:nt], AF.Exp, reads=[ptB], writes=[enBt], scale=1.0 / 16)
        kt, ktt = tmpf.get()
        P.op("dve", lambda e: e.tensor_mul(out=kt[:, 0:nt], in0=gkT[:, c0:c0 + nt], in1=enB[:, 0:nt]),
             reads=[gkT_t, enBt], writes=[ktt])
        bl, blt = blast.get()
        if sample:
            for j in range(4):
                act(bl[:, j:j + 1], psB[:, 4 * j + 3:4 * j + 4], AF.Exp, reads=[ptB], writes=[blt], scale=-1.0 / 16)
        else:
            act(bl[:, 0:1], psB[:, nt - 1:nt], AF.Exp, reads=[ptB], writes=[blt], scale=-1.0 / 16)
        if own:
            eB, eBt = tmpf.get()
            act(eB[:, 0:nt], psB[:, 0:nt], AF.Exp, reads=[ptB, ct], writes=[eBt], scale=-1.0 / 16, bias=epsb[:, 1:2])
            qt_, qtt = tmpb.get()
            P.op("dve", lambda e: e.tensor_mul(out=qt_[:, 0:nt], in0=gqT[:, c0:c0 + nt], in1=eB[:, 0:nt]),
                 reads=[gqT_t, eBt], writes=[qtt])
            ktb, ktbt = tmpb.get()
            P.op("pool", lambda e: e.tensor_copy(out=ktb[:, 0:nt], in_=kt[:, 0:nt]), reads=[ktt], writes=[ktbt])
            psA, ptA = ps_f()
            mm(psA[0:nt, 0:nt], ktb[:, 0:nt], qt_[:, 0:nt], True, True, reads=[ktbt, qtt], writes=[ptA])
            Am, Amt = tmpb.get()
            P.op("dve", lambda e: e.tensor_mul(out=Am[0:nt, 0:nt], in0=psA[0:nt, 0:nt], in1=Umask[0:nt, 0:nt]),
                 reads=[ptA, ct], writes=[Amt])
            if sample:
                psO, ptO = ps_f()
                for vh in range(2):
                    mm(psO[:, vh * 128:vh * 128 + nt], gv[0:nt, ti, vh * 128:(vh + 1) * 128], Am[0:nt, 0:nt], True, True,
                       reads=[gv_t[ti], Amt], writes=[ptO])
        if not sample:
            kh, kht = tmpb.get()
            P.op("pool", lambda e: e.tensor_scalar_mul(out=kh[:, 0:nt], in0=kt[:, 0:nt], scalar1=bl[:, 0:1]),
                 reads=[ktt, blt], writes=[kht])
            pT, pTt = ps_b()
            tr(pT[0:nt, 0:128], kh[:, 0:nt], identb[:], reads=[kht, ct], writes=[pTt])
            khT, khTt = tmpb.get()
            act(khT[0:nt, :], pT[0:nt, 0:128], AF.Copy, reads=[pTt], writes=[khTt])
            def stageB():
                if own:
                    psO, ptO = ps_f()
                    for vh in range(2):
                        mm(psO[:, vh * 128:vh * 128 + nt], gv[0:nt, ti, vh * 128:(vh + 1) * 128], Am[0:nt, 0:nt], True, False,
                           reads=[gv_t[ti], Amt], writes=[ptO])
                        mm(psO[:, vh * 128:vh * 128 + nt], Sb[:, h, vh * 128:(vh + 1) * 128], qt_[:, 0:nt], False, True,
                           reads=[Sb_t[h], qtt], writes=[ptO])
                psS, ptS = ps_f()
                mm(psS[:, 0:256], khT[0:nt, :], gv[0:nt, ti, :], True, True, reads=[khTt, gv_t[ti]], writes=[ptS])
                if own:
                    P.op("act", lambda e: e.activation(out=o_h[:, :, c0:c0 + nt],
                                                       in_=psO[:, 0:256].rearrange("p (a b) -> p a b", a=2)[:, :, 0:nt],
                                                       func=AF.Copy), reads=[ptO], writes=[o_t])
                P.op("dve", lambda e: e.scalar_tensor_tensor(out=Sf[:, h, :], in0=Sf[:, h, :], scalar=bl[:, 0:1],
                                                             in1=psS[:, 0:256], op0=ALU.mult, op1=ALU.add),
                     reads=[Sf_t[h], blt, ptS], writes=[Sf_t[h]])
                P.op("pool", lambda e: e.tensor_copy(out=Sb[:, h, :], in_=Sf[:, h, :]), reads=[Sf_t[h]], writes=[Sb_t[h]])
            return stageB
        else:
            P.op("pool", lambda e: e.memset(khat4[:], 0.0), writes=[khat4_t])
            seqst = []
            for j in range(4):
                s0, s0t = S0.get()
                dma("sp", s0[:], st_d[j, h], writes=[s0t])
                s0b, s0bt = S0b.get()
                P.op("pool", lambda e, s0=s0, s0b=s0b: e.tensor_copy(out=s0b[:], in_=s0[:]), reads=[s0t], writes=[s0bt])
                for vh in range(2):
                    mm(psO[:, 256 + vh * 128 + 4 * j:256 + vh * 128 + 4 * j + 4], s0b[:, vh * 128:(vh + 1) * 128],
                       qt_[:, 4 * j:4 * j + 4], True, True, reads=[s0bt, qtt], writes=[ptO])
                P.op("pool", lambda e, j=j: e.tensor_scalar_mul(out=khat4[:, j, 4 * j:4 * j + 4], in0=kt[:, 4 * j:4 * j + 4],
                                                                scalar1=bl[:, j:j + 1]),
                     reads=[ktt, blt, khat4_t], writes=[khat4_t])
                seqst.append((s0, s0t))
            P.op("act", lambda e: e.activation(out=o_h[:, :, c0:c0 + nt],
                                               in_=psO[:, 0:256].rearrange("p (a b) -> p a b", a=2)[:, :, 0:nt],
                                               func=AF.Copy), reads=[ptO], writes=[o_t])
            P.op("dve", lambda e: e.tensor_add(out=o_h[:, :, c0:c0 + nt], in0=o_h[:, :, c0:c0 + nt],
                                               in1=psO[:, 256:512].rearrange("p (a b) -> p a b", a=2)[:, :, 0:nt]),
                 reads=[ptO, o_t], writes=[o_t])
            for j in range(4):
                s0, s0t = seqst[j]
                pT, pTt = ps_b()
                tr(pT[0:16, 0:128], khat4[:, j, :], identb[:], reads=[khat4_t, ct], writes=[pTt])
                khT, khTt = tmpb.get()
                act(khT[0:16, :], pT[0:16, 0:128], AF.Copy, reads=[pTt], writes=[khTt])
                psS, ptS = ps_f()
                mm(psS[:, 0:256], khT[0:16, :], gv[0:16, ti, :], True, True, reads=[khTt, gv_t[ti]], writes=[ptS])
                so, sot = Sout.get()
                P.op("dve", lambda e, so=so, s0=s0, j=j, psS=psS: e.scalar_tensor_tensor(
                    out=so[:], in0=s0[:], scalar=bl[:, j:j + 1], in1=psS[:, 0:256], op0=ALU.mult, op1=ALU.add),
                    reads=[s0t, blt, ptS], writes=[sot])
                dma("sp", gs_o[j, h], so[:], reads=[sot])

    def gla_batch(h, b, own, G):
        gkT, gkT_t, gqT, gqT_t, gv, gv_t = G["gkT"], G["gkT_t"], G["gqT"], G["gqT_t"], G["gv"], G["gv_t"]
        c0 = b * 512
        X = {}

        def A1():
            psZ, ptZ = ps_f()
            for i in range(4):
                mm(psZ[:, i * 128:(i + 1) * 128], glrT[0:17, c0 + i * 128:c0 + (i + 1) * 128], wgk[0:17, h * 128:(h + 1) * 128],
                   True, True, reads=[glrT_t, ct], writes=[ptZ])
            e1, e1t = f512.get()
            act(e1[:], psZ[:], AF.Exp, reads=[ptZ], writes=[e1t], scale=-1.0)
            lan, lant = f512.get()
            act(lan[:], e1[:], AF.Ln, reads=[e1t], writes=[lant], bias=1.0)
            X["lan"], X["lant"] = lan, lant

        def A2():
            lan, lant = X["lan"], X["lant"]
            psB, ptB = ps_f()
            for i in range(4):
                mm(psB[:, i * 128:(i + 1) * 128], lan[:, i * 128:(i + 1) * 128], Uf[:], True, True, reads=[lant, ct], writes=[ptB])
            enB, enBt = f512.get()
            act(enB[:], psB[:], AF.Exp, reads=[ptB], writes=[enBt], scale=1.0 / 16)
            bl, blt = blast.get()
            act(bl[:, 0:4], psB[:, 127:512:128], AF.Exp, reads=[ptB], writes=[blt], scale=-1.0 / 16)
            kt, ktt = f512.get()
            P.op("dve", lambda e: e.tensor_mul(out=kt[:], in0=gkT[:, c0:c0 + 512], in1=enB[:]), reads=[gkT_t, enBt], writes=[ktt])
            kh, kht = xh.get()
            P.op("pool", lambda e: e.tensor_mul(out=kh[:], in0=kt[:].rearrange("p (a b) -> p a b", a=4),
                                                in1=bl[:, 0:4].unsqueeze(2).broadcast_to([128, 4, 128])),
                 reads=[ktt, blt], writes=[kht])
            X.update(bl=bl, blt=blt, kh=kh, kht=kht)
            if own:
                eB, eBt = f512.get()
                act(eB[:], psB[:], AF.Exp, reads=[ptB, ct], writes=[eBt], scale=-1.0 / 16, bias=epsb[:, 1:2])
                qt_, qtt = xq.get()
                P.op("dve", lambda e: e.tensor_mul(out=qt_[:], in0=gqT[:, c0:c0 + 512], in1=eB[:]), reads=[gqT_t, eBt], writes=[qtt])
                ktb, ktbt = xk.get()
                P.op("pool", lambda e: e.tensor_copy(out=ktb[:], in_=kt[:]), reads=[ktt], writes=[ktbt])
                X.update(qt_=qt_, qtt=qtt, ktb=ktb, ktbt=ktbt)

        def A3():
            kh, kht = X["kh"], X["kht"]
            pT, pTt = ps_b()
            for i in range(4):
                tr(pT[:, i * 128:(i + 1) * 128], kh[:, i, :], identb[:], reads=[kht, ct], writes=[pTt])
            khT, khTt = xhT.get()
            act(khT[:], pT[:, 0:512], AF.Copy, reads=[pTt], writes=[khTt])
            X.update(khT=khT, khTt=khTt)
            if own:
                qt_, qtt, ktb, ktbt = X["qt_"], X["qtt"], X["ktb"], X["ktbt"]
                psA, ptA = ps_f()
                for i in range(4):
                    mm(psA[:, i * 128:(i + 1) * 128], ktb[:, i * 128:(i + 1) * 128], qt_[:, i * 128:(i + 1) * 128], True, True,
                       reads=[ktbt, qtt], writes=[ptA])
                Am, Amt = xa.get()
                P.op("dve", lambda e: e.tensor_mul(out=Am[:], in0=psA[:].rearrange("p (a b) -> p a b", a=4),
                                                   in1=Uf[:].unsqueeze(1).broadcast_to([128, 4, 128])), reads=[ptA, ct], writes=[Amt])
                X.update(Am=Am, Amt=Amt)

        def mk(i):
            ti = b * 4 + i
            t0 = ti * 128

            def stageB():
                bl, blt, khT, khTt = X["bl"], X["blt"], X["khT"], X["khTt"]
                if own:
                    Am, Amt, qt_, qtt = X["Am"], X["Amt"], X["qt_"], X["qtt"]
                    psO, ptO = ps_f()
                    for vh in range(2):
                        mm(psO[:, vh * 128:(vh + 1) * 128], gv[:, ti, vh * 128:(vh + 1) * 128], Am[:, i, :], True, False,
                           reads=[gv_t[ti], Amt], writes=[ptO])
                        mm(psO[:, vh * 128:(vh + 1) * 128], Sb[:, h, vh * 128:(vh + 1) * 128], qt_[:, i * 128:(i + 1) * 128],
                           False, True, reads=[Sb_t[h], qtt], writes=[ptO])
                psS, ptS = ps_f()
                mm(psS[:, 0:256], khT[:, i * 128:(i + 1) * 128], gv[:, ti, :], True, True, reads=[khTt, gv_t[ti]], writes=[ptS])
                if own:
                    P.op("act", lambda e: e.activation(out=o_h[:, :, t0:t0 + 128],
                                                       in_=psO[:, 0:256].rearrange("p (a b) -> p a b", a=2),
                                                       func=AF.Copy), reads=[ptO], writes=[o_t])
                P.op("dve", lambda e: e.scalar_tensor_tensor(out=Sf[:, h, :], in0=Sf[:, h, :], scalar=bl[:, i:i + 1],
                                                             in1=psS[:, 0:256], op0=ALU.mult, op1=ALU.add),
                     reads=[Sf_t[h], blt, ptS], writes=[Sf_t[h]])
                P.op("pool", lambda e: e.tensor_copy(out=Sb[:, h, :], in_=Sf[:, h, :]), reads=[Sf_t[h]], writes=[Sb_t[h]])
            return stageB
        return [A1, A2, A3], [mk(i) for i in range(4)]

    def gla_group(g, own):
        parts = PARTS_OWN if own else PARTS_PRE
        N = NTOK if own else 1024
        P.op("pool", lambda e: e.memset(glrT[:, 0:N], 1.0), writes=[glrT_t])
        wb, wt = wload([(0, C_GLR, 16)])

        def ev_glr(ps, pt, c0, n):
            act(glrT[0:16, c0:c0 + n], ps[0:16, 0:n], AF.Copy, reads=[pt], writes=[glrT_t])
        proj_fm(wb, wt, 0, 16, parts, ev_glr)
        ntile = 9 if own else 8

        def proj_items(h, G):
            gkT, gkT_t, gqT, gqT_t, gv, gv_t, ggs, ggs_t = (G["gkT"], G["gkT_t"], G["gqT"], G["gqT_t"], G["gv"], G["gv_t"],
                                                            G["ggs"], G["ggs_t"])
            cols = [(0, C_GK + h * 128, 128)] + ([(128, C_GQ + h * 128, 128)] if own else [])
            wb, wt = wload(cols)

            def ev_k(ps, pt, c0, n):
                act(gkT[:, c0:c0 + n], ps[:, 0:n], AF.Copy, reads=[pt], writes=[gkT_t])

            def ev_q(ps, pt, c0, n):
                P.op("dve", lambda e: e.tensor_copy(out=gqT[:, c0:c0 + n], in_=ps[:, 0:n]), reads=[pt], writes=[gqT_t])
            for p_ in parts:
                proj_fm(wb, wt, 0, 128, [p_], ev_k)
                yield
            if own:
                for p_ in parts:
                    proj_fm(wb, wt, 128, 128, [p_], ev_q)
                    yield
            wb, wt = wload([(0, C_GV + h * 256, 256)])
            for ti in range(ntile):
                m = 128 if ti < 8 else 16

                def ev_v(ps, pt, ti=ti, m=m):
                    P.op("dve", lambda e: e.tensor_copy(out=gv[0:m, ti, :], in_=ps[0:m, 0:256]), reads=[pt], writes=[gv_t[ti]])
                proj_tm(wb, wt, 256, ti * 128, m, ev_v)
                yield
            if own:
                wb, wt = wload([(0, C_GG + h * 256, 256)])
                for vh in range(2):
                    def ev_g(ps, pt, c0, n, vh=vh):
                        act(ggs[:, vh, c0:c0 + n], ps[:, 0:n], AF.Silu, reads=[pt], writes=[ggs_t])
                    for p_ in parts:
                        proj_fm(wb, wt, vh * 128, 128, [p_], ev_g)
                        yield

        def head_norm(h, G):
            ggs, ggs_t = G["ggs"], G["ggs_t"]
            pss = [ps_f() for _ in parts]
            for vh in range(2):
                q, qt = sqb.get()
                act(q[:, 0:N], o_h[:, vh, 0:N], AF.Square, reads=[o_t], writes=[qt])
                for (c0, n), (ps, pt) in zip(parts, pss):
                    mm(ps[:, 0:n], onesb[:], q[:, c0:c0 + n], vh == 0, vh == 1, reads=[qt, ct], writes=[pt])
            for (c0, n), (ps, pt) in zip(parts, pss):
                act(sgate[:, c0:c0 + n], ps[:, 0:n], AF.Ln, reads=[pt, ct], writes=[sgate_t], scale=1.0 / 256,
                    bias=epsb[:, 0:1])
            act(sgate[:, 0:N], sgate[:, 0:N], AF.Exp, reads=[sgate_t], writes=[sgate_t], scale=-0.5)
            for vh in range(2):
                P.op("dve", lambda e, vh=vh: e.scalar_tensor_tensor(
                    out=o_h[:, vh, 0:N], in0=o_h[:, vh, 0:N], scalar=nw[:, NW_GLA + vh:NW_GLA + vh + 1],
                    in1=sgate[:, 0:N], op0=ALU.mult, op1=ALU.mult), reads=[o_t, sgate_t, ct], writes=[o_t])
                P.op("pool", lambda e, vh=vh: e.tensor_mul(out=AT[:, 2 * h + vh, 0:N], in0=o_h[:, vh, 0:N],
                                                          in1=ggs[:, vh, 0:N]),
                     reads=[o_t, ggs_t], writes=[AT_t[2 * h + vh]])

        for _ in proj_items(0, GB[0]):
            pass
        for h in range(4):
            G = GB[h % 2]
            nxt = proj_items(h + 1, GB[(h + 1) % 2]) if h < 3 else None
            (a0, b0), (a1, b1) = gla_batch(h, 0, own, G), gla_batch(h, 1, own, G)
            steps = [a0[0], a1[0], a0[1], a1[1], a0[2], a1[2]] + b0 + b1
            if own:
                steps.append(lambda h=h, G=G: gla_tile(h, 1024, 16, own, True, G))
                steps.append(lambda h=h, G=G: head_norm(h, G))
            nit = (24 if own else 10)
            per = -(-nit // len(steps))
            for st in steps:
                st()
                if nxt is not None:
                    for _ in range(per):
                        if next(nxt, "done") == "done":
                            nxt = None
                            break
            if nxt is not None:
                for _ in nxt:
                    pass

    A3 = Arena(nc, A1_mark)
    ks_s = A3.alloc([16, 1024], F32)
    vs_s = A3.alloc([16, 1024], F32)
    qs_s = A3.alloc([16, 1024], BF16)
    ks_t, vs_t, qs_t = Trk(), Trk(), Trk()
    amask = A3.alloc([128, 3 * 8 * 256], BF16)
    kb = A3.alloc([128, NU], F32)
    selb = A3.alloc([128, 256], BF16)
    selq = A3.alloc([16, 16 * 128], BF16)
    sbias = A3.alloc([128, 24], F32)
    anwb = A3.alloc([16, 1024], F32)

    def load_attn_consts():
        for dst, src in ((amask[:], amask_d), (kb[:], kb_d), (selb[:], selb_d), (selq[:], selq_d),
                         (sbias[:], sbias_d), (anwb[:], anwb_d)):
            dma("sp", dst, src, writes=[ct])
    A3_mark = A3.off
    aqT = A3.alloc([128, NTOK], BF16)
    akT = A3.alloc([128, 3072 + 16], BF16)
    avT = A3.alloc([128, 3072 + 16], BF16)
    aqT_t, akT_t, avT_t = Trk(), Trk(), Trk()
    acc = A3.alloc([128, 2, 1024], F32)
    acc_t = Trk()
    ssatt = A3.alloc([128, NTOK], F32)
    ssatt_t = Trk()
    et = Rot(A3, [128, 256], F32, 4)
    ptl = Rot(A3, [128, 256], BF16, 6)
    vbl = Rot(A3, [128, 128], BF16, 8)
    kvst = Rot(A3, [128, 1024], BF16, 3)
    kout = Rot(A3, [128, 256], F32, 3)
    attf = A3.alloc([128, 1024], F32)
    attf_t = Trk()
    A3 = Arena(nc, A3_mark)
    Kt = Rot(A3, [128, 1024], BF16, 4)
    Vt = Rot(A3, [128, 1024], BF16, 5)
    prod = Rot(A3, [128, 1024], F32, 2)
    pvb = Rot(A3, [128, 1024], BF16, 3)
    sc8 = Rot(A3, [128, 8], F32, 10)
    p8b = Rot(A3, [128, 8], BF16, 4)
    sm16 = Rot(A3, [16, 1024], F32, 2)
    sm8 = Rot(A3, [16, 8], F32, 4)
    atts_b = A3.alloc([16, 1024], BF16)
    atts_bt = Trk()

    def kv_prefix_group(g):
        for h in range(8):
            wb, wt = wload([(0, C_AK + h * 128, 128), (128, C_AV + h * 128, 128)])
            for which in range(2):
                st_, stt = kvst.get()

                def ev(ps, pt, c0, n, st_=st_, stt=stt):
                    act(st_[:, c0:c0 + n], ps[:, 0:n], AF.Copy, reads=[pt], writes=[stt])
                proj_fm(wb, wt, which * 128, 128, PARTS_PRE, ev)
                dma("sp", kvscr[which, h, :, (g - 1) * 1024:g * 1024], st_[:], reads=[stt], writes=[kvscr_t[which][h]])

    def kv_outputs():
        for which, (cb, outd, outs, sdst, sdt) in enumerate(((C_AK, k_o, ks_o, ks_s, ks_t), (C_AV, v_o, vs_o, vs_s, vs_t))):
            for cbk in range(4):
                wb, wt = wload([(0, cb + cbk * 256, 256)])
                for ti in range(9):
                    m = 128 if ti < 8 else 16
                    ko, kot = kout.get()

                    def ev(ps, pt, ko=ko, kot=kot, m=m):
                        act(ko[0:m, :], ps[0:m, 0:256], AF.Copy, reads=[pt], writes=[kot])
                    proj_tm(wb, wt, 256, ti * 128, m, ev)
                    if ti < 8:
                        dma("sp", outd[ti * 128:(ti + 1) * 128, cbk * 256:(cbk + 1) * 256], ko[:], reads=[kot])
                    else:
                        dma("sp", outs[:, cbk * 256:(cbk + 1) * 256], ko[0:16, :], reads=[kot], writes=[ksvs_t[which]])
                        P.op("pool", lambda e, ko=ko, sdst=sdst, cbk=cbk: e.tensor_copy(
                            out=sdst[:, cbk * 256:(cbk + 1) * 256], in_=ko[0:16, :]), reads=[kot], writes=[sdt])

    def attn_head(h):
        wb, wt = wload([(0, C_AQ + h * 128, 128), (128, C_AK + h * 128, 128)])
        wb2, wt2 = wload([(0, C_AV + h * 128, 128)])
        dma("sp", akT[:, 0:2048], kvscr[0, h], reads=[kvscr_t[0][h]], writes=[akT_t])
        dma("sp", avT[:, 0:2048], kvscr[1, h], reads=[kvscr_t[1][h]], writes=[avT_t])

        def ev_q(ps, pt, c0, n):
            act(aqT[:, c0:c0 + n], ps[:, 0:n], AF.Copy, reads=[pt], writes=[aqT_t])

        def ev_k(ps, pt, c0, n):
            P.op("dve", lambda e: e.tensor_copy(out=akT[:, 2048 + c0:2048 + c0 + n], in_=ps[:, 0:n]), reads=[pt], writes=[akT_t])

        def ev_v(ps, pt, c0, n):
            act(avT[:, 2048 + c0:2048 + c0 + n], ps[:, 0:n], AF.Copy, reads=[pt], writes=[avT_t])
        proj_fm(wb, wt, 0, 128, PARTS_OWN, ev_q)
        proj_fm(wb, wt, 128, 128, PARTS_OWN, ev_k)
        proj_fm(wb2, wt2, 0, 128, PARTS_OWN, ev_v)
        P.op("pool", lambda e: e.memset(acc[:], 0.0), writes=[acc_t])
        def S1(ui):
            u = UNITS[ui]
            D, nk, nq = u["D"], u["nk"], u["nq"]
            ksl = slice(u["kc0"], u["kc0"] + (nk - 1) * D + 1, D)
            qsl = slice(u["qc0"], u["qc0"] + (nq - 1) * D + 1, D)
            pT, pTt = ps_b()
            tr(pT[0:nk, 0:128], avT[:, ksl], identb[:], reads=[avT_t, ct], writes=[pTt])
            vb, vbt = vbl.get()
            act(vb[0:nk, :], pT[0:nk, 0:128], AF.Copy, reads=[pTt], writes=[vbt])
            ps, pt = ps_f()
            mm(ps[0:nk, 0:nq], akT[:, ksl], aqT[:, qsl], True, True, reads=[akT_t, aqT_t], writes=[pt])
            return dict(ui=ui, u=u, nk=nk, nq=nq, qsl=qsl, vb=vb, vbt=vbt, ps=ps, pt=pt)

        def S2(c):
            nk, nq, ui, u = c["nk"], c["nq"], c["ui"], c["u"]
            e_, e_t = et.get()
            act(e_[0:nk, 0:nq], c["ps"][0:nk, 0:nq], AF.Exp, reads=[c["pt"], ct], writes=[e_t], scale=SCALE,
                bias=kb[0:nk, ui:ui + 1])
            p_, p_t = ptl.get()
            mbase = (u["di"] * 8 + h) * 256 + u["m0"]
            P.op("pool", lambda e, p_=p_, e_=e_, nk=nk, nq=nq, mbase=mbase: e.tensor_mul(
                out=p_[0:nk, 0:nq], in0=e_[0:nk, 0:nq], in1=amask[0:nk, mbase:mbase + nq]), reads=[e_t, ct], writes=[p_t])
            c["p_"], c["p_t"] = p_, p_t

        def S3(c):
            nk, nq, qsl, p_, p_t = c["nk"], c["nq"], c["qsl"], c["p_"], c["p_t"]
            ps2, pt2 = ps_f()
            mm(ps2[:, 0:nq], c["vb"][0:nk, :], p_[0:nk, 0:nq], True, True, reads=[c["vbt"], p_t], writes=[pt2])
            mm(ps2[:, 256:256 + nq], onesb[0:nk, :], p_[0:nk, 0:nq], True, True, reads=[p_t, ct], writes=[pt2])
            P.op("dve", lambda e, ps2=ps2, qsl=qsl, nq=nq: e.tensor_add(
                out=acc[:, :, qsl], in0=acc[:, :, qsl], in1=ps2[:, 0:512].rearrange("p (a b) -> p a b", a=2)[:, :, 0:nq]),
                reads=[acc_t, pt2], writes=[acc_t])
        ctxs = {}
        for i in range((NU + 1) // 2 + 2):
            for u_ in (2 * i, 2 * i + 1):
                if u_ < NU:
                    ctxs[u_] = S1(u_)
            for u_ in (2 * i - 2, 2 * i - 1):
                if 0 <= u_ < NU:
                    S2(ctxs[u_])
            for u_ in (2 * i - 4, 2 * i - 3):
                if 0 <= u_ < NU:
                    S3(ctxs.pop(u_))
        assert not ctxs
        act(acc[:, 1, :], acc[:, 1, :], AF.Ln, reads=[acc_t], writes=[acc_t])
        act(acc[:, 1, :], acc[:, 1, :], AF.Exp, reads=[acc_t], writes=[acc_t], scale=-1.0)
        P.op("dve", lambda e: e.tensor_mul(out=attf[:], in0=acc[:, 0, :], in1=acc[:, 1, :]),
             reads=[acc_t], writes=[attf_t])
        P.op("pool", lambda e: e.tensor_copy(out=AT[:, 8 + h, 0:1024], in_=attf[:]), reads=[attf_t], writes=[AT_t[8 + h]])
        q, qt = sqb.get()
        act(q[:, 0:1024], attf[:], AF.Square, reads=[attf_t], writes=[qt])
        for (c0, n) in PARTS_PRE:
            ps, pt = ps_f()
            mm(ps[:, 0:n], onesb[:], q[:, c0:c0 + n], True, True, reads=[qt, ct], writes=[pt])
            if h == 0:
                act(ssatt[:, c0:c0 + n], ps[:, 0:n], AF.Copy, reads=[pt], writes=[ssatt_t])
            else:
                P.op("dve", lambda e, ps=ps, c0=c0, n=n: e.tensor_add(out=ssatt[:, c0:c0 + n], in0=ssatt[:, c0:c0 + n],
                                                                       in1=ps[:, 0:n]), reads=[pt, ssatt_t], writes=[ssatt_t])
        pT, pTt = ps_b()
        tr(pT[0:16, 0:128], aqT[:, 1024:1040], identb[:], reads=[aqT_t, ct], writes=[pTt])
        act(qs_s[:, h * 128:(h + 1) * 128], pT[0:16, 0:128], AF.Copy, reads=[pTt], writes=[qs_t])

    def attn_finish():
        act(ssatt[:, 0:1024], ssatt[:, 0:1024], AF.Ln, reads=[ssatt_t, ct], writes=[ssatt_t], scale=1.0 / 1024, bias=epsb[:, 0:1])
        act(ssatt[:, 0:1024], ssatt[:, 0:1024], AF.Exp, reads=[ssatt_t], writes=[ssatt_t], scale=-0.5)
        for h in range(8):
            P.op("dve", lambda e, h=h: e.scalar_tensor_tensor(
                out=AT[:, 8 + h, 0:1024], in0=AT[:, 8 + h, 0:1024], scalar=nw[:, NW_ATT + h:NW_ATT + h + 1],
                in1=ssatt[:, 0:1024], op0=ALU.mult, op1=ALU.mult), reads=[AT_t[8 + h], ssatt_t, ct], writes=[AT_t[8 + h]])

    def sample_attention():
        psN = [(psf[0], psf_t[0]), (psf[1], psf_t[1])]
        psD = (psf[2], psf_t[2])
        psQ = [(psf[3], psf_t[3]), (psf[4], psf_t[4])]
        nunit = 4 * 3 * 4
        ulist = [(j, di, D, i) for j in range(4) for di, D in enumerate((1, 4, 16)) for i in range(4)]

        def SA(k):
            j, di, D, i = ulist[k]
            Kb, Kbt = Kt.get()
            Vb, Vbt = Vt.get()
            r0 = 2048 + i - D * 128
            ncache = 128 if D > 1 else 128 - i
            for (buf, bt, cd, sd, sdt) in ((Kb, Kbt, ck_d, ks_o, ksvs_t[0]), (Vb, Vbt, cv_d, vs_o, ksvs_t[1])):
                if D == 1:
                    dma("pool", buf[0:112, :], cd[j, r0:r0 + 112, :], writes=[bt])
                    dma("pool", buf[112:ncache, :], cd[j, r0 + 112:r0 + ncache, :], writes=[bt])
                    if i > 0:
                        dma("pool", buf[ncache:128, :], sd[4 * j:4 * j + i, :], reads=[sdt], writes=[bt])
                else:
                    dma("pool", buf[:, :], cd[j, r0:r0 + 127 * D + 1:D, :], writes=[bt])
            return dict(k=k, tok=4 * j + i, di=di, Kb=Kb, Kbt=Kbt, Vb=Vb, Vbt=Vbt)

        def SB(c):
            tok, di, Kb, Kbt = c["tok"], c["di"], c["Kb"], c["Kbt"]
            for hh in range(2):
                mm(psQ[hh][0][:, :], selq[0:16, tok * 128:(tok + 1) * 128], qs_s[:, hh * 512:(hh + 1) * 512], True, True,
                   reads=[qs_t, ct], writes=[psQ[hh][1]])
            pr, prt = prod.get()
            for hh in range(2):
                P.op("dve", lambda e, pr=pr, Kb=Kb, hh=hh: e.tensor_mul(
                    out=pr[:, hh * 512:(hh + 1) * 512], in0=Kb[:, hh * 512:(hh + 1) * 512], in1=psQ[hh][0][:, :]),
                    reads=[Kbt, psQ[hh][1]], writes=[prt])
            s8, s8t = sc8.get()
            P.op("dve", lambda e, s8=s8, pr=pr: e.tensor_reduce(
                out=s8[:], in_=pr[:].rearrange("p (h e) -> p h e", h=8), axis=AX.X, op=ALU.add), reads=[prt], writes=[s8t])
            s9, s9t = sc8.get()
            P.op("dve", lambda e, s8=s8, s9=s9, di=di: e.scalar_tensor_tensor(
                out=s9[:], in0=s8[:], scalar=SCALE, in1=sbias[:, di * 8:(di + 1) * 8], op0=ALU.mult, op1=ALU.add),
                reads=[s8t, ct], writes=[s9t])
            c["s9"], c["s9t"] = s9, s9t

        def SC(c):
            Vb, Vbt, s9, s9t = c["Vb"], c["Vbt"], c["s9"], c["s9t"]
            pe_, pet = sc8.get()
            act(pe_[:], s9[:], AF.Exp, reads=[s9t], writes=[pet])
            p8, p8t = p8b.get()
            act(p8[:], pe_[:], AF.Copy, reads=[pet], writes=[p8t])
            pv, pvt = pvb.get()
            for hd in range(4):
                act(pv[:, hd * 128:(hd + 1) * 128], Vb[:, hd * 128:(hd + 1) * 128], AF.Copy, reads=[Vbt, pet], writes=[pvt],
                    scale=pe_[:, hd:hd + 1])
            P.op("pool", lambda e, pv=pv, Vb=Vb, pe_=pe_: e.tensor_mul(
                out=pv[:, 512:1024].rearrange("p (h e) -> p h e", h=4), in0=Vb[:, 512:1024].rearrange("p (h e) -> p h e", h=4),
                in1=pe_[:, 4:8].unsqueeze(2).broadcast_to([128, 4, 128])), reads=[Vbt, pet], writes=[pvt])
            c["p8"], c["p8t"], c["pv"], c["pvt"] = p8, p8t, pv, pvt

        def SD(c):
            k, tok, pv, pvt, p8, p8t = c["k"], c["tok"], c["pv"], c["pvt"], c["p8"], c["p8t"]
            for hh in range(2):
                mm(psN[hh][0][0:16, :], selb[:, tok * 16:(tok + 1) * 16], pv[:, hh * 512:(hh + 1) * 512],
                   k == 0, k == nunit - 1, reads=[pvt, ct], writes=[psN[hh][1]])
            mm(psD[0][0:16, 0:8], selb[:, tok * 16:(tok + 1) * 16], p8[:], k == 0, k == nunit - 1,
               reads=[p8t, ct], writes=[psD[1]])
        cx = {}
        for it in range(nunit + 4):
            if it < nunit:
                cx[it] = SA(it)
            if 0 <= it - 2 < nunit:
                SB(cx[it - 2])
            if 0 <= it - 3 < nunit:
                SC(cx[it - 3])
            if 0 <= it - 4 < nunit:
                SD(cx.pop(it - 4))
        t1, t1t = sm16.get()
        P.op("dve", lambda e: e.tensor_mul(out=t1[:], in0=qs_s[:], in1=ks_s[:]), reads=[qs_t, ks_t], writes=[t1t])
        s1, s1t = sm8.get()
        P.op("dve", lambda e: e.tensor_reduce(out=s1[:], in_=t1[:].rearrange("p (h e) -> p h e", h=8), axis=AX.X, op=ALU.add),
             reads=[t1t], writes=[s1t])
        p1, p1t = sm8.get()
        act(p1[:], s1[:], AF.Exp, reads=[s1t], writes=[p1t], scale=SCALE)
        den, dent = sm8.get()
        P.op("dve", lambda e: e.scalar_tensor_tensor(out=den[:], in0=p1[:], scalar=3.0, in1=psD[0][0:16, 0:8],
                                                     op0=ALU.mult, op1=ALU.add), reads=[p1t, psD[1]], writes=[dent])
        P.op("dve", lambda e: e.reciprocal(out=den[:], in_=den[:]), reads=[dent], writes=[dent])
        t2, t2t = sm16.get()
        P.op("pool", lambda e: e.tensor_mul(out=t2[:].rearrange("p (h e) -> p h e", h=8),
                                            in0=vs_s[:].rearrange("p (h e) -> p h e", h=8),
                                            in1=p1[:].unsqueeze(2).broadcast_to([16, 8, 128])), reads=[vs_t, p1t], writes=[t2t])
        t3, t3t = sm16.get()
        for hh in range(2):
            P.op("dve", lambda e, hh=hh: e.scalar_tensor_tensor(
                out=t3[:, hh * 512:(hh + 1) * 512], in0=t2[:, hh * 512:(hh + 1) * 512], scalar=3.0, in1=psN[hh][0][0:16, :],
                op0=ALU.mult, op1=ALU.add), reads=[t2t, psN[hh][1]], writes=[t3t])
        P.op("dve", lambda e: e.tensor_mul(out=t3[:].rearrange("p (h e) -> p h e", h=8),
                                           in0=t3[:].rearrange("p (h e) -> p h e", h=8),
                                           in1=den[:].unsqueeze(2).broadcast_to([16, 8, 128])), reads=[t3t, dent], writes=[t3t])
        t4, t4t = sm16.get()
        ss1, ss1t = sm8.get()
        act(t4[:], t3[:], AF.Square, reads=[t3t], writes=[t4t, ss1t], accum_out=ss1[:, 0:1])
        act(ss1[:, 0:1], ss1[:, 0:1], AF.Sqrt, reads=[ss1t, ct], writes=[ss1t], scale=1.0 / 1024, bias=epsb[0:16, 0:1])
        P.op("dve", lambda e: e.reciprocal(out=ss1[:, 0:1], in_=ss1[:, 0:1]), reads=[ss1t], writes=[ss1t])
        P.op("dve", lambda e: e.scalar_tensor_tensor(out=atts_b[:], in0=t3[:], scalar=ss1[:, 0:1], in1=anwb[:],
                                                     op0=ALU.mult, op1=ALU.mult), reads=[t3t, ss1t, ct], writes=[atts_bt])
        for h in range(8):
            pT, pTt = ps_b()
            tr(pT[:, 0:16], atts_b[:, h * 128:(h + 1) * 128], identb[0:16, 0:16], reads=[atts_bt, ct], writes=[pTt])
            act(AT[:, 8 + h, 1024:1040], pT[:, 0:16], AF.Copy, reads=[pTt], writes=[AT_t[8 + h]])

    xTv = xT.rearrange("(kc p) n -> kc p n", p=128)
    xsTv = xsT.rearrange("(kc p) n -> kc p n", p=128)
    for g in range(4):
        own = g == 3
        if DEBUG == "gla0" and not own:
            continue
        parts = PARTS_OWN if own else PARTS_PRE

        def src_chunks(kc, g=g, own=own):
            b, t = xc.get()
            dma("sp" if kc % 2 == 0 else "pool", b[:, 0:1024], xTv[kc, :, g * 1024:(g + 1) * 1024], writes=[t])
            if own:
                P.op("pool", lambda e, b=b, kc=kc: e.tensor_copy(out=b[:, 1024:1040], in_=xs_all[:, kc, :]), reads=[xs_t], writes=[t])
            return b[:, 0:(NTOK if own else 1024)], [t]
        ssq_norm(src_chunks, parts, NW_ATTN, hT, hT_t, 2048.0)
        dumpf(0, rB[:, :], [rB_t])
        dumpb(0, hT[:, 0, :], [hT_t[0]])
        dumpb(1, hT[:, 15, :], [hT_t[15]])
        gla_group(g, own)
        if g in (1, 2):
            P.barrier()
            kv_prefix_group(g)
            P.barrier()
    for h in range(4):
        dma("sp", gst_o[h], Sf[:, h, :], reads=[Sf_t[h]])
    P.barrier()
    load_attn_consts()
    kv_outputs()
    if DEBUG == "A":
        raise StopBuild()
    for h in range(8):
        attn_head(h)
    attn_finish()
    P.barrier()
    if DEBUG == "B":
        raise StopBuild()
    wo_issued = []

    def wo_issue(cb):
        b_, t_ = wblk.get()
        dma("pool", b_[:], wo_flat[:, cb * 4096:(cb + 1) * 4096], writes=[t_])
        wo_issued.append((b_, t_))
    for cb_ in range(3):
        wo_issue(cb_)
    sample_attention()
    P.barrier()
    if DEBUG == "C":
        raise StopBuild()
    if DEBUG == "F":
        for sl, c in zip(range(2, 8), (0, 1, 7, 8, 9, 15)):
            dumpb(sl, AT[:, c, :], [AT_t[c]])

    A4 = Arena(nc, P1C_START)
    x1T = A4.alloc([128, 16, NTOK], F32)
    x1T_t = [Trk() for _ in range(16)]
    x1T_end = A4.off
    xres = Rot(A4, [128, NTOK], F32, 2)
    wo = wblk
    for cb in range(8):
        if cb >= 1 and cb + 2 < 8:
            wo_issue(cb + 2)
        b, t = wo_issued[cb]
        b = b[:].rearrange("p (kc n) -> p kc n", kc=16)
        for sub in range(2):
            dc = cb * 2 + sub
            xr, xrt = xres.get()
            dma("sp", xr[:, 0:1024], xTv[dc, :, 3072:4096], writes=[xrt])
            P.op("pool", lambda e, xr=xr, dc=dc: e.tensor_copy(out=xr[:, 1024:1040], in_=xs_all[:, dc, :]), reads=[xs_t], writes=[xrt])
            for (c0, n) in PARTS_MLP:
                ps, pt = ps_f()
                for ac in range(16):
                    mm(ps[:, 0:n], b[:, ac, sub * 128:(sub + 1) * 128], AT[:, ac, c0:c0 + n], ac == 0, ac == 15,
                       reads=[t, AT_t[ac]], writes=[pt])
                P.op("dve", lambda e, ps=ps, xr=xr, dc=dc, c0=c0, n=n: e.tensor_add(
                    out=x1T[:, dc, c0:c0 + n], in0=ps[:, 0:n], in1=xr[:, c0:c0 + n]), reads=[pt, xrt], writes=[x1T_t[dc]])
    P.barrier()
    if DEBUG == "F":
        dumpf(1, x1T[:, 0, :], [x1T_t[0]])
        dumpf(2, x1T[:, 15, :], [x1T_t[15]])
        P.barrier()
    h2T, h2T_t = hT, hT_t

    def src2(kc):
        return x1T[:, kc, :], [x1T_t[kc]]
    ssq_norm(src2, PARTS_MLP, NW_FFN, h2T, h2T_t, 2048.0)
    P.barrier()
    if DEBUG == "D":
        raise StopBuild()

    A5 = Arena(nc, R_START)
    wu = Rot(A5, [128, 16, 512], BF16, 2)
    wd = Rot(A5, [128, 4, 2048], BF16, 2)
    HT = Rot(A5, [128, 4, NTOK], BF16, 1)
    assert A5.off <= P1C_START, (A5.off, P1C_START)
    A5b = Arena(nc, x1T_end)
    HT.b.append(A5b.alloc([128, 4, NTOK], BF16)); HT.t.append(Trk())
    rl = Rot(A5b, [128, 512], F32, 3)
    for g in range(16):
        ub, ut = wu.get()
        dma("pool", ub[:].rearrange("p a b -> p (a b)"), wu_flat[:, g * 8192:(g + 1) * 8192], writes=[ut])
        db, dt_ = wd.get()
        dma("pool", db[:], w_down[g * 512:(g + 1) * 512, :].rearrange("(fc p) n -> p fc n", p=128), writes=[dt_])
        hb, hbt = HT.get()
        for fc in range(4):
            for (c0, n) in PARTS_MLP:
                ps, pt = ps_f()
                for kc in range(16):
                    mm(ps[:, 0:n], ub[:, kc, fc * 128:(fc + 1) * 128], h2T[:, kc, c0:c0 + n], kc == 0, kc == 15,
                       reads=[ut, h2T_t[kc]], writes=[pt])
                r_, r_t = rl.get()
                act(r_[:, 0:n], ps[:, 0:n], AF.Relu, reads=[pt], writes=[r_t])
                P.op("pool", lambda e, hb=hb, r_=r_, fc=fc, c0=c0, n=n: e.tensor_mul(
                    out=hb[:, fc, c0:c0 + n], in0=r_[:, 0:n], in1=r_[:, 0:n]), reads=[r_t], writes=[hbt])
        for dc in range(16):
            for (c0, n) in PARTS_MLP:
                ps, pt = ps_f()
                for fc in range(4):
                    mm(ps[:, 0:n], db[:, fc, dc * 128:(dc + 1) * 128], hb[:, fc, c0:c0 + n], fc == 0, fc == 3,
                       reads=[dt_, hbt], writes=[pt])
                P.op("dve", lambda e, ps=ps, dc=dc, c0=c0, n=n: e.tensor_add(
                    out=x1T[:, dc, c0:c0 + n], in0=x1T[:, dc, c0:c0 + n], in1=ps[:, 0:n]), reads=[pt, x1T_t[dc]], writes=[x1T_t[dc]])
    P.barrier()

    if DEBUG == "E":
        raise StopBuild()
    if DEBUG == "F":
        dumpf(3, x1T[:, 0, :], [x1T_t[0]])
        dumpf(4, x1T[:, 15, :], [x1T_t[15]])
        P.barrier()
    A6 = Arena(nc, R_START)
    sqb4 = Rot(A6, [128, NTOK], BF16, 2)
    rB4 = A6.alloc([128, NTOK], F32)
    rB4_t = Trk()
    yT, yT_t = x1T, x1T_t
    ytok = Rot(A6, [128, 2048], F32, 2)
    pss = [ps_f() for _ in PARTS_OWN]
    for kc in range(16):
        q, qt = sqb4.get()
        act(q[:, 0:NTOK], x1T[:, kc, :], AF.Square, reads=[x1T_t[kc]], writes=[qt])
        for (c0, n), (ps, pt) in zip(PARTS_OWN, pss):
            mm(ps[:, 0:n], onesb[:], q[:, c0:c0 + n], kc == 0, kc == 15, reads=[qt, ct], writes=[pt])
    for (c0, n), (ps, pt) in zip(PARTS_OWN, pss):
        act(rB4[:, c0:c0 + n], ps[:, 0:n], AF.Ln, reads=[pt, ct], writes=[rB4_t], scale=1.0 / 2048, bias=epsb[:, 0:1])
    act(rB4[:, :], rB4[:, :], AF.Exp, reads=[rB4_t], writes=[rB4_t], scale=-0.5)
    for kc in range(16):
        P.op("dve", lambda e, kc=kc: e.scalar_tensor_tensor(
            out=yT[:, kc, :], in0=x1T[:, kc, :], scalar=nw[:, NW_FIN + kc:NW_FIN + kc + 1], in1=rB4[:, :],
            op0=ALU.mult, op1=ALU.mult), reads=[x1T_t[kc], rB4_t, ct], writes=[yT_t[kc]])
    for ti in range(9):
        m = 128 if ti < 8 else 16
        yb, ybt = ytok.get()
        for q4 in range(4):
            ps, pt = ps_f()
            for s in range(4):
                kc = q4 * 4 + s
                tr(ps[0:m, s * 128:(s + 1) * 128], yT[:, kc, ti * 128:ti * 128 + m], identf[:], reads=[yT_t[kc], ct], writes=[pt])
            if q4 % 2 == 0:
                act(yb[0:m, q4 * 512:(q4 + 1) * 512], ps[0:m, :], AF.Copy, reads=[pt], writes=[ybt])
            else:
                P.op("dve", lambda e, yb=yb, ps=ps, q4=q4, m=m: e.tensor_copy(out=yb[0:m, q4 * 512:(q4 + 1) * 512], in_=ps[0:m, :]),
                     reads=[pt], writes=[ybt])
        if ti < 8:
            dma("sp", y_o[ti * 128:(ti + 1) * 128, :], yb[:], reads=[ybt])
        else:
            dma("sp", ys_o[:, :], yb[0:16, :], reads=[ybt])


_CACHE = {}


def _consts():
    import ml_dtypes
    bf = ml_dtypes.bfloat16
    c = {}
    c["identf"] = np.eye(128, dtype=np.float32)
    c["identb"] = np.eye(128, dtype=np.float32).astype(bf)
    c["onesb"] = np.ones((128, 128), np.float32).astype(bf)
    s = np.arange(128)
    c["U"] = (s[:, None] <= s[None, :]).astype(np.float32)
    t = np.arange(16)
    c["Ub"] = ((t[:, None] <= t[None, :]) & (t[:, None] // 4 == t[None, :] // 4)).astype(np.float32)
    slopes = np.exp2(-8.0 * np.arange(1, 9, dtype=np.float32) / 8).astype(np.float32)
    kk = np.arange(128)[:, None]
    qq = np.arange(256)[None, :]
    dist = qq - kk
    valid = (dist >= 0) & (dist <= 128)
    am = np.zeros((128, 3, 8, 256), np.float32)
    for di, D in enumerate((1, 4, 16)):
        for h in range(8):
            am[:, di, h, :] = np.where(valid, np.exp(-slopes[h] * D * np.maximum(dist, 0).astype(np.float32)), 0.0)
    c["amask"] = am.reshape(128, -1).astype(bf)
    selb = np.zeros((128, 16, 16), np.float32)
    for tok in range(16):
        selb[:, tok, tok] = 1.0
    c["selb"] = selb.reshape(128, 256).astype(bf)
    selq = np.zeros((16, 16, 128), np.float32)
    for tok in range(16):
        selq[tok, tok, :] = 1.0
    c["selq"] = selq.reshape(16, -1).astype(bf)
    m = np.arange(128)
    sb = np.zeros((128, 3, 8), np.float32)
    for di, D in enumerate((1, 4, 16)):
        sb[:, di, :] = -slopes[None, :] * (D * (128 - m))[:, None]
    c["sbias"] = sb.reshape(128, 24)
    c["smask"] = (t[:, None] // 4 == np.arange(4)[None, :]).astype(np.float32)
    return c


def _kb(ci):
    s = ci * 1024
    kb = np.zeros((128, NU), np.float32)
    for ui, u in enumerate(UNITS):
        kk = np.arange(128)
        uloc = (128 * u["j"] + kk) * u["D"] + u["r"]
        tglob = s - 2048 + uloc
        kb[:, ui] = np.where(tglob >= 0, 0.0, -30000.0)
    return kb


def kernel(x_prompt, x_sample, cache_k_win, cache_v_win, state_gla, attn_norm_w, w_in, w_gk_up, b_gk,
           gla_norm_w, att_out_norm_w, w_out, ffn_norm_w, w_up, w_down, final_norm_w):
    f = lambda a: np.ascontiguousarray(np.asarray(a, dtype=np.float32))
    x_prompt, x_sample = f(x_prompt), f(x_sample)
    if "nc" not in _CACHE:
        _CACHE["nc"] = build()
    nc = _CACHE["nc"]
    consts = _consts()
    col = lambda w, n: f(w).reshape(n, 128).T
    nwm = np.concatenate([col(attn_norm_w[0], 16), col(ffn_norm_w[0], 16), col(final_norm_w, 16),
                          col(gla_norm_w[0], 2), col(att_out_norm_w[0], 8)], axis=1)
    def tile_cols(w, c0, n):
        return w[:, c0:c0 + n].reshape(16, 128, n).transpose(1, 0, 2).reshape(128, 16 * n)
    w_in0, w_out0, w_up0 = f(w_in[0]), f(w_out[0]), f(w_up[0])
    wflat = np.empty((128, WTOT[0]), np.float32)
    for key, o in WOFF.items():
        blk = np.concatenate([w_in0[:, sc:sc + n] for (sc, n) in key], axis=1)
        W = blk.shape[1]
        wflat[:, o:o + 16 * W] = tile_cols(blk, 0, W)
    wo_flat = np.concatenate([tile_cols(w_out0, cb * 256, 256) for cb in range(8)], axis=1)
    wu_flat = np.concatenate([tile_cols(w_up0, g * 512, 512) for g in range(16)], axis=1)
    shared = dict(wflat=wflat, wo_flat=np.ascontiguousarray(wo_flat), wu_flat=np.ascontiguousarray(wu_flat), w_down=f(w_down[0]),
                  wgk=np.concatenate([f(w_gk_up[0]), f(b_gk[0])[None, :]], axis=0), nw=np.ascontiguousarray(nwm),
                  anwb=np.ascontiguousarray(np.broadcast_to(f(att_out_norm_w[0])[None, :], (16, 1024))), **consts)
    in_maps = []
    for c in range(8):
        b, ci = c // 4, c % 4
        s = ci * 1024
        xt = np.zeros((2048, 4096), np.float32)
        lo = max(0, s - 3072)
        seg = x_prompt[b, lo:s + 1024, :]
        xt[:, 4096 - seg.shape[0]:] = seg.T
        m = dict(shared)
        m["xT"] = xt
        m["xsT"] = np.ascontiguousarray(x_sample[4 * c:4 * c + 4].reshape(16, 2048).T)
        m["ck"] = np.ascontiguousarray(f(cache_k_win[0, 4 * c:4 * c + 4]).reshape(4, 2048, 1024))
        m["cv"] = np.ascontiguousarray(f(cache_v_win[0, 4 * c:4 * c + 4]).reshape(4, 2048, 1024))
        m["st"] = f(state_gla[0, 4 * c:4 * c + 4])
        m["kb"] = _kb(ci)
        in_maps.append(m)
    res = run_bass_kernel_spmd(nc, in_maps, core_ids=list(range(8)))
    R = res.results
    if DEBUG == "gla0":
        return R
    if DEBUG:
        z16 = np.zeros((16, 2048), np.float32)
        for r in R:
            r.setdefault("y_own", np.zeros((1024, 2048), np.float32))
            r.setdefault("y_s", z16)
    y_prompt = np.zeros((2, 4096, 2048), np.float32)
    y_sample = np.zeros((32, 4, 2048), np.float32)
    k_win = np.zeros((1, 2, 2048, 8, 128), np.float32)
    v_win = np.zeros((1, 2, 2048, 8, 128), np.float32)
    gla_p = np.zeros((1, 2, 4, 128, 256), np.float32)
    k_new = np.zeros((1, 32, 4, 8, 128), np.float32)
    v_new = np.zeros((1, 32, 4, 8, 128), np.float32)
    gla_s = np.zeros((1, 32, 4, 128, 256), np.float32)
    for c in range(8):
        b, ci = c // 4, c % 4
        s = ci * 1024
        r = R[c]
        y_prompt[b, s:s + 1024] = r["y_own"]
        y_sample[4 * c:4 * c + 4] = r["y_s"].reshape(4, 4, 2048)
        if ci >= 2:
            k_win[0, b, s - 2048:s - 1024] = r["k_own"].reshape(1024, 8, 128)
            v_win[0, b, s - 2048:s - 1024] = r["v_own"].reshape(1024, 8, 128)
        if ci == 3:
            gla_p[0, b] = r["gla_st"]
        k_new[0, 4 * c:4 * c + 4] = r["k_s"].reshape(4, 4, 8, 128)
        v_new[0, 4 * c:4 * c + 4] = r["v_s"].reshape(4, 4, 8, 128)
        gla_s[0, 4 * c:4 * c + 4] = r["gla_s"]
    return (y_prompt, y_sample, k_win, v_win, gla_p, k_new, v_new, gla_s)
```

```python
import numpy as np
import concourse.bass as bass
import concourse.mybir as mybir
from concourse.bass_utils import run_bass_kernel_spmd

F32, BF16 = mybir.dt.float32, mybir.dt.bfloat16
AF = mybir.ActivationFunctionType
ALU = mybir.AluOpType
AX = mybir.AxisListType

NQ = 24


class Trk:
    __slots__ = ("w", "r")

    def __init__(self):
        self.w = None
        self.r = []


class Op:
    __slots__ = ("eng", "fn", "deps", "dma", "slot", "inc", "val")

    def __init__(self, eng, fn, deps, dma):
        self.eng, self.fn, self.deps, self.dma = eng, fn, deps, dma
        self.slot, self.inc, self.val = None, False, 0


class Prog:
    ENGS = ["pe", "act", "dve", "pool", "sp"]

    def __init__(self):
        self.ops = []
        self.slot_last = {"sp": [None] * NQ, "pool": [None] * NQ}
        self.dma_cnt = {"sp": 0, "pool": 0}
        self.last = {e: None for e in self.ENGS}

    def op(self, eng, fn, reads=(), writes=(), dma=False):
        deps = set()
        for t in reads:
            if t.w is not None:
                deps.add(t.w)
        for t in writes:
            if t.w is not None:
                deps.add(t.w)
            last = {}
            for r in t.r:
                ro = self.ops[r]
                if ro.dma:
                    deps.add(r)
                else:
                    last[ro.eng] = r
            deps.update(last.values())
        if eng == "pe":
            deps = {d for d in deps if self.ops[d].eng != "pe"}
        i = len(self.ops)
        o = Op(eng, fn, deps, dma)
        if dma:
            k = self.dma_cnt[eng] % NQ
            self.dma_cnt[eng] += 1
            if self.slot_last[eng][k] is not None:
                deps.add(self.slot_last[eng][k])
            self.slot_last[eng][k] = i
            o.slot = k
        self.ops.append(o)
        for t in reads:
            t.r.append(i)
        for t in writes:
            t.w = i
            t.r = []
        self.last[eng] = i
        return i

    def barrier(self):
        allp = set(x for x in self.last.values() if x is not None)
        for q in self.slot_last.values():
            allp.update(x for x in q if x is not None)
        for e in self.ENGS:
            self.ops.append(Op(e, None, set(allp), False))

    def emit(self, nc):
        ops = self.ops
        for o in ops:
            for d in o.deps:
                ops[d].inc = True
        cnt = {e: 0 for e in self.ENGS}
        slotcnt = {"sp": [0] * NQ, "pool": [0] * NQ}
        for o in ops:
            if o.fn is None:
                continue
            if o.dma:
                slotcnt[o.eng][o.slot] += 16
                o.val = slotcnt[o.eng][o.slot]
            elif o.inc:
                cnt[o.eng] += 1
                o.val = cnt[o.eng]
        import contextlib

        with contextlib.ExitStack() as es:
            csem = {e: es.enter_context(nc.semaphore("c_" + e)) for e in self.ENGS}
            dsem = {q: [es.enter_context(nc.semaphore(f"d_{q}{k}")) for k in range(NQ)] for q in ("sp", "pool")}
            block = es.enter_context(nc.Block())

            def run(ename, eng):
                waited = {}
                for o in ops:
                    if o.eng != ename:
                        continue
                    for d in sorted(o.deps):
                        do = ops[d]
                        if do.dma:
                            sem, key = dsem[do.eng][do.slot], ("d", do.eng, do.slot)
                        else:
                            sem, key = csem[do.eng], ("c", do.eng)
                        if waited.get(key, 0) < do.val:
                            eng.wait_ge(sem, do.val)
                            waited[key] = do.val
                    if o.fn is None:
                        continue
                    ins = o.fn(eng)
                    if o.dma:
                        ins.then_inc(dsem[ename][o.slot], 16)
                    elif o.inc:
                        ins.then_inc(csem[ename], 1)

            @block.tensor
            def _(e):
                run("pe", e)

            @block.scalar
            def _(e):
                run("act", e)

            @block.vector
            def _(e):
                run("dve", e)

            @block.gpsimd
            def _(e):
                run("pool", e)

            @block.sync
            def _(e):
                run("sp", e)


class Arena:
    def __init__(self, nc, base=0):
        self.nc, self.off, self.n = nc, base, 0

    def alloc(self, shape, dt):
        per = int(np.prod(shape[1:])) * (4 if dt == F32 else 2)
        per = (per + 31) // 32 * 32
        h = self.nc.alloc_sbuf_tensor_at(f"sb{self.n}", list(shape), dt, offset=self.off)
        self.n += 1
        self.off += per
        assert self.off <= 229376, self.off
        return h


NTOK = 1040
PARTS_OWN = [(0, 512), (512, 512), (1024, 16)]
PARTS_PRE = [(0, 512), (512, 512)]
PARTS_MLP = [(0, 347), (347, 347), (694, 346)]
C_GQ, C_GK, C_GV, C_GG, C_GLR, C_AQ, C_AK, C_AV = 0, 512, 1024, 2048, 3072, 3088, 4112, 5136
SCALE = 128 ** -0.5
EPS = 1e-6


def attn_units():
    units = []
    for di, D in enumerate((1, 4, 16)):
        lq0, lq1 = 2048 // D, 3072 // D
        for r in range(D):
            j = 0
            while 128 * j < lq1:
                if 128 * (j + 1) > lq0 - 128:
                    nk = min(128, lq1 - 128 * j)
                    q0, q1 = max(128 * j, lq0), min(128 * j + 256, lq1)
                    if q1 > q0:
                        units.append(dict(di=di, D=D, r=r, j=j, nk=nk, kc0=128 * j * D + r, q0=q0, nq=q1 - q0,
                                          qc0=q0 * D + r - 2048, m0=q0 - 128 * j))
                j += 1
    return units


UNITS = attn_units()
NU = len(UNITS)


DEBUG = None


class StopBuild(Exception):
    pass


WLIST = None
WLOG = []
WOFF = {}
WTOT = [0]


def build():
    global WLIST
    WLIST = None
    del WLOG[:]
    WOFF.clear()
    WTOT[0] = 0
    try:
        _build(bass.Bass("TRN2", target_bir_lowering=False), Prog())
    except StopBuild:
        pass
    WLIST = list(WLOG)
    o = 0
    for key in WOFF:
        WOFF[key] = o
        o += 16 * sum(n for (_, n) in key)
    WTOT[0] = o
    nc = bass.Bass("TRN2", target_bir_lowering=False)
    P = Prog()
    try:
        _build(nc, P)
    except StopBuild:
        pass
    P.barrier()
    P.emit(nc)
    return nc


def _build(nc, P):

    def din(name, shape, dt=F32):
        return nc.dram_tensor(name, list(shape), dt, kind="ExternalInput").ap()

    def dout(name, shape):
        return nc.dram_tensor(name, list(shape), F32, kind="ExternalOutput").ap()

    xT = din("xT", [2048, 4096])
    xsT = din("xsT", [2048, 16])
    wflat = din("wflat", [128, max(WTOT[0], 16)])
    wo_flat = din("wo_flat", [128, 8 * 4096])
    wu_flat = din("wu_flat", [128, 16 * 8192])
    w_down = din("w_down", [8192, 2048])
    wgk_d = din("wgk", [17, 512])
    nw_d = din("nw", [128, 58])
    anwb_d = din("anwb", [16, 1024])
    ck_d = din("ck", [4, 2048, 1024])
    cv_d = din("cv", [4, 2048, 1024])
    st_d = din("st", [4, 4, 128, 256])
    identf_d = din("identf", [128, 128])
    identb_d = din("identb", [128, 128], BF16)
    onesb_d = din("onesb", [128, 128], BF16)
    U_d = din("U", [128, 128])
    Ub_d = din("Ub", [16, 16])
    amask_d = din("amask", [128, 3 * 8 * 256], BF16)
    kb_d = din("kb", [128, NU])
    selb_d = din("selb", [128, 256], BF16)
    selq_d = din("selq", [16, 16 * 128], BF16)
    sbias_d = din("sbias", [128, 24])
    sm_d = din("smask", [16, 4])

    y_o = dout("y_own", [1024, 2048])
    ys_o = dout("y_s", [16, 2048])
    k_o = dout("k_own", [1024, 1024])
    v_o = dout("v_own", [1024, 1024])
    gst_o = dout("gla_st", [4, 128, 256])
    ks_o = dout("k_s", [16, 1024])
    vs_o = dout("v_s", [16, 1024])
    gs_o = dout("gla_s", [4, 4, 128, 256])
    if DEBUG:
        dbgf = nc.dram_tensor("dbgf", [128, 8, NTOK], F32, kind="ExternalOutput").ap()
        dbgb = nc.dram_tensor("dbgb", [128, 8, NTOK], BF16, kind="ExternalOutput").ap()

    def dumpf(slot, ap, trks, p=128, n=NTOK):
        if DEBUG:
            P.op("sp", lambda e: e.dma_start(out=dbgf[0:p, slot, 0:n], in_=ap), reads=trks, dma=True)

    def dumpb(slot, ap, trks, p=128, n=NTOK):
        if DEBUG:
            P.op("sp", lambda e: e.dma_start(out=dbgb[0:p, slot, 0:n], in_=ap), reads=trks, dma=True)
    kvscr = nc.dram_tensor("kvscr", [2, 8, 128, 2048], BF16).ap()
    kvscr_t = [[Trk() for _ in range(8)] for _ in range(2)]
    ksvs_t = [Trk(), Trk()]

    A0 = Arena(nc, 16640)
    identf, identb, onesb, Uf = (A0.alloc([128, 128], F32), A0.alloc([128, 128], BF16),
                                 A0.alloc([128, 128], BF16), A0.alloc([128, 128], F32))
    nw = A0.alloc([128, 58], F32)
    Sf = A0.alloc([128, 4, 256], F32)
    Sb = A0.alloc([128, 4, 256], BF16)
    ct = Trk()
    Sf_t = [Trk() for _ in range(4)]
    Sb_t = [Trk() for _ in range(4)]

    def dma(q, out, in_, reads=(), writes=()):
        return P.op(q, lambda e: e.dma_start(out=out, in_=in_), reads=reads, writes=writes, dma=True)

    epsb = A0.alloc([128, 2], F32)
    xs_all = A0.alloc([128, 16, 16], F32)
    xs_t = Trk()
    for dst, src in ((identf, identf_d), (identb, identb_d), (onesb, onesb_d), (Uf, U_d), (nw, nw_d)):
        dma("sp", dst[:], src, writes=[ct])
    dma("sp", xs_all[:], xsT.rearrange("(kc p) n -> p kc n", p=128), writes=[xs_t])
    P.op("pool", lambda e: e.memset(Sf[:], 0.0), writes=Sf_t)
    P.op("pool", lambda e: e.memset(Sb[:], 0.0), writes=Sb_t)
    NW_ATTN, NW_FFN, NW_FIN, NW_GLA, NW_ATT = 0, 16, 32, 48, 50

    psf = [nc.alloc_psum_tensor(f"psf{i}", [128, 512], F32) for i in range(6)]
    psf_t = [Trk() for _ in range(6)]
    psb = [nc.alloc_psum_tensor(f"psb{i}", [128, 1024], BF16) for i in range(2)]
    psb_t = [Trk() for _ in range(2)]
    rr = {"f": 0, "b": 0}

    def ps_f():
        i = rr["f"] % 6
        rr["f"] += 1
        return psf[i], psf_t[i]

    def ps_b():
        i = rr["b"] % 2
        rr["b"] += 1
        return psb[i], psb_t[i]

    def mm(out, lhsT, rhs, start, stop, reads, writes):
        P.op("pe", lambda e: e.matmul(out, lhsT=lhsT, rhs=rhs, start=start, stop=stop), reads=reads, writes=writes)

    def tr(out, in_, ident, reads, writes):
        P.op("pe", lambda e: e.transpose(out, in_, ident), reads=reads, writes=writes)

    def act(out, in_, func, reads, writes, **kw):
        P.op("act", lambda e: e.activation(out=out, in_=in_, func=func, **kw), reads=reads, writes=writes)

    class Rot:
        def __init__(self, arena, shape, dt, n):
            self.b = [arena.alloc(shape, dt) for _ in range(n)]
            self.t = [Trk() for _ in range(n)]
            self.i = 0

        def get(self):
            k = self.i % len(self.b)
            self.i += 1
            return self.b[k], self.t[k]

    A1 = Arena(nc, A0.off)
    hT = A1.alloc([128, 16, NTOK], BF16)
    hT_t = [Trk() for _ in range(16)]
    R_START = A1.off
    xc = Rot(A1, [128, NTOK], F32, 4)
    sqb = Rot(A1, [128, NTOK], BF16, 2)
    rB = A1.alloc([128, NTOK], F32)
    rB_t = Trk()
    wblk = Rot(A1, [128, 4096], BF16, 3)
    AT = A1.alloc([128, 16, NTOK], BF16)
    AT_t = [Trk() for _ in range(16)]
    P1C_START = A1.off
    glrT = A1.alloc([32, NTOK], F32)
    glrT_t = Trk()
    wgk = A1.alloc([32, 512], F32)
    Ub = A1.alloc([16, 16], F32)
    smask = A1.alloc([16, 4], F32)
    for dst, src in ((wgk[0:17, :], wgk_d), (Ub[:], Ub_d), (smask[:], sm_d)):
        dma("sp", dst, src, writes=[ct])
    A1_mark = A1.off

    def ssq_norm(src_chunks, parts, nw_col, dst, dst_t, div):
        N = parts[-1][0] + parts[-1][1]
        pss = [ps_f() for _ in parts]
        for kc in range(16):
            s_ap, s_t = src_chunks(kc)
            q, qt = sqb.get()
            act(q[:, 0:N], s_ap, AF.Square, reads=s_t, writes=[qt])
            P.op("dve", lambda e, kc=kc, s_ap=s_ap: e.tensor_scalar_mul(
                out=dst[:, kc, 0:N], in0=s_ap, scalar1=nw[:, nw_col + kc:nw_col + kc + 1]),
                reads=list(s_t) + [ct], writes=[dst_t[kc]])
            for (c0, n), (ps, pt) in zip(parts, pss):
                mm(ps[:, 0:n], onesb[:], q[:, c0:c0 + n], kc == 0, kc == 15, reads=[qt, ct], writes=[pt])
        for (c0, n), (ps, pt) in zip(parts, pss):
            act(rB[:, c0:c0 + n], ps[:, 0:n], AF.Ln, reads=[pt, ct], writes=[rB_t], scale=1.0 / div, bias=epsb[:, 0:1])
        act(rB[:, 0:N], rB[:, 0:N], AF.Exp, reads=[rB_t], writes=[rB_t], scale=-0.5)
        for kc in range(16):
            eng = "dve" if kc % 2 == 0 else "pool"
            P.op(eng, lambda e, kc=kc: e.tensor_mul(out=dst[:, kc, 0:N], in0=dst[:, kc, 0:N], in1=rB[:, 0:N]),
                 reads=[dst_t[kc], rB_t], writes=[dst_t[kc]])

    P.op("pool", lambda e: e.memset(epsb[:, 0:1], EPS), writes=[ct])
    P.op("pool", lambda e: e.memset(epsb[:, 1:2], float(np.log(SCALE))), writes=[ct])
    P.barrier()

    wstate = {"i": 0, "issued": []}

    def _wissue(col_specs):
        b, t = wblk.get()
        W = sum(n for (_, _, n) in col_specs)
        off = 0
        for (dc, sc, n) in col_specs:
            assert dc == off
            off += n
        key = tuple((sc, n) for (_, sc, n) in col_specs)
        view = b[:, 0:16 * W].rearrange("p (kc n) -> p kc n", kc=16)
        if WLIST is None:
            WOFF.setdefault(key, None)
            src = wflat[:, 0:16]
            dma("pool", b[:, 0:16], src, writes=[t])
        else:
            o = WOFF[key]
            dma("pool", b[:, 0:16 * W], wflat[:, o:o + 16 * W], writes=[t])
        return view, t

    def wload(col_specs, src=None, rows=None):
        i = wstate["i"]
        wstate["i"] += 1
        if WLIST is None:
            WLOG.append(col_specs)
            return _wissue(col_specs)
        assert WLIST[i] == col_specs
        while len(wstate["issued"]) <= min(i + 1, len(WLIST) - 1):
            wstate["issued"].append(_wissue(WLIST[len(wstate["issued"])]))
        return wstate["issued"][i]

    def proj_fm(wb, wt, wc, m, parts, evac):
        for (c0, n) in parts:
            ps, pt = ps_f()
            for kc in range(16):
                mm(ps[0:m, 0:n], wb[:, kc, wc:wc + m], hT[:, kc, c0:c0 + n], kc == 0, kc == 15,
                   reads=[wt, hT_t[kc]], writes=[pt])
            evac(ps, pt, c0, n)

    def proj_tm(wb, wt, ncols, t0, m, evac):
        ps, pt = ps_f()
        for kc in range(16):
            mm(ps[0:m, 0:ncols], hT[:, kc, t0:t0 + m], wb[:, kc, 0:ncols], kc == 0, kc == 15,
               reads=[wt, hT_t[kc]], writes=[pt])
        evac(ps, pt)

    A2 = Arena(nc, A1_mark)
    GB = []
    for _ in range(2):
        GB.append(dict(gkT=A2.alloc([128, NTOK], F32), gqT=A2.alloc([128, NTOK], F32), ggs=A2.alloc([128, 2, NTOK], BF16),
                       gv=A2.alloc([128, 9, 256], BF16), gkT_t=Trk(), gqT_t=Trk(), ggs_t=Trk(), gv_t=[Trk() for _ in range(9)]))
    o_h = A2.alloc([128, 2, NTOK], F32)
    o_t = Trk()
    tmpf = Rot(A2, [128, 128], F32, 6)
    f512 = Rot(A2, [128, 512], F32, 5)
    xq = Rot(A2, [128, 512], BF16, 2)
    xk = Rot(A2, [128, 512], BF16, 2)
    xa = Rot(A2, [128, 4, 128], BF16, 2)
    xh = Rot(A2, [128, 4, 128], BF16, 2)
    xhT = Rot(A2, [128, 512], BF16, 2)
    tmpb = Rot(A2, [128, 128], BF16, 8)
    khat4 = A2.alloc([128, 4, 16], BF16)
    khat4_t = Trk()
    blast = Rot(A2, [128, 4], F32, 6)
    sgate, sgate_t = rB, rB_t
    S0 = Rot(A2, [128, 256], F32, 4)
    S0b = Rot(A2, [128, 256], BF16, 2)
    Sout = Rot(A2, [128, 256], F32, 2)

    def copy_evac(dst_ap, dst_t, eng="act"):
        def f(ps, pt, c0=None, n=None):
            pass
        return f

    def gla_tile(h, c0, nt, own, sample, G):
        gkT, gkT_t, gqT, gqT_t, gv, gv_t = G["gkT"], G["gkT_t"], G["gqT"], G["gqT_t"], G["gv"], G["gv_t"]
        Umask = Ub[:] if sample else Uf[:]
        ti = c0 // 128
        ps, pt = ps_f()
        mm(ps[0:nt, 0:128], glrT[0:17, c0:c0 + nt], wgk[0:17, h * 128:(h + 1) * 128], True, True,
           reads=[glrT_t, ct], writes=[pt])
        e1, e1t = tmpf.get()
        act(e1[0:nt, :], ps[0:nt, 0:128], AF.Exp, reads=[pt], writes=[e1t], scale=-1.0)
        lan, lant = tmpf.get()
        act(lan[0:nt, :], e1[0:nt, :], AF.Ln, reads=[e1t], writes=[lant], bias=1.0)
        psB, ptB = ps_f()
        mm(psB[:, 0:nt], lan[0:nt, :], Umask[0:nt, 0:nt], True, True, reads=[lant, ct], writes=[ptB])
        enB, enBt = tmpf.get()
        act(enB[:, 0:nt], psB[:, 0:nt], AF.Exp, reads=[ptB], writes=[enBt], scale=1.0 / 16)
        kt, ktt = tmpf.get()
        P.op("dve", lambda e: e.tensor_mul(out=kt[:, 0:nt], in0=gkT[:, c0:c0 + nt], in1=enB[:, 0:nt]),
             reads=[gkT_t, enBt], writes=[ktt])
        bl, blt = blast.get()
        if sample:
            for j in range(4):
                act(bl[:, j:j + 1], psB[:, 4 * j + 3:4 * j + 4], AF.Exp, reads=[ptB], writes=[blt], scale=-1.0 / 16)
        else:
            act(bl[:, 0:1], psB[:, nt - 1:nt], AF.Exp, reads=[ptB], writes=[blt], scale=-1.0 / 16)
        if own:
            eB, eBt = tmpf.get()
            act(eB[:, 0:nt], psB[:, 0:nt], AF.Exp, reads=[ptB, ct], writes=[eBt], scale=-1.0 / 16, bias=epsb[:, 1:2])
            qt_, qtt = tmpb.get()
            P.op("dve", lambda e: e.tensor_mul(out=qt_[:, 0:nt], in0=gqT[:, c0:c0 + nt], in1=eB[:, 0:nt]),
                 reads=[gqT_t, eBt], writes=[qtt])
            ktb, ktbt = tmpb.get()
            P.op("pool", lambda e: e.tensor_copy(out=ktb[:, 0:nt], in_=kt[:, 0:nt]), reads=[ktt], writes=[ktbt])
            psA, ptA = ps_f()
            mm(psA[0:nt, 0:nt], ktb[:, 0:nt], qt_[:, 0:nt], True, True, reads=[ktbt, qtt], writes=[ptA])
            Am, Amt = tmpb.get()
            P.op("dve", lambda e: e.tensor_mul(out=Am[0:nt, 0:nt], in0=psA[0:nt, 0:nt], in1=Umask[0:nt, 0:nt]),
                 reads=[ptA, ct], writes=[Amt])
            if sample:
                psO, ptO = ps_f()
                for vh in range(2):
                    mm(psO[:, vh * 128:vh * 128 + nt], gv[0:nt, ti, vh * 128:(vh + 1) * 128], Am[0:nt, 0:nt], True, True,
                       reads=[gv_t[ti], Amt], writes=[ptO])
        if not sample:
            kh, kht = tmpb.get()
            P.op("pool", lambda e: e.tensor_scalar_mul(out=kh[:, 0:nt], in0=kt[:, 0:nt], scalar1=bl[:, 0:1]),
                 reads=[ktt, blt], writes=[kht])
            pT, pTt = ps_b()
            tr(pT[0:nt, 0:128], kh[:, 0:nt], identb[:], reads=[kht, ct], writes=[pTt])
            khT, khTt = tmpb.get()
            act(khT[0:nt, :], pT[0:nt, 0:128], AF.Copy, reads=[pTt], writes=[khTt])
            def stageB():
                if own:
                    psO, ptO = ps_f()
                    for vh in range(2):
                        mm(psO[:, vh * 128:vh * 128 + nt], gv[0:nt, ti, vh * 128:(vh + 1) * 128], Am[0:nt, 0:nt], True, False,
                           reads=[gv_t[ti], Amt], writes=[ptO])
                        mm(psO[:, vh * 128:vh * 128 + nt], Sb[:, h, vh * 128:(vh + 1) * 128], qt_[:, 0:nt], False, True,
                           reads=[Sb_t[h], qtt], writes=[ptO])
                psS, ptS = ps_f()
                mm(psS[:, 0:256], khT[0:nt, :], gv[0:nt, ti, :], True, True, reads=[khTt, gv_t[ti]], writes=[ptS])
                if own:
                    P.op("act", lambda e: e.activation(out=o_h[:, :, c0:c0 + nt],
                                                       in_=psO[:, 0:256].rearrange("p (a b) -> p a b", a=2)[:, :, 0:nt],
                                                       func=AF.Copy), reads=[ptO], writes=[o_t])
                P.op("dve", lambda e: e.scalar_tensor_tensor(out=Sf[:, h, :], in0=Sf[:, h, :], scalar=bl[:, 0:1],
                                                             in1=psS[:, 0:256], op0=ALU.mult, op1=ALU.add),
                     reads=[Sf_t[h], blt, ptS], writes=[Sf_t[h]])
                P.op("pool", lambda e: e.tensor_copy(out=Sb[:, h, :], in_=Sf[:, h, :]), reads=[Sf_t[h]], writes=[Sb_t[h]])
            return stageB
        else:
            P.op("pool", lambda e: e.memset(khat4[:], 0.0), writes=[khat4_t])
            seqst = []
            for j in range(4):
                s0, s0t = S0.get()
                dma("sp", s0[:], st_d[j, h], writes=[s0t])
                s0b, s0bt = S0b.get()
                P.op("pool", lambda e, s0=s0, s0b=s0b: e.tensor_copy(out=s0b[:], in_=s0[:]), reads=[s0t], writes=[s0bt])
                for vh in range(2):
                    mm(psO[:, 256 + vh * 128 + 4 * j:256 + vh * 128 + 4 * j + 4], s0b[:, vh * 128:(vh + 1) * 128],
                       qt_[:, 4 * j:4 * j + 4], True, True, reads=[s0bt, qtt], writes=[ptO])
                P.op("pool", lambda e, j=j: e.tensor_scalar_mul(out=khat4[:, j, 4 * j:4 * j + 4], in0=kt[:, 4 * j:4 * j + 4],
                                                                scalar1=bl[:, j:j + 1]),
                     reads=[ktt, blt, khat4_t], writes=[khat4_t])
                seqst.append((s0, s0t))
            P.op("act", lambda e: e.activation(out=o_h[:, :, c0:c0 + nt],
                                               in_=psO[:, 0:256].rearrange("p (a b) -> p a b", a=2)[:, :, 0:nt],
                                               func=AF.Copy), reads=[ptO], writes=[o_t])
            P.op("dve", lambda e: e.tensor_add(out=o_h[:, :, c0:c0 + nt], in0=o_h[:, :, c0:c0 + nt],
                                               in1=psO[:, 256:512].rearrange("p (a b) -> p a b", a=2)[:, :, 0:nt]),
                 reads=[ptO, o_t], writes=[o_t])
            for j in range(4):
                s0, s0t = seqst[j]
                pT, pTt = ps_b()
                tr(pT[0:16, 0:128], khat4[:, j, :], identb[:], reads=[khat4_t, ct], writes=[pTt])
                khT, khTt = tmpb.get()
                act(khT[0:16, :], pT[0:16, 0:128], AF.Copy, reads=[pTt], writes=[khTt])
                psS, ptS = ps_f()
                mm(psS[:, 0:256], khT[0:16, :], gv[0:16, ti, :], True, True, reads=[khTt, gv_t[ti]], writes=[ptS])
                so, sot = Sout.get()
                P.op("dve", lambda e, so=so, s0=s0, j=j, psS=psS: e.scalar_tensor_tensor(
                    out=so[:], in0=s0[:], scalar=bl[:, j:j + 1], in1=psS[:, 0:256], op0=ALU.mult, op1=ALU.add),
                    reads=[s0t, blt, ptS], writes=[sot])
                dma("sp", gs_o[j, h], so[:], reads=[sot])

    def gla_batch(h, b, own, G):
        gkT, gkT_t, gqT, gqT_t, gv, gv_t = G["gkT"], G["gkT_t"], G["gqT"], G["gqT_t"], G["gv"], G["gv_t"]
        c0 = b * 512
        X = {}

        def A1():
            psZ, ptZ = ps_f()
            for i in range(4):
                mm(psZ[:, i * 128:(i + 1) * 128], glrT[0:17, c0 + i * 128:c0 + (i + 1) * 128], wgk[0:17, h * 128:(h + 1) * 128],
                   True, True, reads=[glrT_t, ct], writes=[ptZ])
            e1, e1t = f512.get()
            act(e1[:], psZ[:], AF.Exp, reads=[ptZ], writes=[e1t], scale=-1.0)
            lan, lant = f512.get()
            act(lan[:], e1[:], AF.Ln, reads=[e1t], writes=[lant], bias=1.0)
            X["lan"], X["lant"] = lan, lant

        def A2():
            lan, lant = X["lan"], X["lant"]
            psB, ptB = ps_f()
            for i in range(4):
                mm(psB[:, i * 128:(i + 1) * 128], lan[:, i * 128:(i + 1) * 128], Uf[:], True, True, reads=[lant, ct], writes=[ptB])
            enB, enBt = f512.get()
            act(enB[:], psB[:], AF.Exp, reads=[ptB], writes=[enBt], scale=1.0 / 16)
            bl, blt = blast.get()
            act(bl[:, 0:4], psB[:, 127:512:128], AF.Exp, reads=[ptB], writes=[blt], scale=-1.0 / 16)
            kt, ktt = f512.get()
            P.op("dve", lambda e: e.tensor_mul(out=kt[:], in0=gkT[:, c0:c0 + 512], in1=enB[:]), reads=[gkT_t, enBt], writes=[ktt])
            kh, kht = xh.get()
            P.op("pool", lambda e: e.tensor_mul(out=kh[:], in0=kt[:].rearrange("p (a b) -> p a b", a=4),
                                                in1=bl[:, 0:4].unsqueeze(2).broadcast_to([128, 4, 128])),
                 reads=[ktt, blt], writes=[kht])
            X.update(bl=bl, blt=blt, kh=kh, kht=kht)
            if own:
                eB, eBt = f512.get()
                act(eB[:], psB[:], AF.Exp, reads=[ptB, ct], writes=[eBt], scale=-1.0 / 16, bias=epsb[:, 1:2])
                qt_, qtt = xq.get()
                P.op("dve", lambda e: e.tensor_mul(out=qt_[:], in0=gqT[:, c0:c0 + 512], in1=eB[:]), reads=[gqT_t, eBt], writes=[qtt])
                ktb, ktbt = xk.get()
                P.op("pool", lambda e: e.tensor_copy(out=ktb[:], in_=kt[:]), reads=[ktt], writes=[ktbt])
                X.update(qt_=qt_, qtt=qtt, ktb=ktb, ktbt=ktbt)

        def A3():
            kh, kht = X["kh"], X["kht"]
            pT, pTt = ps_b()
            for i in range(4):
                tr(pT[:, i * 128:(i + 1) * 128], kh[:, i, :], identb[:], reads=[kht, ct], writes=[pTt])
            khT, khTt = xhT.get()
            act(khT[:], pT[:, 0:512], AF.Copy, reads=[pTt], writes=[khTt])
            X.update(khT=khT, khTt=khTt)
            if own:
                qt_, qtt, ktb, ktbt = X["qt_"], X["qtt"], X["ktb"], X["ktbt"]
                psA, ptA = ps_f()
                for i in range(4):
                    mm(psA[:, i * 128:(i + 1) * 128], ktb[:, i * 128:(i + 1) * 128], qt_[:, i * 128:(i + 1) * 128], True, True,
                       reads=[ktbt, qtt], writes=[ptA])
                Am, Amt = xa.get()
                P.op("dve", lambda e: e.tensor_mul(out=Am[:], in0=psA[:].rearrange("p (a b) -> p a b", a=4),
                                                   in1=Uf[:].unsqueeze(1).broadcast_to([128, 4, 128])), reads=[ptA, ct], writes=[Amt])
                X.update(Am=Am, Amt=Amt)

        def mk(i):
            ti = b * 4 + i
            t0 = ti * 128

            def stageB():
                bl, blt, khT, khTt = X["bl"], X["blt"], X["khT"], X["khTt"]
                if own:
                    Am, Amt, qt_, qtt = X["Am"], X["Amt"], X["qt_"], X["qtt"]
                    psO, ptO = ps_f()
                    for vh in range(2):
                        mm(psO[:, vh * 128:(vh + 1) * 128], gv[:, ti, vh * 128:(vh + 1) * 128], Am[:, i, :], True, False,
                           reads=[gv_t[ti], Amt], writes=[ptO])
                        mm(psO[:, vh * 128:(vh + 1) * 128], Sb[:, h, vh * 128:(vh + 1) * 128], qt_[:, i * 128:(i + 1) * 128],
                           False, True, reads=[Sb_t[h], qtt], writes=[ptO])
                psS, ptS = ps_f()
                mm(psS[:, 0:256], khT[:, i * 128:(i + 1) * 128], gv[:, ti, :], True, True, reads=[khTt, gv_t[ti]], writes=[ptS])
                if own:
                    P.op("act", lambda e: e.activation(out=o_h[:, :, t0:t0 + 128],
                                                       in_=psO[:, 0:256].rearrange("p (a b) -> p a b", a=2),
                                                       func=AF.Copy), reads=[ptO], writes=[o_t])
                P.op("dve", lambda e: e.scalar_tensor_tensor(out=Sf[:, h, :], in0=Sf[:, h, :], scalar=bl[:, i:i + 1],
                                                             in1=psS[:, 0:256], op0=ALU.mult, op1=ALU.add),
                     reads=[Sf_t[h], blt, ptS], writes=[Sf_t[h]])
                P.op("pool", lambda e: e.tensor_copy(out=Sb[:, h, :], in_=Sf[:, h, :]), reads=[Sf_t[h]], writes=[Sb_t[h]])
            return stageB
        return [A1, A2, A3], [mk(i) for i in range(4)]

    def gla_group(g, own):
        parts = PARTS_OWN if own else PARTS_PRE
        N = NTOK if own else 1024
        P.op("pool", lambda e: e.memset(glrT[:, 0:N], 1.0), writes=[glrT_t])
        wb, wt = wload([(0, C_GLR, 16)])

        def ev_glr(ps, pt, c0, n):
            act(glrT[0:16, c0:c0 + n], ps[0:16, 0:n], AF.Copy, reads=[pt], writes=[glrT_t])
        proj_fm(wb, wt, 0, 16, parts, ev_glr)
        ntile = 9 if own else 8

        def proj_items(h, G):
            gkT, gkT_t, gqT, gqT_t, gv, gv_t, ggs, ggs_t = (G["gkT"], G["gkT_t"], G["gqT"], G["gqT_t"], G["gv"], G["gv_t"],
                                                            G["ggs"], G["ggs_t"])
            cols = [(0, C_GK + h * 128, 128)] + ([(128, C_GQ + h * 128, 128)] if own else [])
            wb, wt = wload(cols)

            def ev_k(ps, pt, c0, n):
                act(gkT[:, c0:c0 + n], ps[:, 0:n], AF.Copy, reads=[pt], writes=[gkT_t])

            def ev_q(ps, pt, c0, n):
                P.op("dve", lambda e: e.tensor_copy(out=gqT[:, c0:c0 + n], in_=ps[:, 0:n]), reads=[pt], writes=[gqT_t])
            for p_ in parts:
                proj_fm(wb, wt, 0, 128, [p_], ev_k)
                yield
            if own:
                for p_ in parts:
                    proj_fm(wb, wt, 128, 128, [p_], ev_q)
                    yield
            wb, wt = wload([(0, C_GV + h * 256, 256)])
            for ti in range(ntile):
                m = 128 if ti < 8 else 16

                def ev_v(ps, pt, ti=ti, m=m):
                    P.op("dve", lambda e: e.tensor_copy(out=gv[0:m, ti, :], in_=ps[0:m, 0:256]), reads=[pt], writes=[gv_t[ti]])
                proj_tm(wb, wt, 256, ti * 128, m, ev_v)
                yield
            if own:
                wb, wt = wload([(0, C_GG + h * 256, 256)])
                for vh in range(2):
                    def ev_g(ps, pt, c0, n, vh=vh):
                        act(ggs[:, vh, c0:c0 + n], ps[:, 0:n], AF.Silu, reads=[pt], writes=[ggs_t])
                    for p_ in parts:
                        proj_fm(wb, wt, vh * 128, 128, [p_], ev_g)
                        yield

        def head_norm(h, G):
            ggs, ggs_t = G["ggs"], G["ggs_t"]
            pss = [ps_f() for _ in parts]
            for vh in range(2):
                q, qt = sqb.get()
                act(q[:, 0:N], o_h[:, vh, 0:N], AF.Square, reads=[o_t], writes=[qt])
                for (c0, n), (ps, pt) in zip(parts, pss):
                    mm(ps[:, 0:n], onesb[:], q[:, c0:c0 + n], vh == 0, vh == 1, reads=[qt, ct], writes=[pt])
            for (c0, n), (ps, pt) in zip(parts, pss):
                act(sgate[:, c0:c0 + n], ps[:, 0:n], AF.Ln, reads=[pt, ct], writes=[sgate_t], scale=1.0 / 256,
                    bias=epsb[:, 0:1])
            act(sgate[:, 0:N], sgate[:, 0:N], AF.Exp, reads=[sgate_t], writes=[sgate_t], scale=-0.5)
            for vh in range(2):
                P.op("dve", lambda e, vh=vh: e.scalar_tensor_tensor(
                    out=o_h[:, vh, 0:N], in0=o_h[:, vh, 0:N], scalar=nw[:, NW_GLA + vh:NW_GLA + vh + 1],
                    in1=sgate[:, 0:N], op0=ALU.mult, op1=ALU.mult), reads=[o_t, sgate_t, ct], writes=[o_t])
                P.op("pool", lambda e, vh=vh: e.tensor_mul(out=AT[:, 2 * h + vh, 0:N], in0=o_h[:, vh, 0:N],
                                                          in1=ggs[:, vh, 0:N]),
                     reads=[o_t, ggs_t], writes=[AT_t[2 * h + vh]])

        for _ in proj_items(0, GB[0]):
            pass
        for h in range(4):
            G = GB[h % 2]
            nxt = proj_items(h + 1, GB[(h + 1) % 2]) if h < 3 else None
            (a0, b0), (a1, b1) = gla_batch(h, 0, own, G), gla_batch(h, 1, own, G)
            steps = [a0[0], a1[0], a0[1], a1[1], a0[2], a1[2]] + b0 + b1
            if own:
                steps.append(lambda h=h, G=G: gla_tile(h, 1024, 16, own, True, G))
                steps.append(lambda h=h, G=G: head_norm(h, G))
            nit = (24 if own else 10)
            per = -(-nit // len(steps))
            for st in steps:
                st()
                if nxt is not None:
                    for _ in range(per):
                        if next(nxt, "done") == "done":
                            nxt = None
                            break
            if nxt is not None:
                for _ in nxt:
                    pass

    A3 = Arena(nc, A1_mark)
    ks_s = A3.alloc([16, 1024], F32)
    vs_s = A3.alloc([16, 1024], F32)
    qs_s = A3.alloc([16, 1024], BF16)
    ks_t, vs_t, qs_t = Trk(), Trk(), Trk()
    amask = A3.alloc([128, 3 * 8 * 256], BF16)
    kb = A3.alloc([128, NU], F32)
    selb = A3.alloc([128, 256], BF16)
    selq = A3.alloc([16, 16 * 128], BF16)
    sbias = A3.alloc([128, 24], F32)
    anwb = A3.alloc([16, 1024], F32)

    def load_attn_consts():
        for dst, src in ((amask[:], amask_d), (kb[:], kb_d), (selb[:], selb_d), (selq[:], selq_d),
                         (sbias[:], sbias_d), (anwb[:], anwb_d)):
            dma("sp", dst, src, writes=[ct])
    A3_mark = A3.off
    aqT = A3.alloc([128, NTOK], BF16)
    akT = A3.alloc([128, 3072 + 16], BF16)
    avT = A3.alloc([128, 3072 + 16], BF16)
    aqT_t, akT_t, avT_t = Trk(), Trk(), Trk()
    acc = A3.alloc([128, 2, 1024], F32)
    acc_t = Trk()
    ssatt = A3.alloc([128, NTOK], F32)
    ssatt_t = Trk()
    et = Rot(A3, [128, 256], F32, 4)
    ptl = Rot(A3, [128, 256], BF16, 6)
    vbl = Rot(A3, [128, 128], BF16, 8)
    kvst = Rot(A3, [128, 1024], BF16, 3)
    kout = Rot(A3, [128, 256], F32, 3)
    attf = A3.alloc([128, 1024], F32)
    attf_t = Trk()
    A3 = Arena(nc, A3_mark)
    Kt = Rot(A3, [128, 1024], BF16, 4)
    Vt = Rot(A3, [128, 1024], BF16, 5)
    prod = Rot(A3, [128, 1024], F32, 2)
    pvb = Rot(A3, [128, 1024], BF16, 3)
    sc8 = Rot(A3, [128, 8], F32, 10)
    p8b = Rot(A3, [128, 8], BF16, 4)
    sm16 = Rot(A3, [16, 1024], F32, 2)
    sm8 = Rot(A3, [16, 8], F32, 4)
    atts_b = A3.alloc([16, 1024], BF16)
    atts_bt = Trk()

    def kv_prefix_group(g):
        for h in range(8):
            wb, wt = wload([(0, C_AK + h * 128, 128), (128, C_AV + h * 128, 128)])
            for which in range(2):
                st_, stt = kvst.get()

                def ev(ps, pt, c0, n, st_=st_, stt=stt):
                    act(st_[:, c0:c0 + n], ps[:, 0:n], AF.Copy, reads=[pt], writes=[stt])
                proj_fm(wb, wt, which * 128, 128, PARTS_PRE, ev)
                dma("sp", kvscr[which, h, :, (g - 1) * 1024:g * 1024], st_[:], reads=[stt], writes=[kvscr_t[which][h]])

    def kv_outputs():
        for which, (cb, outd, outs, sdst, sdt) in enumerate(((C_AK, k_o, ks_o, ks_s, ks_t), (C_AV, v_o, vs_o, vs_s, vs_t))):
            for cbk in range(4):
                wb, wt = wload([(0, cb + cbk * 256, 256)])
                for ti in range(9):
                    m = 128 if ti < 8 else 16
                    ko, kot = kout.get()

                    def ev(ps, pt, ko=ko, kot=kot, m=m):
                        act(ko[0:m, :], ps[0:m, 0:256], AF.Copy, reads=[pt], writes=[kot])
                    proj_tm(wb, wt, 256, ti * 128, m, ev)
                    if ti < 8:
                        dma("sp", outd[ti * 128:(ti + 1) * 128, cbk * 256:(cbk + 1) * 256], ko[:], reads=[kot])
                    else:
                        dma("sp", outs[:, cbk * 256:(cbk + 1) * 256], ko[0:16, :], reads=[kot], writes=[ksvs_t[which]])
                        P.op("pool", lambda e, ko=ko, sdst=sdst, cbk=cbk: e.tensor_copy(
                            out=sdst[:, cbk * 256:(cbk + 1) * 256], in_=ko[0:16, :]), reads=[kot], writes=[sdt])

    def attn_head(h):
        wb, wt = wload([(0, C_AQ + h * 128, 128), (128, C_AK + h * 128, 128)])
        wb2, wt2 = wload([(0, C_AV + h * 128, 128)])
        dma("sp", akT[:, 0:2048], kvscr[0, h], reads=[kvscr_t[0][h]], writes=[akT_t])
        dma("sp", avT[:, 0:2048], kvscr[1, h], reads=[kvscr_t[1][h]], writes=[avT_t])

        def ev_q(ps, pt, c0, n):
            act(aqT[:, c0:c0 + n], ps[:, 0:n], AF.Copy, reads=[pt], writes=[aqT_t])

        def ev_k(ps, pt, c0, n):
            P.op("dve", lambda e: e.tensor_copy(out=akT[:, 2048 + c0:2048 + c0 + n], in_=ps[:, 0:n]), reads=[pt], writes=[akT_t])

        def ev_v(ps, pt, c0, n):
            act(avT[:, 2048 + c0:2048 + c0 + n], ps[:, 0:n], AF.Copy, reads=[pt], writes=[avT_t])
        proj_fm(wb, wt, 0, 128, PARTS_OWN, ev_q)
        proj_fm(wb, wt, 128, 128, PARTS_OWN, ev_k)
        proj_fm(wb2, wt2, 0, 128, PARTS_OWN, ev_v)
        P.op("pool", lambda e: e.memset(acc[:], 0.0), writes=[acc_t])
        def S1(ui):
            u = UNITS[ui]
            D, nk, nq = u["D"], u["nk"], u["nq"]
            ksl = slice(u["kc0"], u["kc0"] + (nk - 1) * D + 1, D)
            qsl = slice(u["qc0"], u["qc0"] + (nq - 1) * D + 1, D)
            pT, pTt = ps_b()
            tr(pT[0:nk, 0:128], avT[:, ksl], identb[:], reads=[avT_t, ct], writes=[pTt])
            vb, vbt = vbl.get()
            act(vb[0:nk, :], pT[0:nk, 0:128], AF.Copy, reads=[pTt], writes=[vbt])
            ps, pt = ps_f()
            mm(ps[0:nk, 0:nq], akT[:, ksl], aqT[:, qsl], True, True, reads=[akT_t, aqT_t], writes=[pt])
            return dict(ui=ui, u=u, nk=nk, nq=nq, qsl=qsl, vb=vb, vbt=vbt, ps=ps, pt=pt)

        def S2(c):
            nk, nq, ui, u = c["nk"], c["nq"], c["ui"], c["u"]
            e_, e_t = et.get()
            act(e_[0:nk, 0:nq], c["ps"][0:nk, 0:nq], AF.Exp, reads=[c["pt"], ct], writes=[e_t], scale=SCALE,
                bias=kb[0:nk, ui:ui + 1])
            p_, p_t = ptl.get()
            mbase = (u["di"] * 8 + h) * 256 + u["m0"]
            P.op("pool", lambda e, p_=p_, e_=e_, nk=nk, nq=nq, mbase=mbase: e.tensor_mul(
                out=p_[0:nk, 0:nq], in0=e_[0:nk, 0:nq], in1=amask[0:nk, mbase:mbase + nq]), reads=[e_t, ct], writes=[p_t])
            c["p_"], c["p_t"] = p_, p_t

        def S3(c):
            nk, nq, qsl, p_, p_t = c["nk"], c["nq"], c["qsl"], c["p_"], c["p_t"]
            ps2, pt2 = ps_f()
            mm(ps2[:, 0:nq], c["vb"][0:nk, :], p_[0:nk, 0:nq], True, True, reads=[c["vbt"], p_t], writes=[pt2])
            mm(ps2[:, 256:256 + nq], onesb[0:nk, :], p_[0:nk, 0:nq], True, True, reads=[p_t, ct], writes=[pt2])
            P.op("dve", lambda e, ps2=ps2, qsl=qsl, nq=nq: e.tensor_add(
                out=acc[:, :, qsl], in0=acc[:, :, qsl], in1=ps2[:, 0:512].rearrange("p (a b) -> p a b", a=2)[:, :, 0:nq]),
                reads=[acc_t, pt2], writes=[acc_t])
        ctxs = {}
        for i in range((NU + 1) // 2 + 2):
            for u_ in (2 * i, 2 * i + 1):
                if u_ < NU:
                    ctxs[u_] = S1(u_)
            for u_ in (2 * i - 2, 2 * i - 1):
                if 0 <= u_ < NU:
                    S2(ctxs[u_])
            for u_ in (2 * i - 4, 2 * i - 3):
                if 0 <= u_ < NU:
                    S3(ctxs.pop(u_))
        assert not ctxs
        act(acc[:, 1, :], acc[:, 1, :], AF.Ln, reads=[acc_t], writes=[acc_t])
        act(acc[:, 1, :], acc[:, 1, :], AF.Exp, reads=[acc_t], writes=[acc_t], scale=-1.0)
        P.op("dve", lambda e: e.tensor_mul(out=attf[:], in0=acc[:, 0, :], in1=acc[:, 1, :]),
             reads=[acc_t], writes=[attf_t])
        P.op("pool", lambda e: e.tensor_copy(out=AT[:, 8 + h, 0:1024], in_=attf[:]), reads=[attf_t], writes=[AT_t[8 + h]])
        q, qt = sqb.get()
        act(q[:, 0:1024], attf[:], AF.Square, reads=[attf_t], writes=[qt])
        for (c0, n) in PARTS_PRE:
            ps, pt = ps_f()
            mm(ps[:, 0:n], onesb[:], q[:, c0:c0 + n], True, True, reads=[qt, ct], writes=[pt])
            if h == 0:
                act(ssatt[:, c0:c0 + n], ps[:, 0:n], AF.Copy, reads=[pt], writes=[ssatt_t])
            else:
                P.op("dve", lambda e, ps=ps, c0=c0, n=n: e.tensor_add(out=ssatt[:, c0:c0 + n], in0=ssatt[:, c0:c0 + n],
                                                                       in1=ps[:, 0:n]), reads=[pt, ssatt_t], writes=[ssatt_t])
        pT, pTt = ps_b()
        tr(pT[0:16, 0:128], aqT[:, 1024:1040], identb[:], reads=[aqT_t, ct], writes=[pTt])
        act(qs_s[:, h * 128:(h + 1) * 128], pT[0:16, 0:128], AF.Copy, reads=[pTt], writes=[qs_t])

    def attn_finish():
        act(ssatt[:, 0:1024], ssatt[:, 0:1024], AF.Ln, reads=[ssatt_t, ct], writes=[ssatt_t], scale=1.0 / 1024, bias=epsb[:, 0:1])
        act(ssatt[:, 0:1024], ssatt[:, 0:1024], AF.Exp, reads=[ssatt_t], writes=[ssatt_t], scale=-0.5)
        for h in range(8):
            P.op("dve", lambda e, h=h: e.scalar_tensor_tensor(
                out=AT[:, 8 + h, 0:1024], in0=AT[:, 8 + h, 0:1024], scalar=nw[:, NW_ATT + h:NW_ATT + h + 1],
                in1=ssatt[:, 0:1024], op0=ALU.mult, op1=ALU.mult), reads=[AT_t[8 + h], ssatt_t, ct], writes=[AT_t[8 + h]])

    def sample_attention():
        psN = [(psf[0], psf_t[0]), (psf[1], psf_t[1])]
        psD = (psf[2], psf_t[2])
        psQ = [(psf[3], psf_t[3]), (psf[4], psf_t[4])]
        nunit = 4 * 3 * 4
        ulist = [(j, di, D, i) for j in range(4) for di, D in enumerate((1, 4, 16)) for i in range(4)]

        def SA(k):
            j, di, D, i = ulist[k]
            Kb, Kbt = Kt.get()
            Vb, Vbt = Vt.get()
            r0 = 2048 + i - D * 128
            ncache = 128 if D > 1 else 128 - i
            for (buf, bt, cd, sd, sdt) in ((Kb, Kbt, ck_d, ks_o, ksvs_t[0]), (Vb, Vbt, cv_d, vs_o, ksvs_t[1])):
                if D == 1:
                    dma("pool", buf[0:112, :], cd[j, r0:r0 + 112, :], writes=[bt])
                    dma("pool", buf[112:ncache, :], cd[j, r0 + 112:r0 + ncache, :], writes=[bt])
                    if i > 0:
                        dma("pool", buf[ncache:128, :], sd[4 * j:4 * j + i, :], reads=[sdt], writes=[bt])
                else:
                    dma("pool", buf[:, :], cd[j, r0:r0 + 127 * D + 1:D, :], writes=[bt])
            return dict(k=k, tok=4 * j + i, di=di, Kb=Kb, Kbt=Kbt, Vb=Vb, Vbt=Vbt)

        def SB(c):
            tok, di, Kb, Kbt = c["tok"], c["di"], c["Kb"], c["Kbt"]
            for hh in range(2):
                mm(psQ[hh][0][:, :], selq[0:16, tok * 128:(tok + 1) * 128], qs_s[:, hh * 512:(hh + 1) * 512], True, True,
                   reads=[qs_t, ct], writes=[psQ[hh][1]])
            pr, prt = prod.get()
            for hh in range(2):
                P.op("dve", lambda e, pr=pr, Kb=Kb, hh=hh: e.tensor_mul(
                    out=pr[:, hh * 512:(hh + 1) * 512], in0=Kb[:, hh * 512:(hh + 1) * 512], in1=psQ[hh][0][:, :]),
                    reads=[Kbt, psQ[hh][1]], writes=[prt])
            s8, s8t = sc8.get()
            P.op("dve", lambda e, s8=s8, pr=pr: e.tensor_reduce(
                out=s8[:], in_=pr[:].rearrange("p (h e) -> p h e", h=8), axis=AX.X, op=ALU.add), reads=[prt], writes=[s8t])
            s9, s9t = sc8.get()
            P.op("dve", lambda e, s8=s8, s9=s9, di=di: e.scalar_tensor_tensor(
                out=s9[:], in0=s8[:], scalar=SCALE, in1=sbias[:, di * 8:(di + 1) * 8], op0=ALU.mult, op1=ALU.add),
                reads=[s8t, ct], writes=[s9t])
            c["s9"], c["s9t"] = s9, s9t

        def SC(c):
            Vb, Vbt, s9, s9t = c["Vb"], c["Vbt"], c["s9"], c["s9t"]
            pe_, pet = sc8.get()
            act(pe_[:], s9[:], AF.Exp, reads=[s9t], writes=[pet])
            p8, p8t = p8b.get()
            act(p8[:], pe_[:], AF.Copy, reads=[pet], writes=[p8t])
            pv, pvt = pvb.get()
            for hd in range(4):
                act(pv[:, hd * 128:(hd + 1) * 128], Vb[:, hd * 128:(hd + 1) * 128], AF.Copy, reads=[Vbt, pet], writes=[pvt],
                    scale=pe_[:, hd:hd + 1])
            P.op("pool", lambda e, pv=pv, Vb=Vb, pe_=pe_: e.tensor_mul(
                out=pv[:, 512:1024].rearrange("p (h e) -> p h e", h=4), in0=Vb[:, 512:1024].rearrange("p (h e) -> p h e", h=4),
                in1=pe_[:, 4:8].unsqueeze(2).broadcast_to([128, 4, 128])), reads=[Vbt, pet], writes=[pvt])
            c["p8"], c["p8t"], c["pv"], c["pvt"] = p8, p8t, pv, pvt

        def SD(c):
            k, tok, pv, pvt, p8, p8t = c["k"], c["tok"], c["pv"], c["pvt"], c["p8"], c["p8t"]
            for hh in range(2):
                mm(psN[hh][0][0:16, :], selb[:, tok * 16:(tok + 1) * 16], pv[:, hh * 512:(hh + 1) * 512],
                   k == 0, k == nunit - 1, reads=[pvt, ct], writes=[psN[hh][1]])
            mm(psD[0][0:16, 0:8], selb[:, tok * 16:(tok + 1) * 16], p8[:], k == 0, k == nunit - 1,
               reads=[p8t, ct], writes=[psD[1]])
        cx = {}
        for it in range(nunit + 4):
            if it < nunit:
                cx[it] = SA(it)
            if 0 <= it - 2 < nunit:
                SB(cx[it - 2])
            if 0 <= it - 3 < nunit:
                SC(cx[it - 3])
            if 0 <= it - 4 < nunit:
                SD(cx.pop(it - 4))
        t1, t1t = sm16.get()
        P.op("dve", lambda e: e.tensor_mul(out=t1[:], in0=qs_s[:], in1=ks_s[:]), reads=[qs_t, ks_t], writes=[t1t])
        s1, s1t = sm8.get()
        P.op("dve", lambda e: e.tensor_reduce(out=s1[:], in_=t1[:].rearrange("p (h e) -> p h e", h=8), axis=AX.X, op=ALU.add),
             reads=[t1t], writes=[s1t])
        p1, p1t = sm8.get()
        act(p1[:], s1[:], AF.Exp, reads=[s1t], writes=[p1t], scale=SCALE)
        den, dent = sm8.get()
        P.op("dve", lambda e: e.scalar_tensor_tensor(out=den[:], in0=p1[:], scalar=3.0, in1=psD[0][0:16, 0:8],
                                                     op0=ALU.mult, op1=ALU.add), reads=[p1t, psD[1]], writes=[dent])
        P.op("dve", lambda e: e.reciprocal(out=den[:], in_=den[:]), reads=[dent], writes=[dent])
        t2, t2t = sm16.get()
        P.op("pool", lambda e: e.tensor_mul(out=t2[:].rearrange("p (h e) -> p h e", h=8),
                                            in0=vs_s[:].rearrange("p (h e) -> p h e", h=8),
                                            in1=p1[:].unsqueeze(2).broadcast_to([16, 8, 128])), reads=[vs_t, p1t], writes=[t2t])
        t3, t3t = sm16.get()
        for hh in range(2):
            P.op("dve", lambda e, hh=hh: e.scalar_tensor_tensor(
                out=t3[:, hh * 512:(hh + 1) * 512], in0=t2[:, hh * 512:(hh + 1) * 512], scalar=3.0, in1=psN[hh][0][0:16, :],
                op0=ALU.mult, op1=ALU.add), reads=[t2t, psN[hh][1]], writes=[t3t])
        P.op("dve", lambda e: e.tensor_mul(out=t3[:].rearrange("p (h e) -> p h e", h=8),
                                           in0=t3[:].rearrange("p (h e) -> p h e", h=8),
                                           in1=den[:].unsqueeze(2).broadcast_to([16, 8, 128])), reads=[t3t, dent], writes=[t3t])
        t4, t4t = sm16.get()
        ss1, ss1t = sm8.get()
        act(t4[:], t3[:], AF.Square, reads=[t3t], writes=[t4t, ss1t], accum_out=ss1[:, 0:1])
        act(ss1[:, 0:1], ss1[:, 0:1], AF.Sqrt, reads=[ss1t, ct], writes=[ss1t], scale=1.0 / 1024, bias=epsb[0:16, 0:1])
        P.op("dve", lambda e: e.reciprocal(out=ss1[:, 0:1], in_=ss1[:, 0:1]), reads=[ss1t], writes=[ss1t])
        P.op("dve", lambda e: e.scalar_tensor_tensor(out=atts_b[:], in0=t3[:], scalar=ss1[:, 0:1], in1=anwb[:],
                                                     op0=ALU.mult, op1=ALU.mult), reads=[t3t, ss1t, ct], writes=[atts_bt])
        for h in range(8):
            pT, pTt = ps_b()
            tr(pT[:, 0:16], atts_b[:, h * 128:(h + 1) * 128], identb[0:16, 0:16], reads=[atts_bt, ct], writes=[pTt])
            act(AT[:, 8 + h, 1024:1040], pT[:, 0:16], AF.Copy, reads=[pTt], writes=[AT_t[8 + h]])

    xTv = xT.rearrange("(kc p) n -> kc p n", p=128)
    xsTv = xsT.rearrange("(kc p) n -> kc p n", p=128)
    for g in range(4):
        own = g == 3
        if DEBUG == "gla0" and not own:
            continue
        parts = PARTS_OWN if own else PARTS_PRE

        def src_chunks(kc, g=g, own=own):
            b, t = xc.get()
            dma("sp" if kc % 2 == 0 else "pool", b[:, 0:1024], xTv[kc, :, g * 1024:(g + 1) * 1024], writes=[t])
            if own:
                P.op("pool", lambda e, b=b, kc=kc: e.tensor_copy(out=b[:, 1024:1040], in_=xs_all[:, kc, :]), reads=[xs_t], writes=[t])
            return b[:, 0:(NTOK if own else 1024)], [t]
        ssq_norm(src_chunks, parts, NW_ATTN, hT, hT_t, 2048.0)
        dumpf(0, rB[:, :], [rB_t])
        dumpb(0, hT[:, 0, :], [hT_t[0]])
        dumpb(1, hT[:, 15, :], [hT_t[15]])
        gla_group(g, own)
        if g in (1, 2):
            P.barrier()
            kv_prefix_group(g)
            P.barrier()
    for h in range(4):
        dma("sp", gst_o[h], Sf[:, h, :], reads=[Sf_t[h]])
    P.barrier()
    load_attn_consts()
    kv_outputs()
    if DEBUG == "A":
        raise StopBuild()
    for h in range(8):
        attn_head(h)
    attn_finish()
    P.barrier()
    if DEBUG == "B":
        raise StopBuild()
    wo_issued = []

    def wo_issue(cb):
        b_, t_ = wblk.get()
        dma("pool", b_[:], wo_flat[:, cb * 4096:(cb + 1) * 4096], writes=[t_])
        wo_issued.append((b_, t_))
    for cb_ in range(3):
        wo_issue(cb_)
    sample_attention()
    P.barrier()
    if DEBUG == "C":
        raise StopBuild()
    if DEBUG == "F":
        for sl, c in zip(range(2, 8), (0, 1, 7, 8, 9, 15)):
            dumpb(sl, AT[:, c, :], [AT_t[c]])

    A4 = Arena(nc, P1C_START)
    x1T = A4.alloc([128, 16, NTOK], F32)
    x1T_t = [Trk() for _ in range(16)]
    x1T_end = A4.off
    xres = Rot(A4, [128, NTOK], F32, 2)
    wo = wblk
    for cb in range(8):
        if cb >= 1 and cb + 2 < 8:
            wo_issue(cb + 2)
        b, t = wo_issued[cb]
        b = b[:].rearrange("p (kc n) -> p kc n", kc=16)
        for sub in range(2):
            dc = cb * 2 + sub
            xr, xrt = xres.get()
            dma("sp", xr[:, 0:1024], xTv[dc, :, 3072:4096], writes=[xrt])
            P.op("pool", lambda e, xr=xr, dc=dc: e.tensor_copy(out=xr[:, 1024:1040], in_=xs_all[:, dc, :]), reads=[xs_t], writes=[xrt])
            for (c0, n) in PARTS_MLP:
                ps, pt = ps_f()
                for ac in range(16):
                    mm(ps[:, 0:n], b[:, ac, sub * 128:(sub + 1) * 128], AT[:, ac, c0:c0 + n], ac == 0, ac == 15,
                       reads=[t, AT_t[ac]], writes=[pt])
                P.op("dve", lambda e, ps=ps, xr=xr, dc=dc, c0=c0, n=n: e.tensor_add(
                    out=x1T[:, dc, c0:c0 + n], in0=ps[:, 0:n], in1=xr[:, c0:c0 + n]), reads=[pt, xrt], writes=[x1T_t[dc]])
    if DEBUG == "F":
        dumpf(1, x1T[:, 0, :], [x1T_t[0]])
        dumpf(2, x1T[:, 15, :], [x1T_t[15]])
        P.barrier()
    h2T, h2T_t = hT, hT_t

    def src2(kc):
        return x1T[:, kc, :], [x1T_t[kc]]
    ssq_norm(src2, PARTS_MLP, NW_FFN, h2T, h2T_t, 2048.0)
    P.barrier()
    if DEBUG == "D":
        raise StopBuild()

    A5 = Arena(nc, R_START)
    wu = Rot(A5, [128, 16, 512], BF16, 2)
    wd = Rot(A5, [128, 4, 2048], BF16, 2)
    HT = Rot(A5, [128, 4, NTOK], BF16, 1)
    assert A5.off <= P1C_START, (A5.off, P1C_START)
    A5b = Arena(nc, x1T_end)
    HT.b.append(A5b.alloc([128, 4, NTOK], BF16)); HT.t.append(Trk())
    rl = Rot(A5b, [128, 512], F32, 3)
    for g in range(16):
        ub, ut = wu.get()
        dma("pool", ub[:].rearrange("p a b -> p (a b)"), wu_flat[:, g * 8192:(g + 1) * 8192], writes=[ut])
        db, dt_ = wd.get()
        dma("pool", db[:], w_down[g * 512:(g + 1) * 512, :].rearrange("(fc p) n -> p fc n", p=128), writes=[dt_])
        hb, hbt = HT.get()
        for fc in range(4):
            for (c0, n) in PARTS_MLP:
                ps, pt = ps_f()
                for kc in range(16):
                    mm(ps[:, 0:n], ub[:, kc, fc * 128:(fc + 1) * 128], h2T[:, kc, c0:c0 + n], kc == 0, kc == 15,
                       reads=[ut, h2T_t[kc]], writes=[pt])
                r_, r_t = rl.get()
                act(r_[:, 0:n], ps[:, 0:n], AF.Relu, reads=[pt], writes=[r_t])
                P.op("pool", lambda e, hb=hb, r_=r_, fc=fc, c0=c0, n=n: e.tensor_mul(
                    out=hb[:, fc, c0:c0 + n], in0=r_[:, 0:n], in1=r_[:, 0:n]), reads=[r_t], writes=[hbt])
        for dc in range(16):
            for (c0, n) in PARTS_MLP:
                ps, pt = ps_f()
                for fc in range(4):
                    mm(ps[:, 0:n], db[:, fc, dc * 128:(dc + 1) * 128], hb[:, fc, c0:c0 + n], fc == 0, fc == 3,
                       reads=[dt_, hbt], writes=[pt])
                P.op("dve", lambda e, ps=ps, dc=dc, c0=c0, n=n: e.tensor_add(
                    out=x1T[:, dc, c0:c0 + n], in0=x1T[:, dc, c0:c0 + n], in1=ps[:, 0:n]), reads=[pt, x1T_t[dc]], writes=[x1T_t[dc]])
    P.barrier()

    if DEBUG == "E":
        raise StopBuild()
    if DEBUG == "F":
        dumpf(3, x1T[:, 0, :], [x1T_t[0]])
        dumpf(4, x1T[:, 15, :], [x1T_t[15]])
        P.barrier()
    A6 = Arena(nc, R_START)
    sqb4 = Rot(A6, [128, NTOK], BF16, 2)
    rB4 = A6.alloc([128, NTOK], F32)
    rB4_t = Trk()
    yT, yT_t = x1T, x1T_t
    ytok = Rot(A6, [128, 2048], F32, 2)
    pss = [ps_f() for _ in PARTS_OWN]
    for kc in range(16):
        q, qt = sqb4.get()
        act(q[:, 0:NTOK], x1T[:, kc, :], AF.Square, reads=[x1T_t[kc]], writes=[qt])
        for (c0, n), (ps, pt) in zip(PARTS_OWN, pss):
            mm(ps[:, 0:n], onesb[:], q[:, c0:c0 + n], kc == 0, kc == 15, reads=[qt, ct], writes=[pt])
    for (c0, n), (ps, pt) in zip(PARTS_OWN, pss):
        act(rB4[:, c0:c0 + n], ps[:, 0:n], AF.Ln, reads=[pt, ct], writes=[rB4_t], scale=1.0 / 2048, bias=epsb[:, 0:1])
    act(rB4[:, :], rB4[:, :], AF.Exp, reads=[rB4_t], writes=[rB4_t], scale=-0.5)
    for kc in range(16):
        P.op("dve", lambda e, kc=kc: e.scalar_tensor_tensor(
            out=yT[:, kc, :], in0=x1T[:, kc, :], scalar=nw[:, NW_FIN + kc:NW_FIN + kc + 1], in1=rB4[:, :],
            op0=ALU.mult, op1=ALU.mult), reads=[x1T_t[kc], rB4_t, ct], writes=[yT_t[kc]])
    for ti in range(9):
        m = 128 if ti < 8 else 16
        yb, ybt = ytok.get()
        for q4 in range(4):
            ps, pt = ps_f()
            for s in range(4):
                kc = q4 * 4 + s
                tr(ps[0:m, s * 128:(s + 1) * 128], yT[:, kc, ti * 128:ti * 128 + m], identf[:], reads=[yT_t[kc], ct], writes=[pt])
            if q4 % 2 == 0:
                act(yb[0:m, q4 * 512:(q4 + 1) * 512], ps[0:m, :], AF.Copy, reads=[pt], writes=[ybt])
            else:
                P.op("dve", lambda e, yb=yb, ps=ps, q4=q4, m=m: e.tensor_copy(out=yb[0:m, q4 * 512:(q4 + 1) * 512], in_=ps[0:m, :]),
                     reads=[pt], writes=[ybt])
        if ti < 8:
            dma("sp", y_o[ti * 128:(ti + 1) * 128, :], yb[:], reads=[ybt])
        else:
            dma("sp", ys_o[:, :], yb[0:16, :], reads=[ybt])


_CACHE = {}


def _consts():
    import ml_dtypes
    bf = ml_dtypes.bfloat16
    c = {}
    c["identf"] = np.eye(128, dtype=np.float32)
    c["identb"] = np.eye(128, dtype=np.float32).astype(bf)
    c["onesb"] = np.ones((128, 128), np.float32).astype(bf)
    s = np.arange(128)
    c["U"] = (s[:, None] <= s[None, :]).astype(np.float32)
    t = np.arange(16)
    c["Ub"] = ((t[:, None] <= t[None, :]) & (t[:, None] // 4 == t[None, :] // 4)).astype(np.float32)
    slopes = np.exp2(-8.0 * np.arange(1, 9, dtype=np.float32) / 8).astype(np.float32)
    kk = np.arange(128)[:, None]
    qq = np.arange(256)[None, :]
    dist = qq - kk
    valid = (dist >= 0) & (dist <= 128)
    am = np.zeros((128, 3, 8, 256), np.float32)
    for di, D in enumerate((1, 4, 16)):
        for h in range(8):
            am[:, di, h, :] = np.where(valid, np.exp(-slopes[h] * D * np.maximum(dist, 0).astype(np.float32)), 0.0)
    c["amask"] = am.reshape(128, -1).astype(bf)
    selb = np.zeros((128, 16, 16), np.float32)
    for tok in range(16):
        selb[:, tok, tok] = 1.0
    c["selb"] = selb.reshape(128, 256).astype(bf)
    selq = np.zeros((16, 16, 128), np.float32)
    for tok in range(16):
        selq[tok, tok, :] = 1.0
    c["selq"] = selq.reshape(16, -1).astype(bf)
    m = np.arange(128)
    sb = np.zeros((128, 3, 8), np.float32)
    for di, D in enumerate((1, 4, 16)):
        sb[:, di, :] = -slopes[None, :] * (D * (128 - m))[:, None]
    c["sbias"] = sb.reshape(128, 24)
    c["smask"] = (t[:, None] // 4 == np.arange(4)[None, :]).astype(np.float32)
    return c


def _kb(ci):
    s = ci * 1024
    kb = np.zeros((128, NU), np.float32)
    for ui, u in enumerate(UNITS):
        kk = np.arange(128)
        uloc = (128 * u["j"] + kk) * u["D"] + u["r"]
        tglob = s - 2048 + uloc
        kb[:, ui] = np.where(tglob >= 0, 0.0, -30000.0)
    return kb


def kernel(x_prompt, x_sample, cache_k_win, cache_v_win, state_gla, attn_norm_w, w_in, w_gk_up, b_gk,
           gla_norm_w, att_out_norm_w, w_out, ffn_norm_w, w_up, w_down, final_norm_w):
    f = lambda a: np.ascontiguousarray(np.asarray(a, dtype=np.float32))
    x_prompt, x_sample = f(x_prompt), f(x_sample)
    if "nc" not in _CACHE:
        _CACHE["nc"] = build()
    nc = _CACHE["nc"]
    consts = _consts()
    col = lambda w, n: f(w).reshape(n, 128).T
    nwm = np.concatenate([col(attn_norm_w[0], 16), col(ffn_norm_w[0], 16), col(final_norm_w, 16),
                          col(gla_norm_w[0], 2), col(att_out_norm_w[0], 8)], axis=1)
    def tile_cols(w, c0, n):
        return w[:, c0:c0 + n].reshape(16, 128, n).transpose(1, 0, 2).reshape(128, 16 * n)
    w_in0, w_out0, w_up0 = f(w_in[0]), f(w_out[0]), f(w_up[0])
    wflat = np.empty((128, WTOT[0]), np.float32)
    for key, o in WOFF.items():
        blk = np.concatenate([w_in0[:, sc:sc + n] for (sc, n) in key], axis=1)
        W = blk.shape[1]
        wflat[:, o:o + 16 * W] = tile_cols(blk, 0, W)
    wo_flat = np.concatenate([tile_cols(w_out0, cb * 256, 256) for cb in range(8)], axis=1)
    wu_flat = np.concatenate([tile_cols(w_up0, g * 512, 512) for g in range(16)], axis=1)
    shared = dict(wflat=wflat, wo_flat=np.ascontiguousarray(wo_flat), wu_flat=np.ascontiguousarray(wu_flat), w_down=f(w_down[0]),
                  wgk=np.concatenate([f(w_gk_up[0]), f(b_gk[0])[None, :]], axis=0), nw=np.ascontiguousarray(nwm),
                  anwb=np.ascontiguousarray(np.broadcast_to(f(att_out_norm_w[0])[None, :], (16, 1024))), **consts)
    in_maps = []
    for c in range(8):
        b, ci = c // 4, c % 4
        s = ci * 1024
        xt = np.zeros((2048, 4096), np.float32)
        lo = max(0, s - 3072)
        seg = x_prompt[b, lo:s + 1024, :]
        xt[:, 4096 - seg.shape[0]:] = seg.T
        m = dict(shared)
        m["xT"] = xt
        m["xsT"] = np.ascontiguousarray(x_sample[4 * c:4 * c + 4].reshape(16, 2048).T)
        m["ck"] = np.ascontiguousarray(f(cache_k_win[0, 4 * c:4 * c + 4]).reshape(4, 2048, 1024))
        m["cv"] = np.ascontiguousarray(f(cache_v_win[0, 4 * c:4 * c + 4]).reshape(4, 2048, 1024))
        m["st"] = f(state_gla[0, 4 * c:4 * c + 4])
        m["kb"] = _kb(ci)
        in_maps.append(m)
    res = run_bass_kernel_spmd(nc, in_maps, core_ids=list(range(8)))
    R = res.results
    if DEBUG == "gla0":
        return R
    if DEBUG:
        z16 = np.zeros((16, 2048), np.float32)
        for r in R:
            r.setdefault("y_own", np.zeros((1024, 2048), np.float32))
            r.setdefault("y_s", z16)
    y_prompt = np.zeros((2, 4096, 2048), np.float32)
    y_sample = np.zeros((32, 4, 2048), np.float32)
    k_win = np.zeros((1, 2, 2048, 8, 128), np.float32)
    v_win = np.zeros((1, 2, 2048, 8, 128), np.float32)
    gla_p = np.zeros((1, 2, 4, 128, 256), np.float32)
    k_new = np.zeros((1, 32, 4, 8, 128), np.float32)
    v_new = np.zeros((1, 32, 4, 8, 128), np.float32)
    gla_s = np.zeros((1, 32, 4, 128, 256), np.float32)
    for c in range(8):
        b, ci = c // 4, c % 4
        s = ci * 1024
        r = R[c]
        y_prompt[b, s:s + 1024] = r["y_own"]
        y_sample[4 * c:4 * c + 4] = r["y_s"].reshape(4, 4, 2048)
        if ci >= 2:
            k_win[0, b, s - 2048:s - 1024] = r["k_own"].reshape(1024, 8, 128)
            v_win[0, b, s - 2048:s - 1024] = r["v_own"].reshape(1024, 8, 128)
        if ci == 3:
            gla_p[0, b] = r["gla_st"]
        k_new[0, 4 * c:4 * c + 4] = r["k_s"].reshape(4, 4, 8, 128)
        v_new[0, 4 * c:4 * c + 4] = r["v_s"].reshape(4, 4, 8, 128)
        gla_s[0, 4 * c:4 * c + 4] = r["gla_s"]
    return (y_prompt, y_sample, k_win, v_win, gla_p, k_new, v_new, gla_s)
```
